# Optimizing a Trainium2 kernel written in Bass

```python
import math
import jax, jax.numpy as jnp
from jax import lax
import numpy as np

D_MODEL = 1024
BATCH = 16
SEQ = 256
DEPTH = 4
DEC_BATCH = 4
DEC_SEQ = 1024
PAST_LEN = 512

GRID_W = 64
N_MIXERS = 3
D_FF = 4 * D_MODEL
EPS = 1e-6
S5_GROUP = 16
S5_GROUPS = D_MODEL // S5_GROUP
S5_STATE = 64
S5_DT_MIN = 1e-3
S5_DT_MAX = 1e-1
NA_HEAD_DIM = 64
NA_HEADS = D_MODEL // NA_HEAD_DIM
NA_KH = 8
NA_KW = 16
NA_QCB = 16
NA_KCB = 2 * NA_KW
GQA_HEAD_DIM = 128
GQA_Q_HEADS = D_MODEL // GQA_HEAD_DIM
GQA_KV_HEADS = 2
ROPE_THETA = 10000.0
Q_BLOCK = 128
N_S5 = (DEPTH + 2) // 3
N_NA = (DEPTH + 1) // 3
N_GQA = DEPTH // 3

kernel_name = "hybrid_s5_natten_gqa_diffusion_step"

F32 = jnp.float32


def rms_norm(x, g):
    xf = x.astype(F32)
    y = xf * lax.rsqrt(jnp.mean(xf * xf, axis=-1, keepdims=True) + EPS)
    return (y * g.astype(F32)).astype(x.dtype)


def modulation(cvec, w, b):
    m = jax.nn.silu(cvec) @ w + b
    return [t[:, None, :] for t in jnp.split(m, 6, axis=-1)]


def sqrelu_mlp(h, w1, w2):
    a = jax.nn.relu(h @ w1)
    return (a * a) @ w2


def axial_rope_tables(L, d):
    t = jnp.arange(L)
    row = (t // GRID_W).astype(F32)
    col = (t % GRID_W).astype(F32)
    half = d // 2
    inv = ROPE_THETA ** (-jnp.arange(0, half, 2, dtype=F32) / half)
    ang = jnp.concatenate([row[:, None] * inv, col[:, None] * inv], axis=-1)
    return jnp.cos(ang), jnp.sin(ang)


def apply_rope(x, cos, sin):
    x1, x2 = jnp.split(x.astype(F32), 2, axis=-1)
    c = cos[None, :, None, :]
    s = sin[None, :, None, :]
    return jnp.concatenate([x1 * c - x2 * s, x1 * s + x2 * c], axis=-1).astype(x.dtype)


def blocked_attention(q, k, v):
    B_, Lq, Hq, d = q.shape
    Hkv = k.shape[2]
    rep = Hq // Hkv
    nb = Lq // Q_BLOCK
    qb = q.reshape(B_, nb, Q_BLOCK, Hkv, rep, d).transpose(1, 0, 2, 3, 4, 5)
    scale = d ** -0.5

    def one(qblk):
        s = jnp.einsum('bqgrd,bkgd->bgrqk', qblk, k).astype(F32) * scale
        p = jax.nn.softmax(s, axis=-1).astype(v.dtype)
        return jnp.einsum('bgrqk,bkgd->bqgrd', p, v)

    o = lax.map(one, qb)
    return o.transpose(1, 0, 2, 3, 4, 5).reshape(B_, Lq, Hq, d)


def s5_discretize(lam_re, lam_im, log_dt, b_re, b_im):
    dt = jnp.exp(log_dt)[:, None]
    ar = lam_re * dt
    ai = lam_im * dt
    mag = jnp.exp(ar)
    abr = mag * jnp.cos(ai)
    abi = mag * jnp.sin(ai)
    nr = abr - 1.0
    ni = abi
    den = lam_re * lam_re + lam_im * lam_im
    f_re = (nr * lam_re + ni * lam_im) / den
    f_im = (ni * lam_re - nr * lam_im) / den
    bbr = f_re[..., None] * b_re - f_im[..., None] * b_im
    bbi = f_re[..., None] * b_im + f_im[..., None] * b_re
    return ar, ai, abr, abi, bbr, bbi


def s5_combine(e1, e2):
    a1r, a1i, b1r, b1i = e1
    a2r, a2i, b2r, b2i = e2
    return (a2r * a1r - a2i * a1i,
            a2r * a1i + a2i * a1r,
            a2r * b1r - a2i * b1i + b2r,
            a2r * b1i + a2i * b1r + b2i)


def s5_scan(ug, ar, ai, abr, abi, bbr, bbi, h0r, h0i, reverse):
    L = ug.shape[1]
    bur = jnp.einsum('gpk,blgk->blgp', bbr, ug)
    bui = jnp.einsum('gpk,blgk->blgp', bbi, ug)
    a_re = jnp.broadcast_to(abr, bur.shape)
    a_im = jnp.broadcast_to(abi, bur.shape)
    _, _, hr, hi = lax.associative_scan(s5_combine, (a_re, a_im, bur, bui), reverse=reverse, axis=1)
    if h0r is not None:
        t = jnp.arange(L, dtype=F32)
        n = ((L - t) if reverse else (t + 1.0))[:, None, None]
        mag = jnp.exp(n * ar)
        pr = mag * jnp.cos(n * ai)
        pim = mag * jnp.sin(n * ai)
        hr = hr + pr * h0r[:, None] - pim * h0i[:, None]
        hi = hi + pr * h0i[:, None] + pim * h0r[:, None]
    return hr, hi


def s5_mixer(u, lam_re, lam_im, log_dt, b_re, b_im, c_re, c_im, d_skip, glu_w, glu_b, h0):
    B_, L, _ = u.shape
    uf = u.astype(F32)
    ug = uf.reshape(B_, L, S5_GROUPS, S5_GROUP)
    y = uf * d_skip.astype(F32)
    finals = []
    for dr in range(2):
        ar, ai, abr, abi, bbr, bbi = s5_discretize(lam_re[dr].astype(F32), lam_im[dr].astype(F32),
                                                   log_dt[dr].astype(F32), b_re[dr].astype(F32),
                                                   b_im[dr].astype(F32))
        h0r = None if h0 is None else h0[:, dr, 0].astype(F32)
        h0i = None if h0 is None else h0[:, dr, 1].astype(F32)
        hr, hi = s5_scan(ug, ar, ai, abr, abi, bbr, bbi, h0r, h0i, reverse=(dr == 1))
        yd = (jnp.einsum('gkp,blgp->blgk', c_re[dr].astype(F32), hr)
              - jnp.einsum('gkp,blgp->blgk', c_im[dr].astype(F32), hi))
        y = y + yd.reshape(B_, L, D_MODEL)
        idx = L - 1 if dr == 0 else 0
        finals.append(jnp.stack([hr[:, idx], hi[:, idx]], axis=1))
    z = jax.nn.gelu(y).astype(u.dtype) @ glu_w + glu_b
    za, zg = jnp.split(z, 2, axis=-1)
    return za * jax.nn.sigmoid(zg), jnp.stack(finals, axis=1)


def na_qkv(h, w_qkv, qn, kn):
    B_, L, _ = h.shape
    qkv = (h @ w_qkv).reshape(B_, L, 3, NA_HEADS, NA_HEAD_DIM)
    return rms_norm(qkv[:, :, 0], qn), rms_norm(qkv[:, :, 1], kn), qkv[:, :, 2]


def na_context(h, w_qkv, qn, kn, w_o):
    B_, L, _ = h.shape
    q, k, v = na_qkv(h, w_qkv, qn, kn)
    o = blocked_attention(q, k, v)
    return o.reshape(B_, L, D_MODEL) @ w_o, k, v


def na_latent(h, w_qkv, qn, kn, rpb, w_o, k_ctx, v_ctx):
    B_, L, _ = h.shape
    rows = L // GRID_W
    kh = min(NA_KH, rows)
    ncb = GRID_W // NA_QCB
    H, d = NA_HEADS, NA_HEAD_DIM
    q, k, v = na_qkv(h, w_qkv, qn, kn)
    r = jnp.arange(rows)
    rs = jnp.clip(r - kh // 2, 0, rows - kh)
    ri = rs[:, None] + jnp.arange(kh)
    j = jnp.arange(ncb)
    cb = jnp.clip(j * NA_QCB - NA_KW // 2, 0, GRID_W - NA_KCB)
    ci = cb[:, None] + jnp.arange(NA_KCB)
    rsel = ri[:, None, :, None]
    csel = ci[None, :, None, :]
    kg = k.reshape(B_, rows, GRID_W, H, d)[:, rsel, csel].reshape(B_, rows, ncb, kh * NA_KCB, H, d)
    vg = v.reshape(B_, rows, GRID_W, H, d)[:, rsel, csel].reshape(B_, rows, ncb, kh * NA_KCB, H, d)
    qg = q.reshape(B_, rows, ncb, NA_QCB, H, d)
    scale = d ** -0.5
    s_loc = jnp.einsum('brjqhd,brjkhd->bhrjqk', qg, kg).astype(F32) * scale
    qc = j[:, None] * NA_QCB + jnp.arange(NA_QCB)
    cs = jnp.clip(qc - NA_KW // 2, 0, GRID_W - NA_KW)
    kc = ci[:, None, :]
    col_ok = (kc >= cs[:, :, None]) & (kc < cs[:, :, None] + NA_KW)
    dc = jnp.clip(kc - qc[:, :, None] + NA_KW - 1, 0, 2 * NA_KW - 2)
    drow = ri - r[:, None] + NA_KH - 1
    bias = rpb.astype(F32)[:, drow[:, None, None, :, None], dc[None, :, :, None, :]]
    bias = bias.reshape(H, rows, ncb, NA_QCB, kh * NA_KCB)
    mask = jnp.broadcast_to(col_ok[:, :, None, :], (ncb, NA_QCB, kh, NA_KCB)).reshape(ncb, NA_QCB, kh * NA_KCB)
    s_loc = jnp.where(mask, s_loc + bias, -jnp.inf)
    s_ctx = jnp.einsum('brjqhd,bchd->bhrjqc', qg, k_ctx).astype(F32) * scale
    nloc = kh * NA_KCB
    p = jax.nn.softmax(jnp.concatenate([s_loc, s_ctx], axis=-1), axis=-1).astype(v.dtype)
    o = (jnp.einsum('bhrjqk,brjkhd->brjqhd', p[..., :nloc], vg)
         + jnp.einsum('bhrjqc,bchd->brjqhd', p[..., nloc:], v_ctx))
    return o.reshape(B_, L, D_MODEL) @ w_o


def gqa_qkv(h, w_qkv, qn, kn):
    B_, L, _ = h.shape
    z = h @ w_qkv
    nq = GQA_Q_HEADS * GQA_HEAD_DIM
    nk = GQA_KV_HEADS * GQA_HEAD_DIM
    q = z[..., :nq].reshape(B_, L, GQA_Q_HEADS, GQA_HEAD_DIM)
    k = z[..., nq:nq + nk].reshape(B_, L, GQA_KV_HEADS, GQA_HEAD_DIM)
    v = z[..., nq + nk:].reshape(B_, L, GQA_KV_HEADS, GQA_HEAD_DIM)
    return rms_norm(q, qn), rms_norm(k, kn), v


def gqa_context(h, w_qkv, qn, kn, w_o):
    B_, L, _ = h.shape
    q, k, v = gqa_qkv(h, w_qkv, qn, kn)
    o = blocked_attention(q, k, v)
    return o.reshape(B_, L, D_MODEL) @ w_o, k, v


def gqa_latent(h, w_qkv, qn, kn, w_o, k_ctx, v_ctx):
    B_, L, _ = h.shape
    q, k, v = gqa_qkv(h, w_qkv, qn, kn)
    cos, sin = axial_rope_tables(L, GQA_HEAD_DIM)
    q = apply_rope(q, cos, sin)
    k = apply_rope(k, cos, sin)
    k_all = jnp.concatenate([k, k_ctx.astype(k.dtype)], axis=1)
    v_all = jnp.concatenate([v, v_ctx.astype(v.dtype)], axis=1)
    o = blocked_attention(q, k_all, v_all)
    return o.reshape(B_, L, D_MODEL) @ w_o


def setup_inputs(seed: int = 0) -> dict:
    key = jax.random.key(seed)
    ks = iter(jax.random.split(key, 48))

    def nrm(shape, s):
        return jax.random.normal(next(ks), shape, F32) * s

    G, P, K = S5_GROUPS, S5_STATE, S5_GROUP
    D = D_MODEL
    inp = {}
    inp['x_prompt'] = nrm((BATCH, SEQ, D), 1.0)
    inp['x_sample'] = nrm((DEC_BATCH, DEC_SEQ, D), 1.0)
    inp['state_s5'] = nrm((DEC_BATCH, N_S5, 2, 2, G, P), 0.1)
    inp['cache_na_k'] = nrm((DEC_BATCH, N_NA, PAST_LEN, NA_HEADS, NA_HEAD_DIM), 1.0)
    inp['cache_na_v'] = nrm((DEC_BATCH, N_NA, PAST_LEN, NA_HEADS, NA_HEAD_DIM), 1.0)
    inp['cache_gqa_k'] = nrm((DEC_BATCH, N_GQA, PAST_LEN, GQA_KV_HEADS, GQA_HEAD_DIM), 1.0)
    inp['cache_gqa_v'] = nrm((DEC_BATCH, N_GQA, PAST_LEN, GQA_KV_HEADS, GQA_HEAD_DIM), 1.0)
    inp['c'] = nrm((DEC_BATCH, D), 1.0)
    inp['c_ctx'] = nrm((D,), 1.0)
    inp['norm_g'] = 1.0 + nrm((DEPTH, 2, D), 0.01)
    inp['ada_w'] = nrm((DEPTH, D, 6 * D), D ** -0.5)
    inp['ada_b'] = nrm((DEPTH, 6 * D), 0.02)
    inp['mlp_w1'] = nrm((DEPTH, D, D_FF), D ** -0.5)
    inp['mlp_w2'] = nrm((DEPTH, D_FF, D), D_FF ** -0.5)
    inp['s5_lam_re'] = -0.5 + nrm((N_S5, 2, G, P), 0.01)
    inp['s5_lam_im'] = jnp.broadcast_to(jnp.pi * jnp.arange(P, dtype=F32), (N_S5, 2, G, P))
    inp['s5_log_dt'] = jax.random.uniform(next(ks), (N_S5, 2, G), F32,
                                          minval=math.log(S5_DT_MIN), maxval=math.log(S5_DT_MAX))
    inp['s5_b_re'] = nrm((N_S5, 2, G, P, K), (2.0 * K) ** -0.5)
    inp['s5_b_im'] = nrm((N_S5, 2, G, P, K), (2.0 * K) ** -0.5)
    inp['s5_c_re'] = nrm((N_S5, 2, G, K, P), (2.0 * P) ** -0.5)
    inp['s5_c_im'] = nrm((N_S5, 2, G, K, P), (2.0 * P) ** -0.5)
    inp['s5_d'] = nrm((N_S5, D), 0.5)
    inp['s5_glu_w'] = nrm((N_S5, D, 2 * D), D ** -0.5)
    inp['s5_glu_b'] = nrm((N_S5, 2 * D), 0.02)
    inp['na_w_qkv'] = nrm((N_NA, D, 3 * D), D ** -0.5)
    inp['na_q_norm'] = 1.0 + nrm((N_NA, NA_HEAD_DIM), 0.01)
    inp['na_k_norm'] = 1.0 + nrm((N_NA, NA_HEAD_DIM), 0.01)
    inp['na_rpb'] = nrm((N_NA, NA_HEADS, 2 * NA_KH - 1, 2 * NA_KW - 1), 0.1)
    inp['na_w_o'] = nrm((N_NA, D, D), D ** -0.5)
    inp['gqa_w_qkv'] = nrm((N_GQA, D, (GQA_Q_HEADS + 2 * GQA_KV_HEADS) * GQA_HEAD_DIM), D ** -0.5)
    inp['gqa_q_norm'] = 1.0 + nrm((N_GQA, GQA_HEAD_DIM), 0.01)
    inp['gqa_k_norm'] = 1.0 + nrm((N_GQA, GQA_HEAD_DIM), 0.01)
    inp['gqa_w_o'] = nrm((N_GQA, D, D), D ** -0.5)
    return inp


def reference(x_prompt, x_sample, state_s5, cache_na_k, cache_na_v, cache_gqa_k, cache_gqa_v,
              c, c_ctx, norm_g, ada_w, ada_b, mlp_w1, mlp_w2,
              s5_lam_re, s5_lam_im, s5_log_dt, s5_b_re, s5_b_im, s5_c_re, s5_c_im, s5_d,
              s5_glu_w, s5_glu_b, na_w_qkv, na_q_norm, na_k_norm, na_rpb, na_w_o,
              gqa_w_qkv, gqa_q_norm, gqa_k_norm, gqa_w_o):
    xp = x_prompt
    xs = x_sample
    new_s5, new_na_k, new_na_v, new_gqa_k, new_gqa_v = [], [], [], [], []
    for i in range(DEPTH):
        kind = i % N_MIXERS
        slot = i // N_MIXERS
        p_sh1, p_sc1, p_g1, p_sh2, p_sc2, p_g2 = modulation(c_ctx[None, :], ada_w[i], ada_b[i])
        s_sh1, s_sc1, s_g1, s_sh2, s_sc2, s_g2 = modulation(c, ada_w[i], ada_b[i])
        hp = rms_norm(xp, norm_g[i, 0]) * (1.0 + p_sc1) + p_sh1
        hs = rms_norm(xs, norm_g[i, 0]) * (1.0 + s_sc1) + s_sh1
        if kind == 0:
            s5_args = (s5_lam_re[slot], s5_lam_im[slot], s5_log_dt[slot], s5_b_re[slot], s5_b_im[slot],
                       s5_c_re[slot], s5_c_im[slot], s5_d[slot], s5_glu_w[slot], s5_glu_b[slot])
            yp, st = s5_mixer(hp, *s5_args, None)
            ys, _ = s5_mixer(hs, *s5_args, state_s5[:, slot])
            new_s5.append(st)
        elif kind == 1:
            yp, kc, vc = na_context(hp, na_w_qkv[slot], na_q_norm[slot], na_k_norm[slot], na_w_o[slot])
            ys = na_latent(hs, na_w_qkv[slot], na_q_norm[slot], na_k_norm[slot], na_rpb[slot], na_w_o[slot],
                           cache_na_k[:, slot], cache_na_v[:, slot])
            new_na_k.append(kc)
            new_na_v.append(vc)
        else:
            yp, kc, vc = gqa_context(hp, gqa_w_qkv[slot], gqa_q_norm[slot], gqa_k_norm[slot], gqa_w_o[slot])
            ys = gqa_latent(hs, gqa_w_qkv[slot], gqa_q_norm[slot], gqa_k_norm[slot], gqa_w_o[slot],
                            cache_gqa_k[:, slot], cache_gqa_v[:, slot])
            new_gqa_k.append(kc)
            new_gqa_v.append(vc)
        xp = xp + p_g1 * yp.astype(xp.dtype)
        xs = xs + s_g1 * ys.astype(xs.dtype)
        hp = rms_norm(xp, norm_g[i, 1]) * (1.0 + p_sc2) + p_sh2
        hs = rms_norm(xs, norm_g[i, 1]) * (1.0 + s_sc2) + s_sh2
        xp = xp + p_g2 * sqrelu_mlp(hp, mlp_w1[i], mlp_w2[i])
        xs = xs + s_g2 * sqrelu_mlp(hs, mlp_w1[i], mlp_w2[i])
    return (xp, xs, jnp.stack(new_s5, axis=1), jnp.stack(new_na_k, axis=1), jnp.stack(new_na_v, axis=1),
            jnp.stack(new_gqa_k, axis=1), jnp.stack(new_gqa_v, axis=1))
```

```python
import math
import numpy as np
import concourse.bass as bass
import concourse.mybir as mybir
from concourse.bass_utils import run_bass_kernel_spmd

F32 = mybir.dt.float32
BF16 = mybir.dt.bfloat16
AF = mybir.ActivationFunctionType
ALU = mybir.AluOpType

D = 1024
T = 1024
NF = 8
DFF = 4096
DEPTH = 4
EPS = 1e-6
NEG = -30000.0
EPOCH = 30000
ENGS = ('pe', 'act', 'dve', 'pool', 'sp')


class Res:
    __slots__ = ('name', 'w', 'r')

    def __init__(self, name=''):
        self.name = name
        self.w = None
        self.r = []


class DrySched:
    def __init__(self):
        self.count = {e: 0 for e in ENGS}
        self.nsem = 0

    def op(self, *a, **k):
        return None

    def dma(self, *a, **k):
        return None

    def barrier(self):
        pass

    def finish(self):
        pass

    def emit(self):
        pass


class Sched:
    def __init__(self, nc):
        self.nc = nc
        self.ops = {e: [] for e in ENGS}
        self.count = {e: 0 for e in ENGS}
        self.seen = {e: {} for e in ENGS}
        self.esem = {e: [] for e in ENGS}
        self.dsem = {}
        self.out_tokens = []
        self.nsem = 0
        self.pending = {e: [] for e in ENGS}

    def _esem(self, e, epoch):
        while len(self.esem[e]) <= epoch:
            self.esem[e].append(self.nc.alloc_semaphore(f"se_{e}_{len(self.esem[e])}"))
            self.nsem += 1
        return self.esem[e][epoch]

    def _collect(self, eng, reads, writes, extra=()):
        toks = list(extra)
        for r in reads:
            if r.w is not None:
                toks.append(r.w)
        for w in writes:
            if w.w is not None:
                toks.append(w.w)
            toks.extend(w.r)
        need = {}
        for t in toks:
            if t[0] == 'e':
                _, e2, idx = t
                if e2 == eng and eng == 'pe':
                    continue
                key = ('e', e2)
                val = idx
            else:
                key = ('d', t[1])
                val = t[2]
            if self.seen[eng].get(key, 0) >= val:
                continue
            if need.get(key, 0) < val:
                need[key] = val
        waits = []
        for key, val in need.items():
            self.seen[eng][key] = val
            if key[0] == 'e':
                ep = (val - 1) // EPOCH
                waits.append((self._esem(key[1], ep), (val - 1) % EPOCH + 1))
            else:
                waits.append((self.dsem[key[1]][0], val))
        return waits

    def _mark(self, tok, reads, writes):
        for r in reads:
            r.r.append(tok)
        for w in writes:
            w.w = tok
            w.r = []

    def op(self, eng, fn, reads=(), writes=(), extra=()):
        extra = list(extra) + self.pending[eng]
        self.pending[eng] = []
        waits = self._collect(eng, reads, writes, extra)
        self.count[eng] += 1
        idx = self.count[eng]
        sem = self._esem(eng, (idx - 1) // EPOCH)
        self.ops[eng].append((waits, fn, sem, 1))
        tok = ('e', eng, idx)
        self._mark(tok, reads, writes)
        return tok

    def dma(self, eng, fn, key, reads=(), writes=(), is_output=False):
        extra = self.pending[eng]
        self.pending[eng] = []
        waits = self._collect(eng, reads, writes, extra)
        kid = id(key)
        if kid not in self.dsem:
            self.dsem[kid] = [self.nc.alloc_semaphore(f"sd_{len(self.dsem)}"), 0]
            self.nsem += 1
        ent = self.dsem[kid]
        ent[1] += 16
        self.ops[eng].append((waits, fn, ent[0], 16))
        tok = ('d', kid, ent[1])
        self._mark(tok, reads, writes)
        if is_output:
            self.out_tokens.append(tok)
        return tok

    def barrier(self):
        toks = [('e', e, self.count[e]) for e in ENGS if self.count[e] > 0]
        for e in ENGS:
            self.pending[e] = self.pending[e] + toks

    def finish(self):
        toks = list(self.out_tokens)
        waits = self._collect('sp', (), (), toks)
        self.ops['sp'].append((waits, None, None, 0))

    def emit(self):
        nc = self.nc
        ops = self.ops

        def replay(eng_obj, lst):
            for waits, fn, sem, inc in lst:
                for s, v in waits:
                    eng_obj.wait_ge(s, v)
                if fn is None:
                    continue
                ins = fn(eng_obj)
                if sem is not None:
                    ins.then_inc(sem, inc)

        with nc.Block() as block:
            @block.tensor
            def _(e):
                replay(e, ops['pe'])

            @block.scalar
            def _(e):
                replay(e, ops['act'])

            @block.vector
            def _(e):
                replay(e, ops['dve'])

            @block.gpsimd
            def _(e):
                replay(e, ops['pool'])

            @block.sync
            def _(e):
                replay(e, ops['sp'])


class ColMap:
    def __init__(self):
        self.m = {}
        self.n = 0

    def add(self, name, w):
        self.m[name] = (self.n, w)
        self.n += w

    def sl(self, name):
        a, w = self.m[name]
        return slice(a, a + w)


def fm(v):
    v = np.asarray(v, np.float32)
    return np.ascontiguousarray(v.reshape(-1, 128).T)


def small_colmap():
    cm = ColMap()
    cm.add('cvec', 8)
    for i in range(DEPTH):
        cm.add(f'g1_{i}', 8)
        cm.add(f'g2_{i}', 8)
        cm.add(f'adab_{i}', 48)
    for s in range(2):
        cm.add(f's5d_{s}', 8)
        cm.add(f'glub_{s}', 16)
    for nm in ('naq', 'nak', 'gq', 'gk', 'ctxb', 'carry'):
        cm.add(nm, 1)
    return cm


CONST_BF = ColMap()
for _nm in ('ones1024', 'ones128', 'blk64', 'ones', 'ident', 'pswap', 'anti'):
    CONST_BF.add(_nm, 128)


def const_bf_array():
    a = np.zeros((128, CONST_BF.n), np.float32)
    a[:, CONST_BF.sl('ones1024')] = 1.0 / 1024
    a[:, CONST_BF.sl('ones128')] = 1.0 / 128
    b = np.zeros((128, 128), np.float32)
    b[:64, :64] = 1.0 / 64
    b[64:, 64:] = 1.0 / 64
    a[:, CONST_BF.sl('blk64')] = b
    a[:, CONST_BF.sl('ones')] = 1.0
    a[:, CONST_BF.sl('ident')] = np.eye(128, dtype=np.float32)
    a[:, CONST_BF.sl('pswap')] = np.roll(np.eye(128, dtype=np.float32), 64, axis=0)
    a[:, CONST_BF.sl('anti')] = np.eye(128, dtype=np.float32)[::-1]
    return a


class Bank:
    def __init__(self, t, i):
        self.t = t
        self.res = Res(f'bank{i}')
        self.fresh = True


class Builder:
    def __init__(self, layers=(0, 1, 2, 3), mixers=True, plan=None):
        self.layers = layers
        self.mixers = mixers
        self.plan = plan
        self.nc = bass.Bass("TRN2", target_bir_lowering=False)
        self.s = Sched(self.nc) if plan is not None else DrySched()
        self.rec_tags = []
        self.cm = small_colmap()
        self.dram = {}
        self.wq = []
        self.wq_issued = 0

    def din(self, name, shape, dt=F32):
        t = self.nc.dram_tensor(name, list(shape), dt, kind="ExternalInput").ap()
        self.dram[name] = t
        return t

    def dout(self, name, shape, dt=F32):
        t = self.nc.dram_tensor(name, list(shape), dt, kind="ExternalOutput").ap()
        self.dram[name] = t
        return t

    def sb(self, name, shape, dt):
        return self.nc.alloc_sbuf_tensor(name, list(shape), dt)

    def mm(self, bank, out, lhsT, rhs, reads):
        st = bank.fresh
        bank.fresh = False
        return self.s.op('pe', lambda e: e.matmul(out, lhsT, rhs, start=st, stop=True, skip_group_check=True),
                         reads=reads, writes=[bank.res])

    def newbank(self):
        rot = getattr(self, 'rot', None) or list(range(8))
        b = self.banks[rot[self.bank_i % len(rot)]]
        self.bank_i += 1
        b.fresh = True
        return b

    def act(self, out, in_, func, reads, writes, bias=0.0, scale=1.0):
        return self.s.op('act', lambda e: e.activation(out=out, in_=in_, func=func, bias=bias, scale=scale),
                         reads=reads, writes=writes)

    def tt(self, eng, out, a, b, op, reads, writes):
        return self.s.op(eng, lambda e: e.tensor_tensor(out, a, b, op), reads=reads, writes=writes)

    def ts(self, eng, out, a, s1, s2, op0, op1, reads, writes):
        return self.s.op(eng, lambda e: e.tensor_scalar(out, a, s1, s2, op0, op1), reads=reads, writes=writes)

    def stt(self, eng, out, a, sc, b, op0, op1, reads, writes):
        return self.s.op(eng, lambda e: e.scalar_tensor_tensor(out, a, sc, b, op0, op1), reads=reads, writes=writes)

    def rsqrt(self, out, in_, reads, writes):
        self.s.op('act', lambda e: e.activation(out=out, in_=in_, func=AF.Ln, bias=self.EPSC[:, 0:1], scale=1.0),
                  reads=list(reads) + [self.EPSres], writes=writes)
        self.s.op('act', lambda e: e.activation(out=out, in_=out, func=AF.Exp, bias=0.0, scale=-0.5),
                  reads=writes, writes=writes)

    def cp(self, eng, out, in_, reads, writes):
        if eng == 'act':
            return self.s.op('act', lambda e: e.copy(out, in_), reads=reads, writes=writes)
        return self.s.op(eng, lambda e: e.tensor_copy(out, in_), reads=reads, writes=writes)

    def dma(self, eng, out, in_, key, reads=(), writes=(), is_output=False):
        return self.s.dma(eng, lambda e: e.dma_start(out=out, in_=in_), key, reads=reads, writes=writes,
                          is_output=is_output)

    def wq_add(self, src_ap, shape):
        self.wq.append((src_ap, shape))
        return len(self.wq) - 1

    def wq_get(self, idx):
        while self.wq_issued < min(len(self.wq), idx + self.NSLOT):
            j = self.wq_issued
            slot = j % self.NSLOT
            src, shape = self.wq[j]
            n = shape[1] * shape[2]
            dst = self.wslot[slot][:, 0:n].rearrange("p (a b) -> p a b", a=shape[1])
            if isinstance(src, list):
                wpart = shape[2] // len(src)
                for sap, part in src:
                    self.dma('pool', dst[:, :, part * wpart:(part + 1) * wpart], sap, self.wres[slot],
                             writes=[self.wres[slot]])
            else:
                self.dma('pool', dst, src, self.wres[slot], writes=[self.wres[slot]])
            self.wq_issued += 1
        slot = idx % self.NSLOT
        src, shape = self.wq[idx]
        n = shape[1] * shape[2]
        return self.wslot[slot][:, 0:n].rearrange("p (a b) -> p a b", a=shape[1]), self.wres[slot]

    def build(self):
        nc = self.nc
        cm = self.cm
        s = self.s
        xT = self.din('xT', [D, T])
        smallp = self.din('smallp', [128, cm.n])
        cbf = self.din('cbf', [128, CONST_BF.n])
        identf = self.din('identf', [128, 128])
        ada_w = self.din('ada_w', [DEPTH, D, 6 * D])
        mlp_w1 = self.din('mlp_w1', [DEPTH, D, DFF])
        mlp_w2 = self.din('mlp_w2', [DEPTH, DFF, D])
        yT = self.dout('yT', [D, T])
        self.declare_mixer_dram()

        self.X = self.sb('X', [128, NF, T], F32)
        self.Xres = [Res(f'X{j}') for j in range(NF)]
        self.HT = self.sb('HT', [128, NF, T], BF16)
        self.HTres = [Res(f'HT{j}') for j in range(NF)]
        self.SPt = self.sb('SPt', [128, cm.n], F32)
        self.SPres = Res('SP')
        self.CB = self.sb('CB', [128, CONST_BF.n], BF16)
        self.CBres = Res('CB')
        self.IDF = self.sb('IDF', [128, 128], F32)
        self.NSLOT = 3
        self.wslot = [self.sb(f'wslot{k}', [128, 6144], BF16) for k in range(self.NSLOT)]
        self.wres = [Res(f'w{k}') for k in range(self.NSLOT)]
        self.TMP = [self.sb(f'TMP{k}', [128, T], F32) for k in range(2)]
        self.TMPres = [Res(f'TMP{k}') for k in range(2)]
        self.SQ = [self.sb(f'SQ{k}', [128, T], BF16) for k in range(2)]
        self.SQres = [Res(f'SQ{k}') for k in range(2)]
        self.RSTD = self.sb('RSTD', [128, T], F32)
        self.RSTDres = Res('RSTD')
        self.MODTS = [self.sb(f'MODT{k}', [128, 48], F32) for k in range(2)]
        self.MODress = [Res(f'MOD{k}') for k in range(2)]
        self.GSC = self.sb('GSC', [128, 8], F32)
        self.GSCres = Res('GSC')
        self.SIL = self.sb('SIL', [128, 8], BF16)
        self.SILres = Res('SIL')
        self.SCR = self.sb('SCR', [128, 32 * T], BF16)
        self.HID = self.SCR[:, :].rearrange("p (a b) -> p a b", a=32)
        self.HIDres = [Res(f'HID{k}') for k in range(32)]
        self.EPSC = self.sb('EPSC', [128, 1], F32)
        self.EPSres = Res('EPS')
        self.s.op('dve', lambda e: e.memset(self.EPSC[:, :], EPS), writes=[self.EPSres])
        self.RL = [self.sb(f'RL{k}', [128, 512], F32) for k in range(2)]
        self.RLres = [Res(f'RL{k}') for k in range(2)]
        self.rl_i = 0
        self.alloc_mixer_sbuf()
        self.banks = [Bank(nc.alloc_psum_tensor(f'ps{k}', [128, 512], F32), k) for k in range(8)]
        self.bank_i = 0

        self.ada_w, self.mlp_w1, self.mlp_w2 = ada_w, mlp_w1, mlp_w2
        self.wtags = list(self.plan) if self.plan is not None else None
        if self.wtags is not None:
            for tag in self.wtags:
                src, shape = self.wsrc(tag)
                self.wq_add(src, shape)
        self.wnext = 0

        self.dma('sp', self.SPt[:, :], smallp, self.SPres, writes=[self.SPres])
        self.dma('pool', self.CB[:, :], cbf, self.CBres, writes=[self.CBres])
        self.IDFres = Res('IDF')
        self.dma('sp', self.IDF[:, :], identf, self.IDFres, writes=[self.IDFres])
        for j in range(NF):
            self.dma('sp', self.X[:, j, :], xT[j * 128:(j + 1) * 128, :], self.Xres[j], writes=[self.Xres[j]])
        self.act(self.SIL[:, :], self.SPt[:, cm.sl('cvec')], AF.Silu, [self.SPres], [self.SILres])
        self.load_mixer_inputs()

        self.MODT = self.MODTS[0]
        self.MODres = self.MODress[0]
        self.mod_pending = []
        for ch in range(8):
            self.modulation_chunk(self.layers[0], ch, 0)
        if self.mixers and self.layers[0] % 3 == 0:
            self.s5_prefetch_params(self.layers[0] // 3)
        for li, i in enumerate(self.layers):
            self.MODT = self.MODTS[li % 2]
            self.MODres = self.MODress[li % 2]
            if li + 1 < len(self.layers):
                self.mod_pending = [(self.layers[li + 1], ch, (li + 1) % 2) for ch in range(8)]
            self.norm_mod(f'g1_{i}', 8, 0)
            if self.mixers:
                self.mixer(i)
            self.norm_mod(f'g2_{i}', 32 + 0, 24)
            self.mod_hook(8)
            if self.mixers and li + 1 < len(self.layers) and self.layers[li + 1] % 3 == 0:
                self.s5_prefetch_params(self.layers[li + 1] // 3)
            self.mlp(i)

        for j in range(NF):
            self.dma('sp', yT[j * 128:(j + 1) * 128, :], self.X[:, j, :], self.Xres[j], reads=[self.Xres[j]],
                     is_output=True)
        self.finalize_mixer_outputs()
        s.finish()
        s.emit()
        return nc

    def next_w(self, tag):
        if self.wtags is None:
            self.rec_tags.append(tag)
            _, shape = self.wsrc(tag)
            n = shape[1] * shape[2]
            return self.wslot[0][:, 0:n].rearrange("p (a b) -> p a b", a=shape[1]), self.wres[0]
        assert self.wtags[self.wnext] == tag, (self.wtags[self.wnext], tag)
        ap, res = self.wq_get(self.wnext)
        self.wnext += 1
        return ap, res

    def wsrc(self, tag):
        kind, i, ch = tag
        pk = lambda w: w.rearrange("(kt p) n -> p kt n", p=128)
        if kind == 'ada':
            return pk(self.ada_w[i])[:, :, ch * 768:(ch + 1) * 768], [128, 8, 768]
        if kind == 'w1':
            return pk(self.mlp_w1[i])[:, :, ch * 512:(ch + 1) * 512], [128, 8, 512]
        if kind == 'w2':
            return pk(self.mlp_w2[i])[:, :, ch * 128:(ch + 1) * 128], [128, 32, 128]
        return self.wsrc_mixer(tag)

    def modulation_chunk(self, i, ch, buf):
        b = self.newbank()
        w, wres = self.next_w(('ada', i, ch))
        for m6 in range(6):
            for kt in range(8):
                self.mm(b, b.t[:, m6:m6 + 1], w[:, kt, m6 * 128:(m6 + 1) * 128], self.SIL[:, kt:kt + 1],
                        [wres, self.SILres])
        a0 = self.cm.m[f'adab_{i}'][0] + ch * 6
        self.tt('dve', self.MODTS[buf][:, ch * 6:ch * 6 + 6], b.t[:, 0:6], self.SPt[:, a0:a0 + 6], ALU.add,
                [b.res, self.SPres], [self.MODress[buf]])

    def mod_hook(self, n=1):
        for _ in range(n):
            if self.mod_pending:
                self.modulation_chunk(*self.mod_pending.pop(0))

    def cb(self, name):
        return self.CB[:, CONST_BF.sl(name)]

    def norm_mod(self, gname, sc0, sh0):
        cm = self.cm
        self.stt('dve', self.GSC[:, :], self.MODT[:, sc0:sc0 + 8], 1.0, self.SPt[:, cm.sl(gname)], ALU.add, ALU.mult,
                 [self.MODres, self.SPres], [self.GSCres])
        bs = [self.newbank(), self.newbank()]
        for j in range(NF):
            k = j % 2
            self.act(self.SQ[k][:, :], self.X[:, j, :], AF.Square, [self.Xres[j]], [self.SQres[k]])
            for h in range(2):
                self.mm(bs[h], bs[h].t[:, :], self.cb('ones1024'), self.SQ[k][:, h * 512:(h + 1) * 512],
                        [self.SQres[k], self.CBres])
        for h in range(2):
            self.rsqrt(self.RSTD[:, h * 512:(h + 1) * 512], bs[h].t[:, :], [bs[h].res], [self.RSTDres])
        for j in range(NF):
            k = j % 2
            self.tt('dve', self.TMP[k][:, :], self.X[:, j, :], self.RSTD[:, :], ALU.mult,
                    [self.Xres[j], self.RSTDres], [self.TMPres[k]])
            self.act(self.HT[:, j, :], self.TMP[k][:, :], AF.Identity, [self.TMPres[k], self.GSCres, self.MODres],
                     [self.HTres[j]], bias=self.MODT[:, sh0 + j:sh0 + j + 1], scale=self.GSC[:, j:j + 1])

    def mlp(self, i):
        for ch in range(8):
            w, wres = self.next_w(('w1', i, ch))
            for m4 in range(4):
                mt = ch * 4 + m4
                for h in range(2):
                    b = self.newbank()
                    for kt in range(8):
                        self.mm(b, b.t[:, :], w[:, kt, m4 * 128:(m4 + 1) * 128], self.HT[:, kt, h * 512:(h + 1) * 512],
                                [wres, self.HTres[kt]])
                    k = self.rl_i % 2
                    self.rl_i += 1
                    self.act(self.RL[k][:, :], b.t[:, :], AF.Relu, [b.res], [self.RLres[k]])
                    self.tt('dve' if k == 0 else 'pool', self.HID[:, mt, h * 512:(h + 1) * 512], self.RL[k][:, :],
                            self.RL[k][:, :], ALU.mult, [self.RLres[k]], [self.HIDres[mt]])
        for ch in range(8):
            w, wres = self.next_w(('w2', i, ch))
            for m2 in range(1):
                mt = ch
                for h in range(2):
                    b = self.newbank()
                    for kt in range(32):
                        self.mm(b, b.t[:, :], w[:, kt, m2 * 128:(m2 + 1) * 128], self.HID[:, kt, h * 512:(h + 1) * 512],
                                [wres, self.HIDres[kt]])
                    self.stt('dve', self.X[:, mt, h * 512:(h + 1) * 512], b.t[:, :], self.MODT[:, 40 + mt:41 + mt],
                             self.X[:, mt, h * 512:(h + 1) * 512], ALU.mult, ALU.add,
                             [b.res, self.MODres, self.Xres[mt]], [self.Xres[mt]])

    def declare_mixer_dram(self):
        pass

    def alloc_mixer_sbuf(self):
        pass

    def plan_mixer_weights(self, i):
        pass

    def load_mixer_inputs(self):
        pass

    def mixer(self, i):
        pass

    def finalize_mixer_outputs(self):
        pass


def core_tokens(inp, core):
    if core < 4:
        return np.asarray(inp['x_sample'][core], np.float32), np.asarray(inp['c'][core], np.float32)
    b0 = 4 * (core - 4)
    return (np.asarray(inp['x_prompt'][b0:b0 + 4], np.float32).reshape(T, D),
            np.asarray(inp['c_ctx'], np.float32))


def small_array(inp, core, cm):
    a = np.zeros((128, cm.n), np.float32)
    _, cvec = core_tokens(inp, core)
    a[:, cm.sl('cvec')] = fm(cvec)
    for i in range(DEPTH):
        a[:, cm.sl(f'g1_{i}')] = fm(inp['norm_g'][i, 0])
        a[:, cm.sl(f'g2_{i}')] = fm(inp['norm_g'][i, 1])
        a[:, cm.sl(f'adab_{i}')] = fm(inp['ada_b'][i])
    for sl in range(2):
        a[:, cm.sl(f's5d_{sl}')] = fm(inp['s5_d'][sl])
        a[:, cm.sl(f'glub_{sl}')] = fm(inp['s5_glu_b'][sl])
    a[:, cm.sl('naq')] = np.tile(np.asarray(inp['na_q_norm'][0], np.float32), 2)[:, None]
    a[:, cm.sl('nak')] = np.tile(np.asarray(inp['na_k_norm'][0], np.float32), 2)[:, None]
    a[:, cm.sl('gq')] = np.asarray(inp['gqa_q_norm'][0], np.float32)[:, None]
    a[:, cm.sl('gk')] = np.asarray(inp['gqa_k_norm'][0], np.float32)[:, None]
    a[:, cm.sl('ctxb')] = 0.0 if core < 4 else NEG
    a[:, cm.sl('carry')] = 1.0 if core < 4 else 0.0
    return a


def common_inputs(inp, core, cm):
    x, _ = core_tokens(inp, core)
    m = {
        'xT': np.ascontiguousarray(x.T),
        'smallp': small_array(inp, core, cm),
        'cbf': const_bf_array(),
        'identf': np.eye(128, dtype=np.float32),
        'ada_w': np.asarray(inp['ada_w'], np.float32),
        'mlp_w1': np.asarray(inp['mlp_w1'], np.float32),
        'mlp_w2': np.asarray(inp['mlp_w2'], np.float32),
    }
    return m


NA_TILES = [(kt, hf) for kt in range(8) for hf in range(2)
            if not ((kt < 2 and hf == 1) or (kt >= 6 and hf == 0))]
NA_TILE_IDX = {t: n for n, t in enumerate(NA_TILES)}


class FullBuilder(Builder):
    def declare_mixer_dram(self):
        self.na_wqkv = self.din('na_w_qkv', [D, 3 * D])
        self.na_wo = self.din('na_w_o', [D, D])
        self.gqa_wqkv = self.din('gqa_w_qkv', [D, 1536])
        self.gqa_wo = self.din('gqa_w_o', [D, D])
        self.s5_gluw = self.din('s5_glu_w', [2, D, 2 * D])
        self.na_kcT = self.din('na_kcT', [D, 512])
        self.na_vc = self.din('na_vc', [512, D])
        self.gqa_kcT = self.din('gqa_kcT', [256, 512])
        self.gqa_vc = self.din('gqa_vc', [512, 256])
        self.na_bias = self.din('na_bias', [16, len(NA_TILES), 128, 512])
        self.gmask = self.din('gmask', [4, 2048])
        self.rope = self.din('rope', [2, 128, T])
        self.o_nak = self.dout('o_nak', [T, D])
        self.o_nav = self.dout('o_nav', [T, D])
        self.o_gk = self.dout('o_gk', [T, 256])
        self.o_gv = self.dout('o_gv', [T, 256])
        self.declare_s5_dram()

    def alloc_mixer_sbuf(self):
        self.PT = [self.sb(f'PT{k}', [128, 512], BF16) for k in range(3)]
        self.PTres = [Res(f'PT{k}') for k in range(3)]
        self.PT += [self.RL[0][:, :].bitcast(BF16)[:, 0:512], self.RL[1][:, :].bitcast(BF16)[:, 0:512]]
        self.PTres += [self.RLres[0], self.RLres[1]]
        self.BT = [self.sb(f'BT{k}', [128, 512], BF16) for k in range(3)]
        self.BTres = [Res(f'BT{k}') for k in range(3)]
        self.RC = self.sb('RC', [128, 512], F32)
        self.RCres = Res('RC')
        self.VF = [self.sb(f'VF{k}', [128, 1024], F32) for k in range(2)]
        self.VFres = [Res(f'VF{k}') for k in range(2)]
        self.SQt = [self.SQ[0][:, 0:512], self.SQ[0][:, 512:1024], self.SQ[1][:, 0:512], self.SQ[1][:, 512:1024]]
        self.SQtres = [Res(f'SQt{k}') for k in range(4)]
        self.RS = [self.RSTD[:, 0:512], self.RSTD[:, 512:1024]]
        self.RSres = [Res(f'RS{k}') for k in range(2)]
        self.pt_i = 0
        self.bt_i = 0
        self.sq_i = 0
        self.rs_i = 0
        self.vf_i = 0
        self.kf_i = 0
        S = self.SCR
        self.QT = S[:, 0:8192].rearrange("p (a b) -> p a b", a=8)
        self.QTres = [Res(f'QT{j}') for j in range(8)]
        self.KT_na = S[:, 8192:20480].rearrange("p (a b) -> p a b", a=8)
        self.V_na = S[:, 20480:32768].rearrange("p (a b) -> p a b", a=12)
        self.KT_g = S[:, 8192:11264].rearrange("p (a b) -> p a b", a=2)
        self.V_g = S[:, 11264:14336].rearrange("p (a b) -> p a b", a=12)
        self.CC = S[:, 14336:16384].bitcast(F32)
        self.SS = S[:, 16384:18432].bitcast(F32)
        self.GM = S[:, 18432:20480]
        self.KTres = [Res(f'KT{j}') for j in range(8)]
        self.KCres = Res('KC')
        self.Vres = [Res(f'V{j}') for j in range(12)]
        self.ROPEres = Res('ROPE')
        self.GMres = Res('GM')
        self.alloc_s5_sbuf()

    def wsrc_mixer(self, tag):
        kind, i, ch = tag
        pk = lambda w: w.rearrange("(kt p) n -> p kt n", p=128)
        if kind == 'naqkv':
            return pk(self.na_wqkv)[:, :, ch * 512:(ch + 1) * 512], [128, 8, 512]
        if kind == 'gqkv':
            return pk(self.gqa_wqkv)[:, :, ch * 512:(ch + 1) * 512], [128, 8, 512]
        if kind == 'wo':
            w = self.na_wo if i % 3 == 1 else self.gqa_wo
            return pk(w)[:, :, ch * 512:(ch + 1) * 512], [128, 8, 512]
        assert kind == 'glu'
        sl = i // 3
        srcs = [(pk(self.s5_gluw[sl])[:, :, half * 1024 + ch * 256:half * 1024 + (ch + 1) * 256], half)
                for half in range(2)]
        return srcs, [128, 8, 512]

    def load_mixer_inputs(self):
        pass

    def mixer(self, i):
        kind = i % 3
        self.s.barrier()
        if kind == 1:
            self.na_layer(i)
        elif kind == 2:
            self.gqa_layer(i)
        else:
            self.s5_layer(i)
        self.s.barrier()

    def proj_fm(self, w, wres, m4, h):
        b = self.newbank()
        for kt in range(8):
            self.mm(b, b.t[:, :], w[:, kt, m4 * 128:(m4 + 1) * 128], self.HT[:, kt, h * 512:(h + 1) * 512],
                    [wres, self.HTres[kt]])
        return b

    def qknorm(self, b, onesname, gain_col, out, out_res, out_reads=()):
        k = self.sq_i % 4
        self.sq_i += 1
        self.act(self.SQt[k], b.t[:, :], AF.Square, [b.res], [self.SQtres[k]])
        bm = self.newbank()
        self.mm(bm, bm.t[:, :], self.cb(onesname), self.SQt[k], [self.SQtres[k], self.CBres])
        r = self.rs_i % 2
        self.rs_i += 1
        self.rsqrt(self.RS[r], bm.t[:, :], [bm.res], [self.RSres[r]])
        self.stt('dve', out, b.t[:, :], self.SPt[:, self.cm.sl(gain_col)], self.RS[r], ALU.mult, ALU.mult,
                 [b.res, self.RSres[r], self.SPres] + list(out_reads), [out_res])

    def qk_pipeline(self, items, onesname, stage3, cur=None):
        st = {}
        N = len(items)
        cur = cur or [None, None, None]
        for n in range(N + 2):
            if n < N:
                tag, m4, h, ctx = items[n]
                if cur[0] != tag:
                    w, wres = self.next_w(tag)
                    cur = [tag, w, wres]
                b = self.proj_fm(cur[1], cur[2], m4, h)
                k = self.sq_i % 4
                self.sq_i += 1
                self.act(self.SQt[k], b.t[:, :], AF.Square, [b.res], [self.SQtres[k]])
                st[n] = [b, k, None]
            if 0 <= n - 1 < N:
                b, k, _ = st[n - 1]
                bm = self.newbank()
                self.mm(bm, bm.t[:, :], self.cb(onesname), self.SQt[k], [self.SQtres[k], self.CBres])
                r = self.rs_i % 2
                self.rs_i += 1
                self.rsqrt(self.RS[r], bm.t[:, :], [bm.res], [self.RSres[r]])
                st[n - 1][2] = r
            if 0 <= n - 2 < N:
                b, k, r = st.pop(n - 2)
                stage3(items[n - 2][3], b, r)
        return cur

    def emit_k_out(self, KF, KFres, odram, j):
        k = self.vf_i % 2
        self.vf_i += 1
        for half in range(2):
            b = self.newbank()
            for t4 in range(4):
                tt_ = half * 4 + t4
                self.mm(b, b.t[:, t4 * 128:(t4 + 1) * 128], KF[:, tt_ * 128:(tt_ + 1) * 128], self.IDF[:, :],
                        [KFres, self.IDFres])
            self.cp('act', self.VF[k][:, half * 512:(half + 1) * 512], b.t[:, :], [b.res], [self.VFres[k]])
        dst = odram.rearrange("(tt p) f -> p tt f", p=128)[:, :, j * 128:(j + 1) * 128]
        self.dma('sp', dst, self.VF[k][:, :].rearrange("p (a b) -> p a b", a=8), self.VFres[k],
                 reads=[self.VFres[k]], is_output=True)

    def v_proj(self, w, wres, c0, ncols, Vt, vcol0, odram):
        for tt_ in range(8):
            b = self.newbank()
            for kt in range(8):
                self.mm(b, b.t[:, 0:ncols], self.HT[:, kt, tt_ * 128:(tt_ + 1) * 128], w[:, kt, c0:c0 + ncols],
                        [wres, self.HTres[kt]])
            k = self.vf_i % 2
            self.vf_i += 1
            self.cp('act', self.VF[k][:, 0:ncols], b.t[:, 0:ncols], [b.res], [self.VFres[k]])
            self.cp('pool', Vt[:, tt_, vcol0:vcol0 + ncols], self.VF[k][:, 0:ncols], [self.VFres[k]], [self.Vres[tt_]])
            self.dma('sp', odram[tt_ * 128:(tt_ + 1) * 128, vcol0:vcol0 + ncols], self.VF[k][:, 0:ncols], self.VFres[k],
                     reads=[self.VFres[k]], is_output=True)

    def wo_proj(self, i, src):
        for ch in range(2):
            w, wres = self.next_w(('wo', i, ch))
            for m4 in range(4):
                mt = ch * 4 + m4
                for h in range(2):
                    b = self.newbank()
                    for kt in range(8):
                        self.mm(b, b.t[:, :], w[:, kt, m4 * 128:(m4 + 1) * 128], src[:, kt, h * 512:(h + 1) * 512],
                                [wres, self.HTres[kt]])
                    self.stt('dve', self.X[:, mt, h * 512:(h + 1) * 512], b.t[:, :], self.MODT[:, 16 + mt:17 + mt],
                             self.X[:, mt, h * 512:(h + 1) * 512], ALU.mult, ALU.add,
                             [b.res, self.MODres, self.Xres[mt]], [self.Xres[mt]])

    def attention(self, kind):
        na = kind == 'na'
        nheads = 16 if na else 8
        dh = 64 if na else 128
        scale = dh ** -0.5
        KT = self.KT_na if na else self.KT_g
        V = self.V_na if na else self.V_g
        self.rot = [0, 1, 2, 3]
        self.bank_i = 0
        steps = []
        for hd in range(nheads):
            for hf in range(2):
                tiles = [kt for kt in range(12) if (not na) or kt >= 8 or (kt, hf) in NA_TILE_IDX]
                for n, kt in enumerate(tiles):
                    steps.append((hd, hf, kt, n == 0, n == len(tiles) - 1))
        bias_steps = [st for st in steps if na and st[2] < 8]
        bias_slot = {}
        self._bias_n = 0

        def issue_bias(upto):
            while self._bias_n < min(len(bias_steps), upto):
                hd, hf, kt = bias_steps[self._bias_n][:3]
                bi = self._bias_n % 3
                self.dma('pool', self.BT[bi][:, :], self.na_bias[hd, NA_TILE_IDX[(kt, hf)]], self.BTres[bi],
                         writes=[self.BTres[bi]])
                bias_slot[(hd, hf, kt)] = bi
                self._bias_n += 1

        LA = 4
        NPT = 5
        st_info = {}
        nbias = [0]
        acc = [0]

        def front(n):
            hd, hf, kt, first, last = steps[n]
            if na:
                ht, pr = hd // 2, slice(64 * (hd % 2), 64 * (hd % 2) + 64)
                ktile = ht
            else:
                ht, pr = hd, slice(0, 128)
                ktile = hd // 4
            qs = slice(hf * 512, hf * 512 + 512)
            bs_ = self.newbank()
            ks = slice(kt * 128, kt * 128 + 128)
            kres = self.KTres[ktile] if kt < 8 else self.KCres
            self.mm(bs_, bs_.t[:, :], KT[pr, ktile, ks], self.QT[pr, ht, qs], [kres, self.QTres[ht]])
            p = self.pt_i % NPT
            self.pt_i += 1
            if kt < 8:
                if na:
                    issue_bias(nbias[0] + 3)
                    bi = bias_slot[(hd, hf, kt)]
                    nbias[0] += 1
                    self.mm(bs_, bs_.t[:, :], self.cb('ident'), self.BT[bi][:, :], [self.BTres[bi], self.CBres])
                else:
                    self.mm(bs_, bs_.t[:, :], self.GM[0:4, ks], self.GM[0:4, 1024 + hf * 512:1536 + hf * 512],
                            [self.GMres])
                self.act(self.PT[p][:, :], bs_.t[:, :], AF.Exp, [bs_.res], [self.PTres[p]], bias=0.0, scale=scale)
            else:
                self.act(self.PT[p][:, :], bs_.t[:, :], AF.Exp, [bs_.res, self.SPres], [self.PTres[p]],
                         bias=self.SPt[:, self.cm.sl('ctxb')], scale=scale)
            st_info[n] = p

        def back(n):
            hd, hf, kt, first, last = steps[n]
            if na:
                ht, pr = hd // 2, slice(64 * (hd % 2), 64 * (hd % 2) + 64)
                vc = slice(ht * 128, ht * 128 + 128)
            else:
                ht, pr = hd, slice(0, 128)
                vc = slice((hd // 4) * 128, (hd // 4) * 128 + 128)
            qs = slice(hf * 512, hf * 512 + 512)
            if first:
                acc[0] += 1
                self._bo = self.banks[4 + 2 * (acc[0] % 2)]
                self._bsum = self.banks[5 + 2 * (acc[0] % 2)]
                self._bo.fresh = True
                self._bsum.fresh = True
            bo, bsum = self._bo, self._bsum
            p = st_info.pop(n)
            self.mm(bo, bo.t[:, :], V[:, kt, vc], self.PT[p][:, :], [self.Vres[kt], self.PTres[p]])
            sa = acc[0] % 2
            SA, SAres = self.VF[sa][:, 0:512], self.VFres[sa]
            if first:
                self.cp('dve', SA, self.PT[p][:, :], [self.PTres[p]], [SAres])
            else:
                self.tt('dve', SA, SA, self.PT[p][:, :], ALU.add, [self.PTres[p], SAres], [SAres])
            if last:
                SB = self.VF[sa][:, 512:1024].bitcast(BF16)[:, 0:512]
                self.cp('dve', SB, SA, [SAres], [SAres])
                self.mm(bsum, bsum.t[:, :], self.cb('ones'), SB, [SAres, self.CBres])
                self.s.op('dve', lambda e, o=self.RC[pr, :], a=bsum.t[pr, :]: e.reciprocal(o, a),
                          reads=[bsum.res], writes=[self.RCres])
                self.tt('dve', self.HT[pr, ht, qs], bo.t[pr, :], self.RC[pr, :], ALU.mult,
                        [bo.res, self.RCres], [self.HTres[ht]])

        hook_every = max(1, len(steps) // 9)
        for n in range(len(steps) + LA):
            if n < len(steps):
                if n % hook_every == hook_every - 1:
                    self.mod_hook(1)
                front(n)
            if n - LA >= 0:
                back(n - LA)
        self.rot = list(range(8))

    def na_layer(self, i):
        for j in range(8):
            self.dma('pool', self.KT_na[:, j, 1024:1536], self.na_kcT[j * 128:(j + 1) * 128, :], self.KCres,
                     writes=[self.KCres])
        for t in range(4):
            self.dma('pool', self.V_na[:, 8 + t, :], self.na_vc[t * 128:(t + 1) * 128, :], self.Vres[8 + t],
                     writes=[self.Vres[8 + t]])
        gq_col = self.SPt[:, self.cm.sl('naq')]
        gk_col = self.SPt[:, self.cm.sl('nak')]

        def s3_q(ctx, b, r):
            j, h = ctx
            self.stt('dve', self.QT[:, j, h * 512:(h + 1) * 512], b.t[:, :], gq_col, self.RS[r], ALU.mult, ALU.mult,
                     [b.res, self.RSres[r], self.SPres], [self.QTres[j]])

        def s3_k(ctx, b, r):
            j, h, kf = ctx
            self.stt('dve', self.TMP[kf][:, h * 512:(h + 1) * 512], b.t[:, :], gk_col, self.RS[r], ALU.mult, ALU.mult,
                     [b.res, self.RSres[r], self.SPres], [self.TMPres[kf]])
            if h == 1:
                self.cp('pool', self.KT_na[:, j, 0:1024], self.TMP[kf][:, :], [self.TMPres[kf]], [self.KTres[j]])
                self.emit_k_out(self.TMP[kf], self.TMPres[kf], self.o_nak, j)

        items = [(('naqkv', i, ch), m4, h, (ch * 4 + m4, h)) for ch in range(2) for m4 in range(4) for h in range(2)]
        self.qk_pipeline(items, 'blk64', s3_q)
        items = []
        for ch in range(2, 4):
            for m4 in range(4):
                kf = self.kf_i % 2
                self.kf_i += 1
                for h in range(2):
                    items.append((('naqkv', i, ch), m4, h, ((ch - 2) * 4 + m4, h, kf)))
        self.qk_pipeline(items, 'blk64', s3_k)
        for ch in range(4, 6):
            w, wres = self.next_w(('naqkv', i, ch))
            self.v_proj(w, wres, 0, 512, self.V_na, (ch - 4) * 512, self.o_nav)
        self.attention('na')
        self.wo_proj(i, self.HT)

    def rope_norm(self, b, r, gaincol, h, out, outres, kf=None, kfres=None):
        hs = slice(h * 512, (h + 1) * 512)
        QN, QNres = self.RL[0], self.RLres[0]
        T1, T1res = self.RL[1], self.RLres[1]
        T2, T2res = self.RC, self.RCres
        QNb, QNbres = self.PT[0], self.PTres[0]
        self.stt('dve', QN[:, :], b.t[:, :], self.SPt[:, self.cm.sl(gaincol)], self.RS[r], ALU.mult, ALU.mult,
                 [b.res, self.RSres[r], self.SPres], [QNres])
        self.cp('act', QNb[:, :], QN[:, :], [QNres], [QNbres])
        bsw = self.newbank()
        self.mm(bsw, bsw.t[:, :], self.cb('pswap'), QNb[:, :], [QNbres, self.CBres])
        self.tt('pool', T1[:, :], QN[:, :], self.CC[:, hs], ALU.mult, [QNres, self.ROPEres], [T1res])
        self.tt('dve', T2[:, :], bsw.t[:, :], self.SS[:, hs], ALU.mult, [bsw.res, self.ROPEres], [T2res])
        if kf is None:
            self.tt('dve', out, T1[:, :], T2[:, :], ALU.add, [T1res, T2res], [outres])
        else:
            self.tt('dve', kf, T1[:, :], T2[:, :], ALU.add, [T1res, T2res], [kfres])
            self.cp('act', out, kf, [kfres], [outres])

    def gqa_layer(self, i):
        self.dma('sp', self.CC[:, :], self.rope[0], self.ROPEres, writes=[self.ROPEres])
        self.dma('sp', self.SS[:, :], self.rope[1], self.ROPEres, writes=[self.ROPEres])
        self.dma('pool', self.GM[0:4, :], self.gmask, self.GMres, writes=[self.GMres])
        for kv in range(2):
            self.dma('pool', self.KT_g[:, kv, 1024:1536], self.gqa_kcT[kv * 128:(kv + 1) * 128, :], self.KCres,
                     writes=[self.KCres])
        for t in range(4):
            self.dma('pool', self.V_g[:, 8 + t, :], self.gqa_vc[t * 128:(t + 1) * 128, :], self.Vres[8 + t],
                     writes=[self.Vres[8 + t]])
        def s3_q(ctx, b, r):
            j, h = ctx
            self.rope_norm(b, r, 'gq', h, self.QT[:, j, h * 512:(h + 1) * 512], self.QTres[j])

        def s3_k(ctx, b, r):
            kv, h, kf = ctx
            self.rope_norm(b, r, 'gk', h, self.KT_g[:, kv, h * 512:(h + 1) * 512], self.KTres[kv],
                           kf=self.TMP[kf][:, h * 512:(h + 1) * 512], kfres=self.TMPres[kf])
            if h == 1:
                self.emit_k_out(self.TMP[kf], self.TMPres[kf], self.o_gk, kv)

        items = [(('gqkv', i, ch), m4, h, (ch * 4 + m4, h)) for ch in range(2) for m4 in range(4) for h in range(2)]
        cur = None
        for it in items:
            cur = self.qk_pipeline([it], 'ones128', s3_q, cur)
        items = []
        for kv in range(2):
            kf = self.kf_i % 2
            self.kf_i += 1
            for h in range(2):
                items.append((('gqkv', i, 2), kv, h, (kv, h, kf)))
        for it in items:
            cur = self.qk_pipeline([it], 'ones128', s3_k, cur)
        _, w, wres = cur
        self.v_proj(w, wres, 256, 256, self.V_g, 0, self.o_gv)
        self.attention('gqa')
        self.wo_proj(i, self.HT)

    def declare_s5_dram(self):
        pass

    def alloc_s5_sbuf(self):
        pass

    def plan_s5_weights(self, i):
        pass

    def s5_layer(self, i):
        pass


_TABLE_CACHE = {}


def na_bias_table(rpb, sample):
    if sample:
        q = np.arange(T)
        k = np.arange(T)
        qr, qc = (q // 64)[:, None], (q % 64)[:, None]
        kr, kc = (k // 64)[None, :], (k % 64)[None, :]
        rs = np.clip(qr - 4, 0, 8)
        cs = np.clip(qc - 8, 0, 48)
        ok = (kr >= rs) & (kr < rs + 8) & (kc >= cs) & (kc < cs + 16)
        drow = np.clip(kr - qr + 7, 0, 14)
        dc = np.clip(kc - qc + 15, 0, 30)
        full = np.where(ok[None], np.asarray(rpb, np.float32)[:, drow, dc], np.float32(NEG))
    else:
        q = np.arange(T)
        same = (q[:, None] // 256) == (q[None, :] // 256)
        full = np.broadcast_to(np.where(same, np.float32(0.0), np.float32(NEG))[None], (16, T, T))
    out = np.empty((16, len(NA_TILES), 128, 512), np.float32)
    for n, (kt, hf) in enumerate(NA_TILES):
        out[:, n] = np.transpose(full[:, hf * 512:(hf + 1) * 512, kt * 128:(kt + 1) * 128], (0, 2, 1))
    return out


def rope_tables(sample):
    if not sample:
        return np.stack([np.ones((128, T), np.float32), np.zeros((128, T), np.float32)])
    t = np.arange(T)
    row = (t // 64).astype(np.float32)
    col = (t % 64).astype(np.float32)
    half = 64
    inv = (np.float32(10000.0) ** (-np.arange(0, half, 2, dtype=np.float32) / np.float32(half))).astype(np.float32)
    ang = np.concatenate([row[:, None] * inv, col[:, None] * inv], axis=-1).astype(np.float32)
    c, s_ = np.cos(ang).astype(np.float32).T, np.sin(ang).astype(np.float32).T
    return np.stack([np.concatenate([c, c], 0), np.concatenate([-s_, s_], 0)])


def gmask_table(sample):
    g = np.zeros((4, 2048), np.float32)
    k = np.arange(T)
    for j in range(4):
        g[j, :T] = (k // 256 == j)
        if not sample:
            g[j, T:] = np.where(k // 256 == j, 0.0, NEG)
    return g


def mixer_inputs(inp, core):
    sample = core < 4
    m = {
        'na_w_qkv': np.asarray(inp['na_w_qkv'][0], np.float32),
        'na_w_o': np.asarray(inp['na_w_o'][0], np.float32),
        'gqa_w_qkv': np.asarray(inp['gqa_w_qkv'][0], np.float32),
        'gqa_w_o': np.asarray(inp['gqa_w_o'][0], np.float32),
        's5_glu_w': np.asarray(inp['s5_glu_w'], np.float32),
    }
    if sample:
        m['na_kcT'] = np.ascontiguousarray(np.asarray(inp['cache_na_k'][core, 0], np.float32).reshape(512, D).T)
        m['na_vc'] = np.ascontiguousarray(np.asarray(inp['cache_na_v'][core, 0], np.float32).reshape(512, D))
        m['gqa_kcT'] = np.ascontiguousarray(np.asarray(inp['cache_gqa_k'][core, 0], np.float32).reshape(512, 256).T)
        m['gqa_vc'] = np.ascontiguousarray(np.asarray(inp['cache_gqa_v'][core, 0], np.float32).reshape(512, 256))
    else:
        m['na_kcT'] = np.zeros((D, 512), np.float32)
        m['na_vc'] = np.zeros((512, D), np.float32)
        m['gqa_kcT'] = np.zeros((256, 512), np.float32)
        m['gqa_vc'] = np.zeros((512, 256), np.float32)
    key = ('nab', sample)
    if key not in _TABLE_CACHE:
        _TABLE_CACHE[key] = na_bias_table(inp['na_rpb'][0], sample)
        _TABLE_CACHE[('rope', sample)] = rope_tables(sample)
        _TABLE_CACHE[('gm', sample)] = gmask_table(sample)
    m['na_bias'] = _TABLE_CACHE[key]
    m['rope'] = _TABLE_CACHE[('rope', sample)]
    m['gmask'] = _TABLE_CACHE[('gm', sample)]
    return m


def make_builder(layers, mixers=True):
    dry = S5Builder(layers=layers, mixers=mixers, plan=None)
    dry.build()
    b = S5Builder(layers=layers, mixers=mixers, plan=dry.rec_tags)
    nc = b.build()
    return b, nc


def run(inp, layers=(0, 1, 2, 3), cores=None):
    b, nc = make_builder(layers)
    maps = []
    if cores is not None:
        for core in cores:
            m = common_inputs(inp, core, b.cm)
            m.update(mixer_inputs(inp, core))
            m.update(s5_inputs(inp, core))
            maps.append({k: v for k, v in m.items() if k in b.dram})
        res = run_bass_kernel_spmd(nc, maps, core_ids=list(range(len(cores))))
        return res.results
    for core in range(8):
        m = common_inputs(inp, core, b.cm)
        m.update(mixer_inputs(inp, core))
        m.update(s5_inputs(inp, core))
        maps.append({k: v for k, v in m.items() if k in b.dram})
    res = run_bass_kernel_spmd(nc, maps, core_ids=list(range(8)))
    R = res.results
    y_sample = np.stack([R[c]['yT'].T for c in range(4)]).astype(np.float32)
    y_prompt = np.concatenate([R[c]['yT'].T.reshape(4, 256, D) for c in range(4, 8)]).astype(np.float32)
    nak = np.concatenate([R[c]['o_nak'].reshape(4, 256, 16, 64) for c in range(4, 8)])[:, None]
    nav = np.concatenate([R[c]['o_nav'].reshape(4, 256, 16, 64) for c in range(4, 8)])[:, None]
    gk = np.concatenate([R[c]['o_gk'].reshape(4, 256, 2, 128) for c in range(4, 8)])[:, None]
    gv = np.concatenate([R[c]['o_gv'].reshape(4, 256, 2, 128) for c in range(4, 8)])[:, None]
    s5 = s5_assemble(R)
    return (y_prompt, y_sample, s5, nak.astype(np.float32), nav.astype(np.float32), gk.astype(np.float32),
            gv.astype(np.float32))


def kernel(**inputs):
    return run(inputs)


S5_NSM = 48


def s5_host_arrays(inp, core):
    sample = core < 4
    lam = np.zeros((2, 3, 128, 64), np.float32)
    Bm = np.zeros((2, 128, 2, 64, 16), np.float32)
    Cm = np.zeros((2, 128, 2, 64, 16), np.float32)
    h0 = np.zeros((2, 128, 2, 64), np.float32)
    for sl in range(2):
        for dr in range(2):
            rows = slice(dr * 64, dr * 64 + 64)
            lam[sl, 0, rows] = np.asarray(inp['s5_lam_re'][sl, dr], np.float32).T
            lam[sl, 1, rows] = np.asarray(inp['s5_lam_im'][sl, dr], np.float32).T
            lam[sl, 2, rows] = np.asarray(inp['s5_log_dt'][sl, dr], np.float32)[None, :]
            Bm[sl, rows, 0] = np.transpose(np.asarray(inp['s5_b_re'][sl, dr], np.float32), (1, 0, 2))
            Bm[sl, rows, 1] = np.transpose(np.asarray(inp['s5_b_im'][sl, dr], np.float32), (1, 0, 2))
            Cm[sl, rows, 0] = np.transpose(np.asarray(inp['s5_c_re'][sl, dr], np.float32), (2, 0, 1))
            Cm[sl, rows, 1] = np.transpose(np.asarray(inp['s5_c_im'][sl, dr], np.float32), (2, 0, 1))
            if sample:
                for ri in range(2):
                    h0[sl, rows, ri] = np.asarray(inp['state_s5'][core, sl, dr, ri], np.float32).T
    carry = np.ones((128, 128), np.float32)
    carry[:, 0] = 0.0
    carry[:, [32, 64, 96]] = 1.0 if sample else 0.0
    sg = np.arange(128) // 16
    tri = (sg[None, :] >= sg[:, None]).astype(np.float32)
    return {'s5_lam': lam, 's5_B': Bm, 's5_C': Cm, 's5_h0': h0, 's5_carry': carry, 's5_tri': tri,
            's5_antif': np.ascontiguousarray(np.eye(128, dtype=np.float32)[::-1])}


def s5_inputs(inp, core):
    return s5_host_arrays(inp, core)


def s5_core_states(res):
    return np.transpose(res['o_s5'], (1, 0, 2, 3, 4, 5))


def s5_assemble(R):
    return np.concatenate([s5_core_states(R[c]) for c in range(4, 8)]).astype(np.float32)


class S5Builder(FullBuilder):
    def declare_s5_dram(self):
        self.d_lam = self.din('s5_lam', [2, 3, 128, 64])
        self.d_B = self.din('s5_B', [2, 128, 2, 64, 16])
        self.d_C = self.din('s5_C', [2, 128, 2, 64, 16])
        self.d_h0 = self.din('s5_h0', [2, 128, 2, 64])
        self.d_carry = self.din('s5_carry', [128, 128])
        self.d_tri = self.din('s5_tri', [128, 128])
        self.d_antif = self.din('s5_antif', [128, 128])
        self.o_s5 = self.dout('o_s5', [2, 4, 2, 2, 64, 64])

    def alloc_s5_sbuf(self):
        self.SML = self.sb('SML', [128, S5_NSM, 64], F32)
        self.sm_idx = {}
        self.S5S = Res('S5S')
        self.BLKres = Res('BLK')
        self.BLKQres = Res('BLKQ')
        self.LAM3 = self.sb('LAM3', [128, 3, 64], F32)
        self.H0 = self.sb('H0', [128, 2, 64], F32)
        self.CARRY = self.sb('CARRY', [128, 128], F32)
        self.TRI = self.sb('TRI', [128, 128], F32)
        self.ANTIF = self.sb('ANTIF', [128, 128], F32)
        self.S5Cres = Res('S5C')
        self.PB = self.sb('PB', [128, 2, 8, 16], F32)
        self.PC = self.sb('PC', [128, 2, 8, 16], F32)
        self.PBres = Res('PB')
        self.BLK = self.sb('BLK', [128, 11, 72], F32)
        self.FINALL = self.RC[:, :].rearrange("p (s r g) -> p s r g", s=4, r=2)
        self.FINres = self.RCres
        S = self.SCR
        o = [0]

        def carve(n, dt=BF16):
            a = S[:, o[0]:o[0] + n]
            o[0] += n
            return a if dt == BF16 else a.bitcast(F32)
        self.HTOK = carve(1024).rearrange("p (a b) -> p a b", a=8)
        self.HTOKR = carve(1024).rearrange("p (a b) -> p a b", a=8)
        self.U = carve(1024).rearrange("p (a b) -> p a b", a=8)
        self.UR = carve(1024).rearrange("p (a b) -> p a b", a=8)
        self.EIN = carve(2048).rearrange("p (g r n) -> p g r n", g=8, r=2)
        self.AOUT = carve(2304).rearrange("p (g r n) -> p g r n", g=8, r=2)
        self.GEN7 = carve(2048).rearrange("p (g r n) -> p g r n", g=8, r=2)
        self.MGF = carve(2048).rearrange("p (g r n) -> p g r n", g=8, r=2)
        self.MGB = carve(2048).rearrange("p (g r n) -> p g r n", g=8, r=2)
        self.MINT = carve(2048).rearrange("p (g r n) -> p g r n", g=8, r=2)
        self.HPB = carve(2048).rearrange("p (r g c) -> p r g c", r=2, g=8)
        self.GR = carve(2048, F32).rearrange("p (g c) -> p g c", g=8)
        self.GI = carve(2048, F32).rearrange("p (g c) -> p g c", g=8)
        self.COS = carve(2048, F32).rearrange("p (g c) -> p g c", g=8)
        self.SIN = carve(2048, F32).rearrange("p (g c) -> p g c", g=8)
        self.TBG = S[:, o[0] - 8192:o[0] - 4096].bitcast(F32)
        self.TB2 = S[:, o[0]:o[0] + 4096].bitcast(F32)
        self.T1 = carve(2048, F32)
        self.T2 = carve(2048, F32)
        self.T3 = S[:, 8448:10496].bitcast(F32)
        self.AMt = S[:, 4096:6144].bitcast(F32)
        assert o[0] <= 32768, o[0]
        self.r_htok, self.r_htokr, self.r_u, self.r_ur = Res('htok'), Res('htokr'), Res('u'), Res('ur')
        self.r_ein, self.r_aout, self.r_gen7 = Res('ein'), Res('aout'), Res('gen7')
        self.r_mgf, self.r_mgb, self.r_mint, self.r_hpb = Res('mgf'), Res('mgb'), Res('mint'), Res('hpb')
        self.r_g, self.r_cs, self.r_t1, self.r_t2 = Res('g'), Res('cs'), Res('t1'), Res('t2')

    def sm(self, name):
        if name not in self.sm_idx:
            self.sm_idx[name] = len(self.sm_idx)
            assert len(self.sm_idx) <= S5_NSM - 2, name
        return self.SML[:, self.sm_idx[name], :]

    def _srw(self, extra):
        w = getattr(self, '_s_wres', None) or self.S5S
        return [self.S5S, w] + list(extra), [w]

    def s_tt(self, out, a, b, op, extra=()):
        r, w = self._srw(extra)
        self.s.op(getattr(self, '_s_eng', 'dve'), lambda e: e.tensor_tensor(out, a, b, op), reads=r, writes=w)

    def s_ts(self, out, a, s1, s2, op0, op1=None, extra=()):
        r, w = self._srw(extra)
        if op1 is None:
            self.s.op('dve', lambda e: e.tensor_scalar(out, a, s1, None, op0), reads=r, writes=w)
        else:
            self.s.op('dve', lambda e: e.tensor_scalar(out, a, s1, s2, op0, op1), reads=r, writes=w)

    def s_stt(self, out, a, sc, b, op0, op1, extra=()):
        r, w = self._srw(extra)
        self.s.op('dve', lambda e: e.scalar_tensor_tensor(out, a, sc, b, op0, op1), reads=r, writes=w)

    def s_cmul(self, outr, outi, ar, ai, br, bi, t1, t2, extra=()):
        self.s_tt(t1, ar, br, ALU.mult, extra)
        self.s_tt(t2, ai, bi, ALU.mult, extra)
        self.s_tt(outr, t1, t2, ALU.subtract)
        self.s_tt(t1, ar, bi, ALU.mult, extra)
        self.s_tt(t2, ai, br, ALU.mult, extra)
        self.s_tt(outi, t1, t2, ALU.add)

    def s_expm1(self, dr, di, zr, zi, nsq, deg, pre):
        t1, t2, t3, t4 = (self.sm(pre + n) for n in ('t1', 't2', 't3', 't4'))
        sr, si = self.sm(pre + 'sr'), self.sm(pre + 'si')
        xr, xi = self.sm(pre + 'xr'), self.sm(pre + 'xi')
        sc = 1.0 / (1 << nsq)
        self.s_ts(xr, zr, sc, None, ALU.mult)
        if zi is not None:
            self.s_ts(xi, zi, sc, None, ALU.mult)
        self.s_ts(sr, xr, 1.0 / deg, 1.0, ALU.mult, ALU.add)
        if zi is not None:
            self.s_ts(si, xi, 1.0 / deg, None, ALU.mult)
        for k in range(deg - 1, 1, -1):
            if zi is not None:
                self.s_tt(t1, xr, sr, ALU.mult)
                self.s_tt(t2, xi, si, ALU.mult)
                self.s_tt(t3, xr, si, ALU.mult)
                self.s_tt(t4, xi, sr, ALU.mult)
                self.s_tt(t1, t1, t2, ALU.subtract)
                self.s_tt(t3, t3, t4, ALU.add)
                self.s_ts(sr, t1, 1.0 / k, 1.0, ALU.mult, ALU.add)
                self.s_ts(si, t3, 1.0 / k, None, ALU.mult)
            else:
                self.s_tt(t1, xr, sr, ALU.mult)
                self.s_ts(sr, t1, 1.0 / k, 1.0, ALU.mult, ALU.add)
        if zi is not None:
            self.s_cmul(dr, di, xr, xi, sr, si, t1, t2)
        else:
            self.s_tt(dr, xr, sr, ALU.mult)
        for _ in range(nsq):
            if zi is not None:
                self.s_tt(t1, dr, dr, ALU.mult)
                self.s_tt(t2, di, di, ALU.mult)
                self.s_tt(t3, dr, di, ALU.mult)
                self.s_tt(t1, t1, t2, ALU.subtract)
                self.s_stt(dr, dr, 2.0, t1, ALU.mult, ALU.add)
                self.s_tt(t3, t3, di, ALU.add)
                self.s_ts(di, t3, 2.0, None, ALU.mult)
            else:
                self.s_tt(t1, dr, dr, ALU.mult)
                self.s_stt(dr, dr, 2.0, t1, ALU.mult, ALU.add)

    def s5_layer_params(self, sl):
        sm = self.sm
        self.dma('sp', self.LAM3[:, :, :], self.d_lam[sl].rearrange("a p g -> p a g"), self.S5S, writes=[self.S5S])
        self.dma('sp', self.H0[:, :, :], self.d_h0[sl], self.S5S, writes=[self.S5S])
        lamr, lami, ldt = self.LAM3[:, 0, :], self.LAM3[:, 1, :], self.LAM3[:, 2, :]
        self.s_expm1(sm('dt'), None, ldt, None, 6, 7, 'e_')
        self.s_ts(sm('dt'), sm('dt'), 1.0, None, ALU.add)
        self.s_tt(sm('ar'), lamr, sm('dt'), ALU.mult)
        self.s_tt(sm('ai'), lami, sm('dt'), ALU.mult)
        self.s_expm1(sm('nr'), sm('ni'), sm('ar'), sm('ai'), 7, 8, 'e_')
        self.s_ts(sm('abr'), sm('nr'), 1.0, None, ALU.add)
        t1, t2 = sm('e_t1'), sm('e_t2')
        self.s_tt(t1, lamr, lamr, ALU.mult)
        self.s_tt(t2, lami, lami, ALU.mult)
        self.s_tt(t1, t1, t2, ALU.add)
        self.s.op('dve', lambda e: e.reciprocal(sm('rden'), t1), reads=[self.S5S], writes=[self.S5S])
        self.s_tt(t1, sm('nr'), lamr, ALU.mult)
        self.s_tt(t2, sm('ni'), lami, ALU.mult)
        self.s_tt(t1, t1, t2, ALU.add)
        self.s_tt(sm('fre'), t1, sm('rden'), ALU.mult)
        self.s_tt(t1, sm('ni'), lamr, ALU.mult)
        self.s_tt(t2, sm('nr'), lami, ALU.mult)
        self.s_tt(t1, t1, t2, ALU.subtract)
        self.s_tt(sm('fim'), t1, sm('rden'), ALU.mult)
        self.s_ts(t1, sm('ar'), 2.0, None, ALU.mult)
        self.s_expm1(sm('m2'), None, t1, None, 5, 6, 'e_')
        self.s_ts(sm('m2'), sm('m2'), 1.0, None, ALU.add)
        self.s.op('dve', lambda e: e.reciprocal(sm('im2'), sm('m2')), reads=[self.S5S], writes=[self.S5S])
        self.s_tt(sm('q1r'), sm('abr'), sm('im2'), ALU.mult)
        self.s_stt(sm('q1i'), sm('ni'), -1.0, sm('im2'), ALU.mult, ALU.mult)
        cr, ci = sm('abr'), sm('ni')
        for k, nm in enumerate(('p2', 'p4', 'mu')):
            self.s_cmul(sm(nm + 'r'), sm(nm + 'i'), cr, ci, cr, ci, t1, t2)
            cr, ci = sm(nm + 'r'), sm(nm + 'i')
        self.s_tt(sm('rho8'), sm('m2'), sm('m2'), ALU.mult)
        self.s_tt(sm('rho8'), sm('rho8'), sm('rho8'), ALU.mult)
        self.s.op('dve', lambda e: e.reciprocal(t1, sm('rho8')), reads=[self.S5S], writes=[self.S5S])
        self.s_tt(sm('E0r'), sm('mur'), t1, ALU.mult)
        self.s_tt(sm('E0i'), sm('mui'), t1, ALU.mult)
        for k in range(1, 7):
            self.s_cmul(sm(f'E{k}r'), sm(f'E{k}i'), sm(f'E{k-1}r'), sm(f'E{k-1}i'), sm(f'E{k-1}r'), sm(f'E{k-1}i'), t1, t2)
        self.s_cmul(sm('g0r'), sm('g0i'), sm('mur'), sm('mui'), self.H0[:, 0, :], self.H0[:, 1, :], t1, t2)

    def blk(self, k):
        return self.BLK[:, k, :].rearrange("p (g n) -> p g n", g=8)

    def s5_block_tables(self, sl, j):
        sm = self.sm
        gs = slice(8 * j, 8 * j + 8)
        B = self.blk
        PWR, PWI, QR, QI, WBR, WBI, W7R, W7I, TA, TB = (B(k) for k in range(10))
        self._s_wres = self.BLKres
        self.dma('sp', self.PB[:, :, :, :], self.d_B[sl][:, :, gs, :], self.PBres, writes=[self.PBres])
        self.dma('sp', self.PC[:, :, :, :], self.d_C[sl][:, :, gs, :], self.PBres, writes=[self.PBres])

        def col(ap, n):
            return ap[:, :, n:n + 1]

        def sv(name):
            return sm(name)[:, gs].unsqueeze(2)

        def bc(ap, shape):
            return ap.to_broadcast(shape)

        TA2 = self.SML[:, S5_NSM - 2, :].rearrange("p (g n) -> p g n", g=8)
        TB2 = self.SML[:, S5_NSM - 1, :].rearrange("p (g n) -> p g n", g=8)

        def cmul_tab(outr, outi, ar, ai, br, bi, shape, tmps=None, extra=()):
            ta_, tb_ = tmps or (TA, TB)
            ta = ta_[:, :, 0:shape[2]]
            tb = tb_[:, :, 0:shape[2]]
            self.s_cmul(outr, outi, ar, ai, bc(br, shape), bc(bi, shape), ta, tb, extra)

        for (TR, TI, b1r, b1i, b2r, b2i, b4r, b4i) in (
                (PWR, PWI, 'abr', 'ni', 'p2r', 'p2i', 'p4r', 'p4i'),):
            self.s.op('dve', lambda e, o=col(TR, 0): e.memset(o, 1.0), reads=[self.BLKres], writes=[self.BLKres])
            self.s.op('dve', lambda e, o=col(TI, 0): e.memset(o, 0.0), reads=[self.BLKres], writes=[self.BLKres])
            self.s_tt(col(TR, 1), sv(b1r), sv(b1r), ALU.max)
            self.s_tt(col(TI, 1), sv(b1i), sv(b1i), ALU.max)
            self.s_tt(col(TR, 2), sv(b2r), sv(b2r), ALU.max)
            self.s_tt(col(TI, 2), sv(b2i), sv(b2i), ALU.max)
            cmul_tab(TR[:, :, 3:5], TI[:, :, 3:5], TR[:, :, 1:3], TI[:, :, 1:3], sv(b2r), sv(b2i), [128, 8, 2])
            cmul_tab(TR[:, :, 5:9], TI[:, :, 5:9], TR[:, :, 1:5], TI[:, :, 1:5], sv(b4r), sv(b4i), [128, 8, 4])
        self._s_eng = 'pool'
        self._s_wres = self.BLKQres
        q2 = (TA2, TB2)
        self.s.op('pool', lambda e, o=col(QR, 0): e.memset(o, 1.0), reads=[self.BLKQres], writes=[self.BLKQres])
        self.s.op('pool', lambda e, o=col(QI, 0): e.memset(o, 0.0), reads=[self.BLKQres], writes=[self.BLKQres])
        self.s.op('pool', lambda e, o=col(QR, 1), a=sv('q1r'): e.tensor_copy(o, a), reads=[self.S5S, self.BLKQres],
                  writes=[self.BLKQres])
        self.s.op('pool', lambda e, o=col(QI, 1), a=sv('q1i'): e.tensor_copy(o, a), reads=[self.S5S, self.BLKQres],
                  writes=[self.BLKQres])
        self.s_cmul(col(QR, 2), col(QI, 2), col(QR, 1), col(QI, 1), col(QR, 1), col(QI, 1), col(TA2, 0), col(TB2, 0))
        cmul_tab(QR[:, :, 3:5], QI[:, :, 3:5], QR[:, :, 1:3], QI[:, :, 1:3], col(QR, 2), col(QI, 2), [128, 8, 2], q2)
        cmul_tab(QR[:, :, 5:9], QI[:, :, 5:9], QR[:, :, 1:5], QI[:, :, 1:5], col(QR, 4), col(QI, 4), [128, 8, 4], q2)
        cmul_tab(WBR[:, :, 0:8], WBI[:, :, 0:8], QR[:, :, 0:8], QI[:, :, 0:8], sv('fre'), sv('fim'), [128, 8, 8], q2)
        self._s_eng = 'dve'
        self._s_wres = self.BLKres
        cmul_tab(W7R[:, :, 0:8], W7I[:, :, 0:8], WBR[:, :, 0:8], WBI[:, :, 0:8], col(PWR, 7), col(PWI, 7), [128, 8, 8],
                 None, [self.BLKQres])
        self._s_wres = None

    def s5_block_expand_in(self, sl, j):
        B = self.blk
        PWR, PWI, QR, QI, WBR, WBI, W7R, W7I, TA, TB = (B(k) for k in range(10))
        sh = [128, 8, 8, 16]
        Br = self.PB[:, 0, :, :].unsqueeze(2).to_broadcast(sh)
        Bi = self.PB[:, 1, :, :].unsqueeze(2).to_broadcast(sh)
        for (WR, WI, DST, dres) in ((WBR, WBI, self.EIN, self.r_ein), (W7R, W7I, self.GEN7, self.r_gen7)):
            wr = WR[:, :, 0:8].unsqueeze(3).to_broadcast(sh)
            wi = WI[:, :, 0:8].unsqueeze(3).to_broadcast(sh)
            a1 = self.TBG[:, 0:1024].rearrange("p (g s k) -> p g s k", g=8, s=8)
            a2 = self.TBG[:, 1024:2048].rearrange("p (g s k) -> p g s k", g=8, s=8)
            ex = [self.PBres, self.r_g, self.BLKres, self.BLKQres]
            w_ = [self.r_g]
            self.s.op('dve', lambda e, o=a1, x=wr, y=Br: e.tensor_tensor(o, x, y, ALU.mult), reads=[self.S5S] + ex, writes=w_)
            self.s.op('dve', lambda e, o=a2, x=wi, y=Bi: e.tensor_tensor(o, x, y, ALU.mult), reads=[self.S5S] + ex, writes=w_)
            dre = DST[:, :, 0, :].rearrange("p g (s k) -> p g s k", s=8)
            self.s.op('dve', lambda e, o=dre, x=a1, y=a2: e.tensor_tensor(o, x, y, ALU.subtract), reads=w_, writes=w_ + [dres])
            self.s.op('dve', lambda e, o=a1, x=wr, y=Bi: e.tensor_tensor(o, x, y, ALU.mult), reads=[self.S5S] + ex, writes=w_)
            self.s.op('dve', lambda e, o=a2, x=wi, y=Br: e.tensor_tensor(o, x, y, ALU.mult), reads=[self.S5S] + ex, writes=w_)
            dim_ = DST[:, :, 1, :].rearrange("p g (s k) -> p g s k", s=8)
            self.s.op('dve', lambda e, o=dim_, x=a1, y=a2: e.tensor_tensor(o, x, y, ALU.add), reads=w_, writes=w_ + [dres])

    def s5_block_expand(self, sl, j):
        B = self.blk
        PWR, PWI, QR, QI, WBR, WBI, W7R, W7I, TA, TB = (B(k) for k in range(10))
        t1 = self.TBG[:, 0:1152]
        t2 = self.TB2[:, 0:1152]
        sh9 = [128, 8, 9, 16]
        Cr = self.PC[:, 0, :, :].unsqueeze(2).to_broadcast(sh9)
        Ci = self.PC[:, 1, :, :].unsqueeze(2).to_broadcast(sh9)
        pr = PWR[:, :, 0:9].unsqueeze(3).to_broadcast(sh9)
        pi = PWI[:, :, 0:9].unsqueeze(3).to_broadcast(sh9)
        a1 = t1.rearrange("p (g s k) -> p g s k", g=8, s=9)
        a2 = t2.rearrange("p (g s k) -> p g s k", g=8, s=9)
        ex = [self.PBres, self.r_t1, self.r_t2, self.r_g, self.BLKres, self.BLKQres]
        w_ = [self.r_t1, self.r_t2, self.r_g]
        self.s.op('dve', lambda e: e.tensor_tensor(a1, Cr, pr, ALU.mult), reads=[self.S5S] + ex, writes=w_)
        self.s.op('dve', lambda e: e.tensor_tensor(a2, Ci, pi, ALU.mult), reads=[self.S5S] + ex, writes=w_)
        dre = self.AOUT[:, :, 0, :].rearrange("p g (s k) -> p g s k", s=9)
        self.s.op('dve', lambda e: e.tensor_tensor(dre, a1, a2, ALU.subtract), reads=w_, writes=w_ + [self.r_aout])
        self.s.op('dve', lambda e: e.tensor_tensor(a1, Cr, pi, ALU.mult), reads=[self.S5S] + ex, writes=w_)
        self.s.op('dve', lambda e: e.tensor_tensor(a2, Ci, pr, ALU.mult), reads=[self.S5S] + ex, writes=w_)
        dim_ = self.AOUT[:, :, 1, :].rearrange("p g (s k) -> p g s k", s=9)
        self.s.op('dve', lambda e: e.scalar_tensor_tensor(dim_, a1, -1.0, a2, ALU.mult, ALU.subtract), reads=w_,
                  writes=w_ + [self.r_aout])

    def s5_block(self, i, sl, j):
        sm = self.sm
        gs = slice(8 * j, 8 * j + 8)
        ident, anti = self.cb('ident'), self.cb('anti')
        hsrc = self.HT[:, j, :].rearrange("p (c s) -> p s c", s=8)
        for (dst, dres, rev) in ((self.HTOK, self.r_htok, False), (self.HTOKR, self.r_htokr, True)):
            for half in range(2):
                b = self.newbank()
                for q in range(4):
                    s_ = half * 4 + q
                    src_s = 7 - s_ if rev else s_
                    self.mm(b, b.t[:, q * 128:(q + 1) * 128], hsrc[:, src_s, :], ident, [self.HTres[j], self.CBres])
                dv = dst.rearrange("p g (s k) -> p s g k", s=8)[:, half * 4:half * 4 + 4, :, :]
                self.cp('act', dv, b.t[:, :].rearrange("p (s g k) -> p s g k", s=4, g=8), [b.res], [dres])
        for (src, sres, dst, dres, mat) in ((self.HTOK, self.r_htok, self.U, self.r_u, ident),
                                            (self.HTOKR, self.r_htokr, self.UR, self.r_ur, anti)):
            for half in range(2):
                b = self.newbank()
                for q in range(4):
                    g = half * 4 + q
                    self.mm(b, b.t[:, q * 128:(q + 1) * 128], src[:, g, :], mat, [sres, self.CBres])
                self.cp('act', dst[:, half * 4:half * 4 + 4, :], b.t[:, :].rearrange("p (a b) -> p a b", a=4),
                        [b.res], [dres])
        if j == 0:
            self.s5_block_tables(sl, 0)
            self.s5_block_expand_in(sl, 0)
        self.s5_block_expand(sl, j)
        for ri in range(2):
            for half in range(2):
                b = self.newbank()
                for q in range(4):
                    g = half * 4 + q
                    self.mm(b, b.t[:, q * 128:(q + 1) * 128], self.GEN7[:, g, ri, :], ident, [self.r_gen7, self.CBres])
                v = b.t[:, :].rearrange("p (a b) -> p a b", a=4)
                self.cp('act', self.MGF[:, half * 4:half * 4 + 4, ri, 0:64], v[:, :, 0:64], [b.res], [self.r_mgf])
                self.cp('act', self.MGB[:, half * 4:half * 4 + 4, ri, 64:128], v[:, :, 64:128], [b.res], [self.r_mgb])
        for dr in range(2):
            rows = slice(dr * 64, dr * 64 + 64)
            for half in range(2):
                b = self.newbank()
                for q in range(4):
                    g = half * 4 + q
                    for ri in range(2):
                        self.mm(b, b.t[:, q * 128:(q + 1) * 128], self.EIN[rows, g, ri, :], self.AOUT[rows, g, ri, 0:128],
                                [self.r_ein, self.r_aout])
                self.s.op('dve', lambda e, o=self.MINT[:, half * 4:half * 4 + 4, dr, :],
                          a=b.t[:, :].rearrange("p (a b) -> p a b", a=4),
                          m=self.TRI[:, :].unsqueeze(1).to_broadcast([128, 4, 128]): e.tensor_tensor(o, a, m, ALU.mult),
                          reads=[b.res, self.S5Cres], writes=[self.r_mint])
        gb = {}
        for ri in range(2):
            for half in range(2):
                b = self.newbank()
                gb[(ri, half)] = b
                for q in range(4):
                    g = half * 4 + q
                    self.mm(b, b.t[:, q * 128:(q + 1) * 128], self.MGF[:, g, ri, :], self.U[:, g, :], [self.r_mgf, self.r_u])
                    self.mm(b, b.t[:, q * 128:(q + 1) * 128], self.MGB[:, g, ri, :], self.UR[:, g, :], [self.r_mgb, self.r_ur])
        for ri, GG in ((0, self.GR), (1, self.GI)):
            for half in range(2):
                b = gb[(ri, half)]
                self.cp('act', GG[:, half * 4:half * 4 + 4, :], b.t[:, :].rearrange("p (a b) -> p a b", a=4),
                        [b.res], [self.r_g])
            g0 = sm('g0r' if ri == 0 else 'g0i')[:, gs].unsqueeze(2)
            self.s.op('dve', lambda e, o=GG[:, :, 0:1], a=GG[:, :, 0:1], b_=g0: e.tensor_tensor(o, a, b_, ALU.add),
                      reads=[self.S5S, self.r_g], writes=[self.r_g])
        self.s5_scan(sl, j)
        if j + 1 < NF:
            self.s5_block_tables(sl, j + 1)
            self.s5_block_expand_in(sl, j + 1)
        YF = self.T1.rearrange("p (t f) -> p t f", t=8)
        YB = self.T2.rearrange("p (t f) -> p t f", t=8)
        for dr, (Uc, ures, Y, yres) in enumerate(((self.U, self.r_u, YF, self.r_t1), (self.UR, self.r_ur, YB, self.r_t2))):
            rows = slice(dr * 64, dr * 64 + 64)
            for half in range(2):
                b = self.newbank()
                for q in range(4):
                    g = half * 4 + q
                    cs = slice(q * 128, (q + 1) * 128)
                    self.mm(b, b.t[:, cs], Uc[:, g, :], self.MINT[:, g, dr, :], [ures, self.r_mint])
                    for ri in range(2):
                        self.mm(b, b.t[:, cs], self.HPB[rows, ri, g, :], self.AOUT[rows, g, ri, 16:144],
                                [self.r_hpb, self.r_aout])
                src = b.t[:, :].rearrange("p (g t k) -> p g t k", g=4, t=8)
                dst = Y[:, :, half * 64:half * 64 + 64].rearrange("p t (g k) -> p g t k", g=4)
                self.cp('act', dst, src, [b.res], [yres])
        yfull = self.TMP[j % 2]
        yres = self.TMPres[j % 2]
        for half in range(2):
            b = self.newbank()
            for q in range(4):
                t_ = half * 4 + q
                cs = slice(q * 128, (q + 1) * 128)
                self.mm(b, b.t[:, cs], YF[:, t_, :], self.IDF[:, :], [self.r_t1, self.IDFres])
                self.mm(b, b.t[:, cs], YB[:, 7 - t_, :], self.ANTIF[:, :], [self.r_t2, self.S5Cres])
            dst = yfull[:, :].rearrange("p (c t) -> p t c", t=8)[:, half * 4:half * 4 + 4, :]
            usrc = self.HT[:, j, :].rearrange("p (c t) -> p t c", t=8)[:, half * 4:half * 4 + 4, :]
            self.s.op('dve', lambda e, o=dst, u=usrc, d=self.SPt[:, self.cm.m[f's5d_{sl}'][0] + j:self.cm.m[f's5d_{sl}'][0] + j + 1],
                      y=b.t[:, :].rearrange("p (t c) -> p t c", t=4): e.scalar_tensor_tensor(o, u, d, y, ALU.mult, ALU.add),
                      reads=[b.res, self.HTres[j], self.SPres], writes=[yres])
        self.act(self.HT[:, j, :], yfull[:, :], AF.Gelu_apprx_tanh, [yres], [self.HTres[j]])

    def s5_scan(self, sl, j):
        sm = self.sm
        gs = slice(8 * j, 8 * j + 8)
        GR, GI, COS, SIN = self.GR, self.GI, self.COS, self.SIN
        sh = [128, 8, 128]
        PTMP = self.HPB.rearrange("p r g c -> p (r g c)").bitcast(F32)
        self.s.op('pool', lambda e: e.memset(COS[:, :, 0:1], 1.0), reads=[self.r_cs], writes=[self.r_cs])
        self.s.op('pool', lambda e: e.memset(SIN[:, :, 0:1], 0.0), reads=[self.r_cs], writes=[self.r_cs])
        for k in range(7):
            d = 1 << k
            er = sm(f'E{k}r')[:, gs].unsqueeze(2).to_broadcast([128, 8, d])
            ei = sm(f'E{k}i')[:, gs].unsqueeze(2).to_broadcast([128, 8, d])
            t1 = PTMP[:, 0:8 * d].rearrange("p (g c) -> p g c", g=8)
            t2 = PTMP[:, 512:512 + 8 * d].rearrange("p (g c) -> p g c", g=8)
            rd = [self.S5S, self.r_cs, self.r_hpb]
            wr = [self.r_hpb, self.r_cs]
            o = self.s.op
            o('pool', lambda e, a=t1, x=COS[:, :, 0:d], y=er: e.tensor_tensor(a, x, y, ALU.mult), reads=rd, writes=wr)
            o('pool', lambda e, a=t2, x=SIN[:, :, 0:d], y=ei: e.tensor_tensor(a, x, y, ALU.mult), reads=rd, writes=wr)
            o('pool', lambda e, a=COS[:, :, d:2 * d], x=t1, y=t2: e.tensor_tensor(a, x, y, ALU.subtract), reads=rd, writes=wr)
            o('pool', lambda e, a=t1, x=COS[:, :, 0:d], y=ei: e.tensor_tensor(a, x, y, ALU.mult), reads=rd, writes=wr)
            o('pool', lambda e, a=t2, x=SIN[:, :, 0:d], y=er: e.tensor_tensor(a, x, y, ALU.mult), reads=rd, writes=wr)
            o('pool', lambda e, a=SIN[:, :, d:2 * d], x=t1, y=t2: e.tensor_tensor(a, x, y, ALU.add), reads=rd, writes=wr)
        T1 = self.T1[:, 0:1024].rearrange("p (g c) -> p g c", g=8)
        T2 = self.T2[:, 0:1024].rearrange("p (g c) -> p g c", g=8)
        T3 = self.T3.rearrange("p (g c) -> p g c", g=8)
        AM = self.AMt.rearrange("p (g c) -> p g c", g=8)
        rd = [self.S5S, self.r_cs, self.r_t1, self.r_t2, self.r_g, self.S5Cres, self.r_ein, self.r_gen7]
        wr = [self.r_t1, self.r_t2, self.r_g, self.r_ein, self.r_gen7]
        o = self.s.op
        o('pool', lambda e: e.tensor_tensor(AM, sm('rho8')[:, gs].unsqueeze(2).to_broadcast(sh),
                                            self.CARRY[:, :].unsqueeze(1).to_broadcast(sh), ALU.mult),
          reads=[self.S5S, self.S5Cres, self.r_ein], writes=[self.r_ein])
        o('dve', lambda e: e.tensor_tensor(T1, GR, COS, ALU.mult), reads=rd, writes=wr)
        o('dve', lambda e: e.tensor_tensor(T2, GI, SIN, ALU.mult), reads=rd, writes=wr)
        o('dve', lambda e: e.tensor_tensor(T1, T1, T2, ALU.add), reads=rd, writes=wr)
        o('dve', lambda e: e.tensor_tensor(T3, GI, COS, ALU.mult), reads=rd, writes=wr)
        o('dve', lambda e: e.tensor_tensor(T2, GR, SIN, ALU.mult), reads=rd, writes=wr)
        o('dve', lambda e: e.tensor_tensor(T3, T3, T2, ALU.subtract), reads=rd, writes=wr)
        fl = lambda a: a.rearrange("p g c -> p (g c)")
        o('dve', lambda e: e.tensor_tensor_scan(fl(GR), fl(AM), fl(T1), 0.0, ALU.mult, ALU.add), reads=rd, writes=wr)
        o('dve', lambda e: e.tensor_tensor_scan(fl(GI), fl(AM), fl(T3), 0.0, ALU.mult, ALU.add), reads=rd, writes=wr)
        o('dve', lambda e: e.tensor_tensor(T1, GR, COS, ALU.mult), reads=rd, writes=wr)
        o('dve', lambda e: e.tensor_tensor(T2, GI, SIN, ALU.mult), reads=rd, writes=wr)
        o('dve', lambda e: e.tensor_tensor(T1, T1, T2, ALU.subtract), reads=rd, writes=wr)
        o('dve', lambda e: e.tensor_tensor(T3, GI, COS, ALU.mult), reads=rd, writes=wr)
        o('dve', lambda e: e.tensor_tensor(T2, GR, SIN, ALU.mult), reads=rd, writes=wr)
        o('dve', lambda e: e.tensor_tensor(T3, T3, T2, ALU.add), reads=rd, writes=wr)
        cm_ = self.CARRY[:, 1:128].unsqueeze(1).to_broadcast([128, 8, 127])
        prd = [self.S5S, self.S5Cres, self.r_t1, self.r_gen7]
        for ri, Hf in ((0, T1), (1, T3)):
            o('pool', lambda e, a=self.HPB[:, ri, :, 1:128], x=Hf[:, :, 0:127], y=cm_: e.tensor_tensor(a, x, y, ALU.mult),
              reads=prd, writes=[self.r_hpb])
            o('pool', lambda e, a=self.HPB[:, ri, :, 0:1], x=self.H0[:, ri, gs].unsqueeze(2): e.tensor_copy(a, x),
              reads=[self.S5S], writes=[self.r_hpb])
            fv = self.FINALL[:, :, ri, gs].rearrange("p s g -> p g s")
            hv = Hf.rearrange("p g (s c) -> p g s c", s=4)[:, :, :, 31]
            o('pool', lambda e, a=fv, x=hv: e.tensor_copy(a, x), reads=prd, writes=[self.FINres])

    def s5_prefetch_params(self, sl):
        self.s5_layer_params(sl)
        self._s5_params_ready = sl

    def s5_layer(self, i):
        sl = i // 3
        if not getattr(self, '_s5_const_loaded', False):
            self._s5_const_loaded = True
            self.dma('sp', self.CARRY[:, :], self.d_carry, self.S5Cres, writes=[self.S5Cres])
            self.dma('sp', self.TRI[:, :], self.d_tri, self.S5Cres, writes=[self.S5Cres])
            self.dma('sp', self.ANTIF[:, :], self.d_antif, self.S5Cres, writes=[self.S5Cres])
        self.s.op('pool', lambda e: e.memset(self.MGF[:, :, :, 64:128], 0.0), writes=[self.r_mgf])
        self.s.op('pool', lambda e: e.memset(self.MGB[:, :, :, 0:64], 0.0), writes=[self.r_mgb])
        if getattr(self, '_s5_params_ready', None) != sl:
            self.s5_prefetch_params(sl)
        self._s5_params_ready = None
        for j in range(NF):
            self.s5_block(i, sl, j)
            self.mod_hook(1)
        for seg in range(4):
            for ri in range(2):
                b = self.newbank()
                self.mm(b, b.t[0:64, 0:128], self.FINALL[:, seg, ri, :], self.IDF[:, :], [self.FINres, self.IDFres])
                k = self.vf_i % 2
                self.vf_i += 1
                self.cp('act', self.VF[k][0:64, 0:128], b.t[0:64, 0:128], [b.res], [self.VFres[k]])
                for dr in range(2):
                    oseg = seg if dr == 0 else 3 - seg
                    self.dma('sp', self.o_s5[sl, oseg, dr, ri], self.VF[k][0:64, dr * 64:dr * 64 + 64], self.VFres[k],
                             reads=[self.VFres[k]], is_output=True)
        gb0 = self.cm.m[f'glub_{sl}'][0]
        for ch in range(4):
            w, wres = self.next_w(('glu', i, ch))
            for m2 in range(2):
                mt = ch * 2 + m2
                for h in range(2):
                    hs = slice(h * 512, (h + 1) * 512)
                    ba = self.newbank()
                    bg = self.newbank()
                    for kt in range(8):
                        self.mm(ba, ba.t[:, :], w[:, kt, m2 * 128:(m2 + 1) * 128], self.HT[:, kt, hs], [wres, self.HTres[kt]])
                    for kt in range(8):
                        self.mm(bg, bg.t[:, :], w[:, kt, 256 + m2 * 128:256 + (m2 + 1) * 128], self.HT[:, kt, hs],
                                [wres, self.HTres[kt]])
                    k = self.rl_i % 2
                    self.rl_i += 1
                    self.act(self.RL[k][:, :], bg.t[:, :], AF.Sigmoid, [bg.res, self.SPres], [self.RLres[k]],
                             bias=self.SPt[:, gb0 + 8 + mt:gb0 + 9 + mt], scale=1.0)
                    self.stt('dve', self.RL[k][:, :], ba.t[:, :], self.SPt[:, gb0 + mt:gb0 + mt + 1], self.RL[k][:, :],
                             ALU.add, ALU.mult, [ba.res, self.SPres, self.RLres[k]], [self.RLres[k]])
                    self.stt('dve', self.X[:, mt, hs], self.RL[k][:, :], self.MODT[:, 16 + mt:17 + mt], self.X[:, mt, hs],
                             ALU.mult, ALU.add, [self.RLres[k], self.MODres, self.Xres[mt]], [self.Xres[mt]])
```

```python
import math
import numpy as np
import concourse.bass as bass
import concourse.mybir as mybir
from concourse.bass_utils import run_bass_kernel_spmd

F32 = mybir.dt.float32
BF16 = mybir.dt.bfloat16
AF = mybir.ActivationFunctionType
ALU = mybir.AluOpType

D = 1024
T = 1024
NF = 8
DFF = 4096
DEPTH = 4
EPS = 1e-6
NEG = -30000.0
EPOCH = 30000
ENGS = ('pe', 'act', 'dve', 'pool', 'sp')


class Res:
    __slots__ = ('name', 'w', 'r')

    def __init__(self, name=''):
        self.name = name
        self.w = None
        self.r = []


class DrySched:
    def __init__(self):
        self.count = {e: 0 for e in ENGS}
        self.nsem = 0

    def op(self, *a, **k):
        return None

    def dma(self, *a, **k):
        return None

    def barrier(self):
        pass

    def finish(self):
        pass

    def emit(self):
        pass


class Sched:
    def __init__(self, nc):
        self.nc = nc
        self.ops = {e: [] for e in ENGS}
        self.count = {e: 0 for e in ENGS}
        self.seen = {e: {} for e in ENGS}
        self.esem = {e: [] for e in ENGS}
        self.dsem = {}
        self.out_tokens = []
        self.nsem = 0
        self.pending = {e: [] for e in ENGS}

    def _esem(self, e, epoch):
        while len(self.esem[e]) <= epoch:
            self.esem[e].append(self.nc.alloc_semaphore(f"se_{e}_{len(self.esem[e])}"))
            self.nsem += 1
        return self.esem[e][epoch]

    def _collect(self, eng, reads, writes, extra=()):
        toks = list(extra)
        for r in reads:
            if r.w is not None:
                toks.append(r.w)
        for w in writes:
            if w.w is not None:
                toks.append(w.w)
            toks.extend(w.r)
        need = {}
        for t in toks:
            if t[0] == 'e':
                _, e2, idx = t
                if e2 == eng and eng == 'pe':
                    continue
                key = ('e', e2)
                val = idx
            else:
                key = ('d', t[1])
                val = t[2]
            if self.seen[eng].get(key, 0) >= val:
                continue
            if need.get(key, 0) < val:
                need[key] = val
        waits = []
        for key, val in need.items():
            self.seen[eng][key] = val
            if key[0] == 'e':
                ep = (val - 1) // EPOCH
                waits.append((self._esem(key[1], ep), (val - 1) % EPOCH + 1))
            else:
                waits.append((self.dsem[key[1]][0], val))
        return waits

    def _mark(self, tok, reads, writes):
        for r in reads:
            r.r.append(tok)
        for w in writes:
            w.w = tok
            w.r = []

    def op(self, eng, fn, reads=(), writes=(), extra=()):
        extra = list(extra) + self.pending[eng]
        self.pending[eng] = []
        waits = self._collect(eng, reads, writes, extra)
        self.count[eng] += 1
        idx = self.count[eng]
        sem = self._esem(eng, (idx - 1) // EPOCH)
        self.ops[eng].append((waits, fn, sem, 1))
        tok = ('e', eng, idx)
        self._mark(tok, reads, writes)
        return tok

    def dma(self, eng, fn, key, reads=(), writes=(), is_output=False):
        extra = self.pending[eng]
        self.pending[eng] = []
        waits = self._collect(eng, reads, writes, extra)
        kid = id(key)
        if kid not in self.dsem:
            self.dsem[kid] = [self.nc.alloc_semaphore(f"sd_{len(self.dsem)}"), 0]
            self.nsem += 1
        ent = self.dsem[kid]
        ent[1] += 16
        self.ops[eng].append((waits, fn, ent[0], 16))
        tok = ('d', kid, ent[1])
        self._mark(tok, reads, writes)
        if is_output:
            self.out_tokens.append(tok)
        return tok

    def barrier(self):
        toks = [('e', e, self.count[e]) for e in ENGS if self.count[e] > 0]
        for e in ENGS:
            self.pending[e] = self.pending[e] + toks

    def finish(self):
        toks = list(self.out_tokens)
        waits = self._collect('sp', (), (), toks)
        self.ops['sp'].append((waits, None, None, 0))

    def emit(self):
        nc = self.nc
        ops = self.ops

        def replay(eng_obj, lst):
            for waits, fn, sem, inc in lst:
                for s, v in waits:
                    eng_obj.wait_ge(s, v)
                if fn is None:
                    continue
                ins = fn(eng_obj)
                if sem is not None:
                    ins.then_inc(sem, inc)

        with nc.Block() as block:
            @block.tensor
            def _(e):
                replay(e, ops['pe'])

            @block.scalar
            def _(e):
                replay(e, ops['act'])

            @block.vector
            def _(e):
                replay(e, ops['dve'])

            @block.gpsimd
            def _(e):
                replay(e, ops['pool'])

            @block.sync
            def _(e):
                replay(e, ops['sp'])


class ColMap:
    def __init__(self):
        self.m = {}
        self.n = 0

    def add(self, name, w):
        self.m[name] = (self.n, w)
        self.n += w

    def sl(self, name):
        a, w = self.m[name]
        return slice(a, a + w)


def fm(v):
    v = np.asarray(v, np.float32)
    return np.ascontiguousarray(v.reshape(-1, 128).T)


def small_colmap():
    cm = ColMap()
    cm.add('cvec', 8)
    for i in range(DEPTH):
        cm.add(f'g1_{i}', 8)
        cm.add(f'g2_{i}', 8)
        cm.add(f'adab_{i}', 48)
    for s in range(2):
        cm.add(f's5d_{s}', 8)
        cm.add(f'glub_{s}', 16)
    for nm in ('naq', 'nak', 'gq', 'gk', 'ctxb', 'carry'):
        cm.add(nm, 1)
    return cm


CONST_BF = ColMap()
for _nm in ('ones1024', 'ones128', 'blk64', 'ones', 'ident', 'pswap', 'anti'):
    CONST_BF.add(_nm, 128)


def const_bf_array():
    a = np.zeros((128, CONST_BF.n), np.float32)
    a[:, CONST_BF.sl('ones1024')] = 1.0 / 1024
    a[:, CONST_BF.sl('ones128')] = 1.0 / 128
    b = np.zeros((128, 128), np.float32)
    b[:64, :64] = 1.0 / 64
    b[64:, 64:] = 1.0 / 64
    a[:, CONST_BF.sl('blk64')] = b
    a[:, CONST_BF.sl('ones')] = 1.0
    a[:, CONST_BF.sl('ident')] = np.eye(128, dtype=np.float32)
    a[:, CONST_BF.sl('pswap')] = np.roll(np.eye(128, dtype=np.float32), 64, axis=0)
    a[:, CONST_BF.sl('anti')] = np.eye(128, dtype=np.float32)[::-1]
    return a


class Bank:
    def __init__(self, t, i):
        self.t = t
        self.res = Res(f'bank{i}')
        self.fresh = True


class Builder:
    def __init__(self, layers=(0, 1, 2, 3), mixers=True, plan=None):
        self.layers = layers
        self.mixers = mixers
        self.plan = plan
        self.nc = bass.Bass("TRN2", target_bir_lowering=False)
        self.s = Sched(self.nc) if plan is not None else DrySched()
        self.rec_tags = []
        self.cm = small_colmap()
        self.dram = {}
        self.wq = []
        self.wq_issued = 0

    def din(self, name, shape, dt=F32):
        t = self.nc.dram_tensor(name, list(shape), dt, kind="ExternalInput").ap()
        self.dram[name] = t
        return t

    def dout(self, name, shape, dt=F32):
        t = self.nc.dram_tensor(name, list(shape), dt, kind="ExternalOutput").ap()
        self.dram[name] = t
        return t

    def sb(self, name, shape, dt):
        return self.nc.alloc_sbuf_tensor(name, list(shape), dt)

    def mm(self, bank, out, lhsT, rhs, reads):
        st = bank.fresh
        bank.fresh = False
        return self.s.op('pe', lambda e: e.matmul(out, lhsT, rhs, start=st, stop=True, skip_group_check=True),
                         reads=reads, writes=[bank.res])

    def newbank(self):
        rot = getattr(self, 'rot', None) or list(range(8))
        b = self.banks[rot[self.bank_i % len(rot)]]
        self.bank_i += 1
        b.fresh = True
        return b

    def act(self, out, in_, func, reads, writes, bias=0.0, scale=1.0):
        return self.s.op('act', lambda e: e.activation(out=out, in_=in_, func=func, bias=bias, scale=scale),
                         reads=reads, writes=writes)

    def tt(self, eng, out, a, b, op, reads, writes):
        return self.s.op(eng, lambda e: e.tensor_tensor(out, a, b, op), reads=reads, writes=writes)

    def ts(self, eng, out, a, s1, s2, op0, op1, reads, writes):
        return self.s.op(eng, lambda e: e.tensor_scalar(out, a, s1, s2, op0, op1), reads=reads, writes=writes)

    def stt(self, eng, out, a, sc, b, op0, op1, reads, writes):
        return self.s.op(eng, lambda e: e.scalar_tensor_tensor(out, a, sc, b, op0, op1), reads=reads, writes=writes)

    def rsqrt(self, out, in_, reads, writes):
        self.s.op('act', lambda e: e.activation(out=out, in_=in_, func=AF.Ln, bias=self.EPSC[:, 0:1], scale=1.0),
                  reads=list(reads) + [self.EPSres], writes=writes)
        self.s.op('act', lambda e: e.activation(out=out, in_=out, func=AF.Exp, bias=0.0, scale=-0.5),
                  reads=writes, writes=writes)

    def cp(self, eng, out, in_, reads, writes):
        if eng == 'act':
            return self.s.op('act', lambda e: e.copy(out, in_), reads=reads, writes=writes)
        return self.s.op(eng, lambda e: e.tensor_copy(out, in_), reads=reads, writes=writes)

    def dma(self, eng, out, in_, key, reads=(), writes=(), is_output=False):
        return self.s.dma(eng, lambda e: e.dma_start(out=out, in_=in_), key, reads=reads, writes=writes,
                          is_output=is_output)

    def wq_add(self, src_ap, shape):
        self.wq.append((src_ap, shape))
        return len(self.wq) - 1

    def wq_get(self, idx):
        while self.wq_issued < min(len(self.wq), idx + self.NSLOT):
            j = self.wq_issued
            slot = j % self.NSLOT
            src, shape = self.wq[j]
            n = shape[1] * shape[2]
            dst = self.wslot[slot][:, 0:n].rearrange("p (a b) -> p a b", a=shape[1])
            if isinstance(src, list):
                wpart = shape[2] // len(src)
                for sap, part in src:
                    self.dma('pool', dst[:, :, part * wpart:(part + 1) * wpart], sap, self.wres[slot],
                             writes=[self.wres[slot]])
            else:
                self.dma('pool', dst, src, self.wres[slot], writes=[self.wres[slot]])
            self.wq_issued += 1
        slot = idx % self.NSLOT
        src, shape = self.wq[idx]
        n = shape[1] * shape[2]
        return self.wslot[slot][:, 0:n].rearrange("p (a b) -> p a b", a=shape[1]), self.wres[slot]

    def build(self):
        nc = self.nc
        cm = self.cm
        s = self.s
        xT = self.din('xT', [D, T])
        smallp = self.din('smallp', [128, cm.n])
        cbf = self.din('cbf', [128, CONST_BF.n])
        identf = self.din('identf', [128, 128])
        ada_w = self.din('ada_w', [DEPTH, D, 6 * D])
        mlp_w1 = self.din('mlp_w1', [DEPTH, D, DFF])
        mlp_w2 = self.din('mlp_w2', [DEPTH, DFF, D])
        yT = self.dout('yT', [D, T])
        self.declare_mixer_dram()

        self.X = self.sb('X', [128, NF, T], F32)
        self.Xres = [Res(f'X{j}') for j in range(NF)]
        self.HT = self.sb('HT', [128, NF, T], BF16)
        self.HTres = [Res(f'HT{j}') for j in range(NF)]
        self.SPt = self.sb('SPt', [128, cm.n], F32)
        self.SPres = Res('SP')
        self.CB = self.sb('CB', [128, CONST_BF.n], BF16)
        self.CBres = Res('CB')
        self.IDF = self.sb('IDF', [128, 128], F32)
        self.NSLOT = 3
        self.wslot = [self.sb(f'wslot{k}', [128, 6144], BF16) for k in range(self.NSLOT)]
        self.wres = [Res(f'w{k}') for k in range(self.NSLOT)]
        self.TMP = [self.sb(f'TMP{k}', [128, T], F32) for k in range(2)]
        self.TMPres = [Res(f'TMP{k}') for k in range(2)]
        self.SQ = [self.sb(f'SQ{k}', [128, T], BF16) for k in range(2)]
        self.SQres = [Res(f'SQ{k}') for k in range(2)]
        self.RSTD = self.sb('RSTD', [128, T], F32)
        self.RSTDres = Res('RSTD')
        self.MODTS = [self.sb(f'MODT{k}', [128, 48], F32) for k in range(2)]
        self.MODress = [Res(f'MOD{k}') for k in range(2)]
        self.GSC = self.sb('GSC', [128, 8], F32)
        self.GSCres = Res('GSC')
        self.SIL = self.sb('SIL', [128, 8], BF16)
        self.SILres = Res('SIL')
        self.SCR = self.sb('SCR', [128, 32 * T], BF16)
        self.HID = self.SCR[:, :].rearrange("p (a b) -> p a b", a=32)
        self.HIDres = [Res(f'HID{k}') for k in range(32)]
        self.EPSC = self.sb('EPSC', [128, 1], F32)
        self.EPSres = Res('EPS')
        self.s.op('dve', lambda e: e.memset(self.EPSC[:, :], EPS), writes=[self.EPSres])
        self.RL = [self.sb(f'RL{k}', [128, 512], F32) for k in range(2)]
        self.RLres = [Res(f'RL{k}') for k in range(2)]
        self.rl_i = 0
        self.alloc_mixer_sbuf()
        self.banks = [Bank(nc.alloc_psum_tensor(f'ps{k}', [128, 512], F32), k) for k in range(8)]
        self.bank_i = 0

        self.ada_w, self.mlp_w1, self.mlp_w2 = ada_w, mlp_w1, mlp_w2
        self.wtags = list(self.plan) if self.plan is not None else None
        if self.wtags is not None:
            for tag in self.wtags:
                src, shape = self.wsrc(tag)
                self.wq_add(src, shape)
        self.wnext = 0

        self.dma('sp', self.SPt[:, :], smallp, self.SPres, writes=[self.SPres])
        self.dma('pool', self.CB[:, :], cbf, self.CBres, writes=[self.CBres])
        self.IDFres = Res('IDF')
        self.dma('sp', self.IDF[:, :], identf, self.IDFres, writes=[self.IDFres])
        for j in range(NF):
            self.dma('sp', self.X[:, j, :], xT[j * 128:(j + 1) * 128, :], self.Xres[j], writes=[self.Xres[j]])
        self.act(self.SIL[:, :], self.SPt[:, cm.sl('cvec')], AF.Silu, [self.SPres], [self.SILres])
        self.load_mixer_inputs()

        self.MODT = self.MODTS[0]
        self.MODres = self.MODress[0]
        self.mod_pending = []
        for ch in range(8):
            self.modulation_chunk(self.layers[0], ch, 0)
        if self.mixers and self.layers[0] % 3 == 0:
            self.s5_prefetch_params(self.layers[0] // 3)
        for li, i in enumerate(self.layers):
            self.MODT = self.MODTS[li % 2]
            self.MODres = self.MODress[li % 2]
            if li + 1 < len(self.layers):
                self.mod_pending = [(self.layers[li + 1], ch, (li + 1) % 2) for ch in range(8)]
            self.norm_mod(f'g1_{i}', 8, 0)
            if self.mixers:
                self.mixer(i)
            self.norm_mod(f'g2_{i}', 32 + 0, 24)
            self.mod_hook(8)
            if self.mixers and li + 1 < len(self.layers) and self.layers[li + 1] % 3 == 0:
                self.s5_prefetch_params(self.layers[li + 1] // 3)
            self.mlp(i)

        for j in range(NF):
            self.dma('sp', yT[j * 128:(j + 1) * 128, :], self.X[:, j, :], self.Xres[j], reads=[self.Xres[j]],
                     is_output=True)
        self.finalize_mixer_outputs()
        s.finish()
        s.emit()
        return nc

    def next_w(self, tag):
        if self.wtags is None:
            self.rec_tags.append(tag)
            _, shape = self.wsrc(tag)
            n = shape[1] * shape[2]
            return self.wslot[0][:, 0:n].rearrange("p (a b) -> p a b", a=shape[1]), self.wres[0]
        assert self.wtags[self.wnext] == tag, (self.wtags[self.wnext], tag)
        ap, res = self.wq_get(self.wnext)
        self.wnext += 1
        return ap, res

    def wsrc(self, tag):
        kind, i, ch = tag
        pk = lambda w: w.rearrange("(kt p) n -> p kt n", p=128)
        if kind == 'ada':
            return pk(self.ada_w[i])[:, :, ch * 768:(ch + 1) * 768], [128, 8, 768]
        if kind == 'w1':
            return pk(self.mlp_w1[i])[:, :, ch * 512:(ch + 1) * 512], [128, 8, 512]
        if kind == 'w2':
            return pk(self.mlp_w2[i])[:, :, ch * 128:(ch + 1) * 128], [128, 32, 128]
        return self.wsrc_mixer(tag)

    def modulation_chunk(self, i, ch, buf):
        b = self.newbank()
        w, wres = self.next_w(('ada', i, ch))
        for m6 in range(6):
            for kt in range(8):
                self.mm(b, b.t[:, m6:m6 + 1], w[:, kt, m6 * 128:(m6 + 1) * 128], self.SIL[:, kt:kt + 1],
                        [wres, self.SILres])
        a0 = self.cm.m[f'adab_{i}'][0] + ch * 6
        self.tt('dve', self.MODTS[buf][:, ch * 6:ch * 6 + 6], b.t[:, 0:6], self.SPt[:, a0:a0 + 6], ALU.add,
                [b.res, self.SPres], [self.MODress[buf]])

    def mod_hook(self, n=1):
        for _ in range(n):
            if self.mod_pending:
                self.modulation_chunk(*self.mod_pending.pop(0))

    def cb(self, name):
        return self.CB[:, CONST_BF.sl(name)]

    def norm_mod(self, gname, sc0, sh0):
        cm = self.cm
        self.stt('dve', self.GSC[:, :], self.MODT[:, sc0:sc0 + 8], 1.0, self.SPt[:, cm.sl(gname)], ALU.add, ALU.mult,
                 [self.MODres, self.SPres], [self.GSCres])
        bs = [self.newbank(), self.newbank()]
        for j in range(NF):
            k = j % 2
            self.act(self.SQ[k][:, :], self.X[:, j, :], AF.Square, [self.Xres[j]], [self.SQres[k]])
            for h in range(2):
                self.mm(bs[h], bs[h].t[:, :], self.cb('ones1024'), self.SQ[k][:, h * 512:(h + 1) * 512],
                        [self.SQres[k], self.CBres])
        for h in range(2):
            self.rsqrt(self.RSTD[:, h * 512:(h + 1) * 512], bs[h].t[:, :], [bs[h].res], [self.RSTDres])
        for j in range(NF):
            k = j % 2
            self.tt('dve', self.TMP[k][:, :], self.X[:, j, :], self.RSTD[:, :], ALU.mult,
                    [self.Xres[j], self.RSTDres], [self.TMPres[k]])
            self.act(self.HT[:, j, :], self.TMP[k][:, :], AF.Identity, [self.TMPres[k], self.GSCres, self.MODres],
                     [self.HTres[j]], bias=self.MODT[:, sh0 + j:sh0 + j + 1], scale=self.GSC[:, j:j + 1])

    def mlp(self, i):
        for ch in range(8):
            w, wres = self.next_w(('w1', i, ch))
            for m4 in range(4):
                mt = ch * 4 + m4
                for h in range(2):
                    b = self.newbank()
                    for kt in range(8):
                        self.mm(b, b.t[:, :], w[:, kt, m4 * 128:(m4 + 1) * 128], self.HT[:, kt, h * 512:(h + 1) * 512],
                                [wres, self.HTres[kt]])
                    k = self.rl_i % 2
                    self.rl_i += 1
                    self.act(self.RL[k][:, :], b.t[:, :], AF.Relu, [b.res], [self.RLres[k]])
                    self.tt('dve' if k == 0 else 'pool', self.HID[:, mt, h * 512:(h + 1) * 512], self.RL[k][:, :],
                            self.RL[k][:, :], ALU.mult, [self.RLres[k]], [self.HIDres[mt]])
        for ch in range(8):
            w, wres = self.next_w(('w2', i, ch))
            for m2 in range(1):
                mt = ch
                for h in range(2):
                    b = self.newbank()
                    for kt in range(32):
                        self.mm(b, b.t[:, :], w[:, kt, m2 * 128:(m2 + 1) * 128], self.HID[:, kt, h * 512:(h + 1) * 512],
                                [wres, self.HIDres[kt]])
                    self.stt('dve', self.X[:, mt, h * 512:(h + 1) * 512], b.t[:, :], self.MODT[:, 40 + mt:41 + mt],
                             self.X[:, mt, h * 512:(h + 1) * 512], ALU.mult, ALU.add,
                             [b.res, self.MODres, self.Xres[mt]], [self.Xres[mt]])

    def declare_mixer_dram(self):
        pass

    def alloc_mixer_sbuf(self):
        pass

    def plan_mixer_weights(self, i):
        pass

    def load_mixer_inputs(self):
        pass

    def mixer(self, i):
        pass

    def finalize_mixer_outputs(self):
        pass


def core_tokens(inp, core):
    if core < 4:
        return np.asarray(inp['x_sample'][core], np.float32), np.asarray(inp['c'][core], np.float32)
    b0 = 4 * (core - 4)
    return (np.asarray(inp['x_prompt'][b0:b0 + 4], np.float32).reshape(T, D),
            np.asarray(inp['c_ctx'], np.float32))


def small_array(inp, core, cm):
    a = np.zeros((128, cm.n), np.float32)
    _, cvec = core_tokens(inp, core)
    a[:, cm.sl('cvec')] = fm(cvec)
    for i in range(DEPTH):
        a[:, cm.sl(f'g1_{i}')] = fm(inp['norm_g'][i, 0])
        a[:, cm.sl(f'g2_{i}')] = fm(inp['norm_g'][i, 1])
        a[:, cm.sl(f'adab_{i}')] = fm(inp['ada_b'][i])
    for sl in range(2):
        a[:, cm.sl(f's5d_{sl}')] = fm(inp['s5_d'][sl])
        a[:, cm.sl(f'glub_{sl}')] = fm(inp['s5_glu_b'][sl])
    a[:, cm.sl('naq')] = np.tile(np.asarray(inp['na_q_norm'][0], np.float32), 2)[:, None]
    a[:, cm.sl('nak')] = np.tile(np.asarray(inp['na_k_norm'][0], np.float32), 2)[:, None]
    a[:, cm.sl('gq')] = np.asarray(inp['gqa_q_norm'][0], np.float32)[:, None]
    a[:, cm.sl('gk')] = np.asarray(inp['gqa_k_norm'][0], np.float32)[:, None]
    a[:, cm.sl('ctxb')] = 0.0 if core < 4 else NEG
    a[:, cm.sl('carry')] = 1.0 if core < 4 else 0.0
    return a


def common_inputs(inp, core, cm):
    x, _ = core_tokens(inp, core)
    m = {
        'xT': np.ascontiguousarray(x.T),
        'smallp': small_array(inp, core, cm),
        'cbf': const_bf_array(),
        'identf': np.eye(128, dtype=np.float32),
        'ada_w': np.asarray(inp['ada_w'], np.float32),
        'mlp_w1': np.asarray(inp['mlp_w1'], np.float32),
        'mlp_w2': np.asarray(inp['mlp_w2'], np.float32),
    }
    return m


NA_TILES = [(kt, hf) for kt in range(8) for hf in range(2)
            if not ((kt < 2 and hf == 1) or (kt >= 6 and hf == 0))]
NA_TILE_IDX = {t: n for n, t in enumerate(NA_TILES)}


class FullBuilder(Builder):
    def declare_mixer_dram(self):
        self.na_wqkv = self.din('na_w_qkv', [D, 3 * D])
        self.na_wo = self.din('na_w_o', [D, D])
        self.gqa_wqkv = self.din('gqa_w_qkv', [D, 1536])
        self.gqa_wo = self.din('gqa_w_o', [D, D])
        self.s5_gluw = self.din('s5_glu_w', [2, D, 2 * D])
        self.na_kcT = self.din('na_kcT', [D, 512])
        self.na_vc = self.din('na_vc', [512, D])
        self.gqa_kcT = self.din('gqa_kcT', [256, 512])
        self.gqa_vc = self.din('gqa_vc', [512, 256])
        self.na_bias = self.din('na_bias', [16, len(NA_TILES), 128, 512])
        self.gmask = self.din('gmask', [4, 2048])
        self.rope = self.din('rope', [2, 128, T])
        self.o_nak = self.dout('o_nak', [T, D])
        self.o_nav = self.dout('o_nav', [T, D])
        self.o_gk = self.dout('o_gk', [T, 256])
        self.o_gv = self.dout('o_gv', [T, 256])
        self.declare_s5_dram()

    def alloc_mixer_sbuf(self):
        self.PT = [self.sb(f'PT{k}', [128, 512], BF16) for k in range(3)]
        self.PTres = [Res(f'PT{k}') for k in range(3)]
        self.PT += [self.RL[0][:, :].bitcast(BF16)[:, 0:512], self.RL[1][:, :].bitcast(BF16)[:, 0:512]]
        self.PTres += [self.RLres[0], self.RLres[1]]
        self.BT = [self.sb(f'BT{k}', [128, 512], BF16) for k in range(3)]
        self.BTres = [Res(f'BT{k}') for k in range(3)]
        self.RC = self.sb('RC', [128, 512], F32)
        self.RCres = Res('RC')
        self.VF = [self.sb(f'VF{k}', [128, 1024], F32) for k in range(2)]
        self.VFres = [Res(f'VF{k}') for k in range(2)]
        self.SQt = [self.SQ[0][:, 0:512], self.SQ[0][:, 512:1024], self.SQ[1][:, 0:512], self.SQ[1][:, 512:1024]]
        self.SQtres = [Res(f'SQt{k}') for k in range(4)]
        self.RS = [self.RSTD[:, 0:512], self.RSTD[:, 512:1024]]
        self.RSres = [Res(f'RS{k}') for k in range(2)]
        self.SAres = [[Res(f'SAd{k}'), Res(f'SAp{k}')] for k in range(2)]
        self.pt_i = 0
        self.bt_i = 0
        self.sq_i = 0
        self.rs_i = 0
        self.vf_i = 0
        self.kf_i = 0
        S = self.SCR
        self.QT = S[:, 0:8192].rearrange("p (a b) -> p a b", a=8)
        self.QTres = [Res(f'QT{j}') for j in range(8)]
        self.KT_na = S[:, 8192:20480].rearrange("p (a b) -> p a b", a=8)
        self.V_na = S[:, 20480:32768].rearrange("p (a b) -> p a b", a=12)
        self.KT_g = S[:, 8192:11264].rearrange("p (a b) -> p a b", a=2)
        self.V_g = S[:, 11264:14336].rearrange("p (a b) -> p a b", a=12)
        self.CC = S[:, 14336:16384].bitcast(F32)
        self.SS = S[:, 16384:18432].bitcast(F32)
        self.GM = S[:, 18432:20480]
        self.KTres = [Res(f'KT{j}') for j in range(8)]
        self.KCres = Res('KC')
        self.Vres = [Res(f'V{j}') for j in range(12)]
        self.ROPEres = Res('ROPE')
        self.GMres = Res('GM')
        self.alloc_s5_sbuf()

    def wsrc_mixer(self, tag):
        kind, i, ch = tag
        pk = lambda w: w.rearrange("(kt p) n -> p kt n", p=128)
        if kind == 'naqkv':
            return pk(self.na_wqkv)[:, :, ch * 512:(ch + 1) * 512], [128, 8, 512]
        if kind == 'gqkv':
            return pk(self.gqa_wqkv)[:, :, ch * 512:(ch + 1) * 512], [128, 8, 512]
        if kind == 'wo':
            w = self.na_wo if i % 3 == 1 else self.gqa_wo
            return pk(w)[:, :, ch * 512:(ch + 1) * 512], [128, 8, 512]
        assert kind == 'glu'
        sl = i // 3
        srcs = [(pk(self.s5_gluw[sl])[:, :, half * 1024 + ch * 256:half * 1024 + (ch + 1) * 256], half)
                for half in range(2)]
        return srcs, [128, 8, 512]

    def load_mixer_inputs(self):
        pass

    def mixer(self, i):
        kind = i % 3
        self.s.barrier()
        if kind == 1:
            self.na_layer(i)
        elif kind == 2:
            self.gqa_layer(i)
        else:
            self.s5_layer(i)
        self.s.barrier()

    def proj_fm(self, w, wres, m4, h):
        b = self.newbank()
        for kt in range(8):
            self.mm(b, b.t[:, :], w[:, kt, m4 * 128:(m4 + 1) * 128], self.HT[:, kt, h * 512:(h + 1) * 512],
                    [wres, self.HTres[kt]])
        return b

    def qknorm(self, b, onesname, gain_col, out, out_res, out_reads=()):
        k = self.sq_i % 4
        self.sq_i += 1
        self.act(self.SQt[k], b.t[:, :], AF.Square, [b.res], [self.SQtres[k]])
        bm = self.newbank()
        self.mm(bm, bm.t[:, :], self.cb(onesname), self.SQt[k], [self.SQtres[k], self.CBres])
        r = self.rs_i % 2
        self.rs_i += 1
        self.rsqrt(self.RS[r], bm.t[:, :], [bm.res], [self.RSres[r]])
        self.stt('dve', out, b.t[:, :], self.SPt[:, self.cm.sl(gain_col)], self.RS[r], ALU.mult, ALU.mult,
                 [b.res, self.RSres[r], self.SPres] + list(out_reads), [out_res])

    def qk_pipeline(self, items, onesname, stage3, cur=None):
        st = {}
        N = len(items)
        cur = cur or [None, None, None]
        for n in range(N + 2):
            if n < N:
                tag, m4, h, ctx = items[n]
                if cur[0] != tag:
                    w, wres = self.next_w(tag)
                    cur = [tag, w, wres]
                b = self.proj_fm(cur[1], cur[2], m4, h)
                k = self.sq_i % 4
                self.sq_i += 1
                self.act(self.SQt[k], b.t[:, :], AF.Square, [b.res], [self.SQtres[k]])
                st[n] = [b, k, None]
            if 0 <= n - 1 < N:
                b, k, _ = st[n - 1]
                bm = self.newbank()
                self.mm(bm, bm.t[:, :], self.cb(onesname), self.SQt[k], [self.SQtres[k], self.CBres])
                r = self.rs_i % 2
                self.rs_i += 1
                self.rsqrt(self.RS[r], bm.t[:, :], [bm.res], [self.RSres[r]])
                st[n - 1][2] = r
            if 0 <= n - 2 < N:
                b, k, r = st.pop(n - 2)
                stage3(items[n - 2][3], b, r)
        return cur

    def emit_k_out(self, KF, KFres, odram, j):
        k = self.vf_i % 2
        self.vf_i += 1
        for half in range(2):
            b = self.newbank()
            for t4 in range(4):
                tt_ = half * 4 + t4
                self.mm(b, b.t[:, t4 * 128:(t4 + 1) * 128], KF[:, tt_ * 128:(tt_ + 1) * 128], self.IDF[:, :],
                        [KFres, self.IDFres])
            self.cp('act', self.VF[k][:, half * 512:(half + 1) * 512], b.t[:, :], [b.res], [self.VFres[k]])
        dst = odram.rearrange("(tt p) f -> p tt f", p=128)[:, :, j * 128:(j + 1) * 128]
        self.dma('sp', dst, self.VF[k][:, :].rearrange("p (a b) -> p a b", a=8), self.VFres[k],
                 reads=[self.VFres[k]], is_output=True)

    def v_proj(self, w, wres, c0, ncols, Vt, vcol0, odram):
        for tt_ in range(8):
            b = self.newbank()
            for kt in range(8):
                self.mm(b, b.t[:, 0:ncols], self.HT[:, kt, tt_ * 128:(tt_ + 1) * 128], w[:, kt, c0:c0 + ncols],
                        [wres, self.HTres[kt]])
            k = self.vf_i % 2
            self.vf_i += 1
            self.cp('act', self.VF[k][:, 0:ncols], b.t[:, 0:ncols], [b.res], [self.VFres[k]])
            self.cp('pool', Vt[:, tt_, vcol0:vcol0 + ncols], self.VF[k][:, 0:ncols], [self.VFres[k]], [self.Vres[tt_]])
            self.dma('sp', odram[tt_ * 128:(tt_ + 1) * 128, vcol0:vcol0 + ncols], self.VF[k][:, 0:ncols], self.VFres[k],
                     reads=[self.VFres[k]], is_output=True)

    def wo_proj(self, i, src):
        for ch in range(2):
            w, wres = self.next_w(('wo', i, ch))
            for m4 in range(4):
                mt = ch * 4 + m4
                for h in range(2):
                    b = self.newbank()
                    for kt in range(8):
                        self.mm(b, b.t[:, :], w[:, kt, m4 * 128:(m4 + 1) * 128], src[:, kt, h * 512:(h + 1) * 512],
                                [wres, self.HTres[kt]])
                    self.stt('dve', self.X[:, mt, h * 512:(h + 1) * 512], b.t[:, :], self.MODT[:, 16 + mt:17 + mt],
                             self.X[:, mt, h * 512:(h + 1) * 512], ALU.mult, ALU.add,
                             [b.res, self.MODres, self.Xres[mt]], [self.Xres[mt]])

    def attention(self, kind):
        na = kind == 'na'
        nheads = 16 if na else 8
        dh = 64 if na else 128
        scale = dh ** -0.5
        KT = self.KT_na if na else self.KT_g
        V = self.V_na if na else self.V_g
        self.rot = [0, 1, 2, 3]
        self.bank_i = 0
        steps = []
        for hd in range(nheads):
            for hf in range(2):
                tiles = [kt for kt in range(12) if (not na) or kt >= 8 or (kt, hf) in NA_TILE_IDX]
                for n, kt in enumerate(tiles):
                    steps.append((hd, hf, kt, n == 0, n == len(tiles) - 1))
        bias_steps = [st for st in steps if na and st[2] < 8]
        bias_slot = {}
        self._bias_n = 0

        def issue_bias(upto):
            while self._bias_n < min(len(bias_steps), upto):
                hd, hf, kt = bias_steps[self._bias_n][:3]
                bi = self._bias_n % 3
                self.dma('pool', self.BT[bi][:, :], self.na_bias[hd, NA_TILE_IDX[(kt, hf)]], self.BTres[bi],
                         writes=[self.BTres[bi]])
                bias_slot[(hd, hf, kt)] = bi
                self._bias_n += 1

        LA = 4
        NPT = 5
        st_info = {}
        nbias = [0]
        acc = [0]

        def front(n):
            hd, hf, kt, first, last = steps[n]
            if na:
                ht, pr = hd // 2, slice(64 * (hd % 2), 64 * (hd % 2) + 64)
                ktile = ht
            else:
                ht, pr = hd, slice(0, 128)
                ktile = hd // 4
            qs = slice(hf * 512, hf * 512 + 512)
            bs_ = self.newbank()
            ks = slice(kt * 128, kt * 128 + 128)
            kres = self.KTres[ktile] if kt < 8 else self.KCres
            self.mm(bs_, bs_.t[:, :], KT[pr, ktile, ks], self.QT[pr, ht, qs], [kres, self.QTres[ht]])
            p = self.pt_i % NPT
            self.pt_i += 1
            if kt < 8:
                if na:
                    issue_bias(nbias[0] + 3)
                    bi = bias_slot[(hd, hf, kt)]
                    nbias[0] += 1
                    self.mm(bs_, bs_.t[:, :], self.cb('ident'), self.BT[bi][:, :], [self.BTres[bi], self.CBres])
                else:
                    self.mm(bs_, bs_.t[:, :], self.GM[0:4, ks], self.GM[0:4, 1024 + hf * 512:1536 + hf * 512],
                            [self.GMres])
                self.act(self.PT[p][:, :], bs_.t[:, :], AF.Exp, [bs_.res], [self.PTres[p]], bias=0.0, scale=scale)
            else:
                self.act(self.PT[p][:, :], bs_.t[:, :], AF.Exp, [bs_.res, self.SPres], [self.PTres[p]],
                         bias=self.SPt[:, self.cm.sl('ctxb')], scale=scale)
            st_info[n] = p

        def back(n):
            hd, hf, kt, first, last = steps[n]
            if na:
                ht, pr = hd // 2, slice(64 * (hd % 2), 64 * (hd % 2) + 64)
                vc = slice(ht * 128, ht * 128 + 128)
            else:
                ht, pr = hd, slice(0, 128)
                vc = slice((hd // 4) * 128, (hd // 4) * 128 + 128)
            qs = slice(hf * 512, hf * 512 + 512)
            if first:
                acc[0] += 1
                self._bo = self.banks[4 + 2 * (acc[0] % 2)]
                self._bsum = self.banks[5 + 2 * (acc[0] % 2)]
                self._bo.fresh = True
                self._bsum.fresh = True
            bo, bsum = self._bo, self._bsum
            p = st_info.pop(n)
            self.mm(bo, bo.t[:, :], V[:, kt, vc], self.PT[p][:, :], [self.Vres[kt], self.PTres[p]])
            sa = acc[0] % 2
            SA, SAp = self.VF[sa][:, 0:512], self.VF[sa][:, 512:1024]
            rD, rP = self.SAres[sa]
            if first:
                self._cnt = 0
                self._pool_used = False
            if self._cnt % 3 == 2:
                if not self._pool_used:
                    self.cp('pool', SAp, self.PT[p][:, :], [self.PTres[p]], [rP, self.VFres[sa]])
                    self._pool_used = True
                else:
                    self.tt('pool', SAp, SAp, self.PT[p][:, :], ALU.add, [self.PTres[p], rP], [rP])
            else:
                if self._cnt == 0:
                    self.cp('dve', SA, self.PT[p][:, :], [self.PTres[p]], [rD, self.VFres[sa]])
                else:
                    self.tt('dve', SA, SA, self.PT[p][:, :], ALU.add, [self.PTres[p], rD], [rD])
            self._cnt += 1
            if last:
                if self._pool_used:
                    self.tt('dve', SA, SA, SAp, ALU.add, [rD, rP], [rD])
                SB, SBres = self.SQt[0], self.SQtres[0]
                self.cp('dve', SB, SA, [rD], [SBres])
                self.mm(bsum, bsum.t[:, :], self.cb('ones'), SB, [SBres, self.CBres])
                self.s.op('act', lambda e, o=self.RC[pr, :], a=bsum.t[pr, :]: e.activation(out=o, in_=a, func=AF.Ln,
                                                                                          bias=0.0, scale=1.0),
                          reads=[bsum.res], writes=[self.RCres])
                self.s.op('act', lambda e, o=self.RC[pr, :]: e.activation(out=o, in_=o, func=AF.Exp, bias=0.0, scale=-1.0),
                          reads=[self.RCres], writes=[self.RCres])
                self.tt('dve', self.HT[pr, ht, qs], bo.t[pr, :], self.RC[pr, :], ALU.mult,
                        [bo.res, self.RCres], [self.HTres[ht]])

        hook_every = max(1, len(steps) // 9)
        for n in range(len(steps) + LA):
            if n < len(steps):
                if n % hook_every == hook_every - 1:
                    self.mod_hook(1)
                front(n)
            if n - LA >= 0:
                back(n - LA)
        self.rot = list(range(8))

    def na_layer(self, i):
        for j in range(8):
            self.dma('pool', self.KT_na[:, j, 1024:1536], self.na_kcT[j * 128:(j + 1) * 128, :], self.KCres,
                     writes=[self.KCres])
        for t in range(4):
            self.dma('pool', self.V_na[:, 8 + t, :], self.na_vc[t * 128:(t + 1) * 128, :], self.Vres[8 + t],
                     writes=[self.Vres[8 + t]])
        gq_col = self.SPt[:, self.cm.sl('naq')]
        gk_col = self.SPt[:, self.cm.sl('nak')]

        def s3_q(ctx, b, r):
            j, h = ctx
            self.stt('dve', self.QT[:, j, h * 512:(h + 1) * 512], b.t[:, :], gq_col, self.RS[r], ALU.mult, ALU.mult,
                     [b.res, self.RSres[r], self.SPres], [self.QTres[j]])

        def s3_k(ctx, b, r):
            j, h, kf = ctx
            self.stt('dve', self.TMP[kf][:, h * 512:(h + 1) * 512], b.t[:, :], gk_col, self.RS[r], ALU.mult, ALU.mult,
                     [b.res, self.RSres[r], self.SPres], [self.TMPres[kf]])
            if h == 1:
                self.cp('pool', self.KT_na[:, j, 0:1024], self.TMP[kf][:, :], [self.TMPres[kf]], [self.KTres[j]])
                self.emit_k_out(self.TMP[kf], self.TMPres[kf], self.o_nak, j)

        items = [(('naqkv', i, ch), m4, h, (ch * 4 + m4, h)) for ch in range(2) for m4 in range(4) for h in range(2)]
        self.qk_pipeline(items, 'blk64', s3_q)
        items = []
        for ch in range(2, 4):
            for m4 in range(4):
                kf = self.kf_i % 2
                self.kf_i += 1
                for h in range(2):
                    items.append((('naqkv', i, ch), m4, h, ((ch - 2) * 4 + m4, h, kf)))
        self.qk_pipeline(items, 'blk64', s3_k)
        for ch in range(4, 6):
            w, wres = self.next_w(('naqkv', i, ch))
            self.v_proj(w, wres, 0, 512, self.V_na, (ch - 4) * 512, self.o_nav)
        self.attention('na')
        self.wo_proj(i, self.HT)

    def rope_norm(self, b, r, gaincol, h, out, outres, kf=None, kfres=None):
        hs = slice(h * 512, (h + 1) * 512)
        QN, QNres = self.RL[0], self.RLres[0]
        T1, T1res = self.RL[1], self.RLres[1]
        T2, T2res = self.RC, self.RCres
        QNb, QNbres = self.PT[0], self.PTres[0]
        self.stt('dve', QN[:, :], b.t[:, :], self.SPt[:, self.cm.sl(gaincol)], self.RS[r], ALU.mult, ALU.mult,
                 [b.res, self.RSres[r], self.SPres], [QNres])
        self.cp('act', QNb[:, :], QN[:, :], [QNres], [QNbres])
        bsw = self.newbank()
        self.mm(bsw, bsw.t[:, :], self.cb('pswap'), QNb[:, :], [QNbres, self.CBres])
        self.tt('pool', T1[:, :], QN[:, :], self.CC[:, hs], ALU.mult, [QNres, self.ROPEres], [T1res])
        self.tt('dve', T2[:, :], bsw.t[:, :], self.SS[:, hs], ALU.mult, [bsw.res, self.ROPEres], [T2res])
        if kf is None:
            self.tt('dve', out, T1[:, :], T2[:, :], ALU.add, [T1res, T2res], [outres])
        else:
            self.tt('dve', kf, T1[:, :], T2[:, :], ALU.add, [T1res, T2res], [kfres])
            self.cp('act', out, kf, [kfres], [outres])

    def gqa_layer(self, i):
        self.dma('sp', self.CC[:, :], self.rope[0], self.ROPEres, writes=[self.ROPEres])
        self.dma('sp', self.SS[:, :], self.rope[1], self.ROPEres, writes=[self.ROPEres])
        self.dma('pool', self.GM[0:4, :], self.gmask, self.GMres, writes=[self.GMres])
        for kv in range(2):
            self.dma('pool', self.KT_g[:, kv, 1024:1536], self.gqa_kcT[kv * 128:(kv + 1) * 128, :], self.KCres,
                     writes=[self.KCres])
        for t in range(4):
            self.dma('pool', self.V_g[:, 8 + t, :], self.gqa_vc[t * 128:(t + 1) * 128, :], self.Vres[8 + t],
                     writes=[self.Vres[8 + t]])
        def s3_q(ctx, b, r):
            j, h = ctx
            self.rope_norm(b, r, 'gq', h, self.QT[:, j, h * 512:(h + 1) * 512], self.QTres[j])

        def s3_k(ctx, b, r):
            kv, h, kf = ctx
            self.rope_norm(b, r, 'gk', h, self.KT_g[:, kv, h * 512:(h + 1) * 512], self.KTres[kv],
                           kf=self.TMP[kf][:, h * 512:(h + 1) * 512], kfres=self.TMPres[kf])
            if h == 1:
                self.emit_k_out(self.TMP[kf], self.TMPres[kf], self.o_gk, kv)

        items = [(('gqkv', i, ch), m4, h, (ch * 4 + m4, h)) for ch in range(2) for m4 in range(4) for h in range(2)]
        cur = None
        for it in items:
            cur = self.qk_pipeline([it], 'ones128', s3_q, cur)
        items = []
        for kv in range(2):
            kf = self.kf_i % 2
            self.kf_i += 1
            for h in range(2):
                items.append((('gqkv', i, 2), kv, h, (kv, h, kf)))
        for it in items:
            cur = self.qk_pipeline([it], 'ones128', s3_k, cur)
        _, w, wres = cur
        self.v_proj(w, wres, 256, 256, self.V_g, 0, self.o_gv)
        self.attention('gqa')
        self.wo_proj(i, self.HT)

    def declare_s5_dram(self):
        pass

    def alloc_s5_sbuf(self):
        pass

    def plan_s5_weights(self, i):
        pass

    def s5_layer(self, i):
        pass


_TABLE_CACHE = {}


def na_bias_table(rpb, sample):
    if sample:
        q = np.arange(T)
        k = np.arange(T)
        qr, qc = (q // 64)[:, None], (q % 64)[:, None]
        kr, kc = (k // 64)[None, :], (k % 64)[None, :]
        rs = np.clip(qr - 4, 0, 8)
        cs = np.clip(qc - 8, 0, 48)
        ok = (kr >= rs) & (kr < rs + 8) & (kc >= cs) & (kc < cs + 16)
        drow = np.clip(kr - qr + 7, 0, 14)
        dc = np.clip(kc - qc + 15, 0, 30)
        full = np.where(ok[None], np.asarray(rpb, np.float32)[:, drow, dc], np.float32(NEG))
    else:
        q = np.arange(T)
        same = (q[:, None] // 256) == (q[None, :] // 256)
        full = np.broadcast_to(np.where(same, np.float32(0.0), np.float32(NEG))[None], (16, T, T))
    out = np.empty((16, len(NA_TILES), 128, 512), np.float32)
    for n, (kt, hf) in enumerate(NA_TILES):
        out[:, n] = np.transpose(full[:, hf * 512:(hf + 1) * 512, kt * 128:(kt + 1) * 128], (0, 2, 1))
    return out


def rope_tables(sample):
    if not sample:
        return np.stack([np.ones((128, T), np.float32), np.zeros((128, T), np.float32)])
    t = np.arange(T)
    row = (t // 64).astype(np.float32)
    col = (t % 64).astype(np.float32)
    half = 64
    inv = (np.float32(10000.0) ** (-np.arange(0, half, 2, dtype=np.float32) / np.float32(half))).astype(np.float32)
    ang = np.concatenate([row[:, None] * inv, col[:, None] * inv], axis=-1).astype(np.float32)
    c, s_ = np.cos(ang).astype(np.float32).T, np.sin(ang).astype(np.float32).T
    return np.stack([np.concatenate([c, c], 0), np.concatenate([-s_, s_], 0)])


def gmask_table(sample):
    g = np.zeros((4, 2048), np.float32)
    k = np.arange(T)
    for j in range(4):
        g[j, :T] = (k // 256 == j)
        if not sample:
            g[j, T:] = np.where(k // 256 == j, 0.0, NEG)
    return g


def mixer_inputs(inp, core):
    sample = core < 4
    m = {
        'na_w_qkv': np.asarray(inp['na_w_qkv'][0], np.float32),
        'na_w_o': np.asarray(inp['na_w_o'][0], np.float32),
        'gqa_w_qkv': np.asarray(inp['gqa_w_qkv'][0], np.float32),
        'gqa_w_o': np.asarray(inp['gqa_w_o'][0], np.float32),
        's5_glu_w': np.asarray(inp['s5_glu_w'], np.float32),
    }
    if sample:
        m['na_kcT'] = np.ascontiguousarray(np.asarray(inp['cache_na_k'][core, 0], np.float32).reshape(512, D).T)
        m['na_vc'] = np.ascontiguousarray(np.asarray(inp['cache_na_v'][core, 0], np.float32).reshape(512, D))
        m['gqa_kcT'] = np.ascontiguousarray(np.asarray(inp['cache_gqa_k'][core, 0], np.float32).reshape(512, 256).T)
        m['gqa_vc'] = np.ascontiguousarray(np.asarray(inp['cache_gqa_v'][core, 0], np.float32).reshape(512, 256))
    else:
        m['na_kcT'] = np.zeros((D, 512), np.float32)
        m['na_vc'] = np.zeros((512, D), np.float32)
        m['gqa_kcT'] = np.zeros((256, 512), np.float32)
        m['gqa_vc'] = np.zeros((512, 256), np.float32)
    key = ('nab', sample)
    if key not in _TABLE_CACHE:
        _TABLE_CACHE[key] = na_bias_table(inp['na_rpb'][0], sample)
        _TABLE_CACHE[('rope', sample)] = rope_tables(sample)
        _TABLE_CACHE[('gm', sample)] = gmask_table(sample)
    m['na_bias'] = _TABLE_CACHE[key]
    m['rope'] = _TABLE_CACHE[('rope', sample)]
    m['gmask'] = _TABLE_CACHE[('gm', sample)]
    return m


def make_builder(layers, mixers=True):
    dry = S5Builder(layers=layers, mixers=mixers, plan=None)
    dry.build()
    b = S5Builder(layers=layers, mixers=mixers, plan=dry.rec_tags)
    nc = b.build()
    return b, nc


def run(inp, layers=(0, 1, 2, 3), cores=None):
    b, nc = make_builder(layers)
    maps = []
    if cores is not None:
        for core in cores:
            m = common_inputs(inp, core, b.cm)
            m.update(mixer_inputs(inp, core))
            m.update(s5_inputs(inp, core))
            maps.append({k: v for k, v in m.items() if k in b.dram})
        res = run_bass_kernel_spmd(nc, maps, core_ids=list(range(len(cores))))
        return res.results
    for core in range(8):
        m = common_inputs(inp, core, b.cm)
        m.update(mixer_inputs(inp, core))
        m.update(s5_inputs(inp, core))
        maps.append({k: v for k, v in m.items() if k in b.dram})
    res = run_bass_kernel_spmd(nc, maps, core_ids=list(range(8)))
    R = res.results
    y_sample = np.stack([R[c]['yT'].T for c in range(4)]).astype(np.float32)
    y_prompt = np.concatenate([R[c]['yT'].T.reshape(4, 256, D) for c in range(4, 8)]).astype(np.float32)
    nak = np.concatenate([R[c]['o_nak'].reshape(4, 256, 16, 64) for c in range(4, 8)])[:, None]
    nav = np.concatenate([R[c]['o_nav'].reshape(4, 256, 16, 64) for c in range(4, 8)])[:, None]
    gk = np.concatenate([R[c]['o_gk'].reshape(4, 256, 2, 128) for c in range(4, 8)])[:, None]
    gv = np.concatenate([R[c]['o_gv'].reshape(4, 256, 2, 128) for c in range(4, 8)])[:, None]
    s5 = s5_assemble(R)
    return (y_prompt, y_sample, s5, nak.astype(np.float32), nav.astype(np.float32), gk.astype(np.float32),
            gv.astype(np.float32))


def kernel(**inputs):
    return run(inputs)


S5_NSM = 48


def s5_host_arrays(inp, core):
    sample = core < 4
    lam = np.zeros((2, 3, 128, 64), np.float32)
    Bm = np.zeros((2, 128, 2, 64, 16), np.float32)
    Cm = np.zeros((2, 128, 2, 64, 16), np.float32)
    h0 = np.zeros((2, 128, 2, 64), np.float32)
    for sl in range(2):
        for dr in range(2):
            rows = slice(dr * 64, dr * 64 + 64)
            lam[sl, 0, rows] = np.asarray(inp['s5_lam_re'][sl, dr], np.float32).T
            lam[sl, 1, rows] = np.asarray(inp['s5_lam_im'][sl, dr], np.float32).T
            lam[sl, 2, rows] = np.asarray(inp['s5_log_dt'][sl, dr], np.float32)[None, :]
            Bm[sl, rows, 0] = np.transpose(np.asarray(inp['s5_b_re'][sl, dr], np.float32), (1, 0, 2))
            Bm[sl, rows, 1] = np.transpose(np.asarray(inp['s5_b_im'][sl, dr], np.float32), (1, 0, 2))
            Cm[sl, rows, 0] = np.transpose(np.asarray(inp['s5_c_re'][sl, dr], np.float32), (2, 0, 1))
            Cm[sl, rows, 1] = np.transpose(np.asarray(inp['s5_c_im'][sl, dr], np.float32), (2, 0, 1))
            if sample:
                for ri in range(2):
                    h0[sl, rows, ri] = np.asarray(inp['state_s5'][core, sl, dr, ri], np.float32).T
    carry = np.ones((128, 128), np.float32)
    carry[:, 0] = 0.0
    carry[:, [32, 64, 96]] = 1.0 if sample else 0.0
    sg = np.arange(128) // 16
    tri = (sg[None, :] >= sg[:, None]).astype(np.float32)
    return {'s5_lam': lam, 's5_B': Bm, 's5_C': Cm, 's5_h0': h0, 's5_carry': carry, 's5_tri': tri,
            's5_antif': np.ascontiguousarray(np.eye(128, dtype=np.float32)[::-1])}


def s5_inputs(inp, core):
    return s5_host_arrays(inp, core)


def s5_core_states(res):
    return np.transpose(res['o_s5'], (1, 0, 2, 3, 4, 5))


def s5_assemble(R):
    return np.concatenate([s5_core_states(R[c]) for c in range(4, 8)]).astype(np.float32)


class S5Builder(FullBuilder):
    def declare_s5_dram(self):
        self.d_lam = self.din('s5_lam', [2, 3, 128, 64])
        self.d_B = self.din('s5_B', [2, 128, 2, 64, 16])
        self.d_C = self.din('s5_C', [2, 128, 2, 64, 16])
        self.d_h0 = self.din('s5_h0', [2, 128, 2, 64])
        self.d_carry = self.din('s5_carry', [128, 128])
        self.d_tri = self.din('s5_tri', [128, 128])
        self.d_antif = self.din('s5_antif', [128, 128])
        self.o_s5 = self.dout('o_s5', [2, 4, 2, 2, 64, 64])

    def alloc_s5_sbuf(self):
        self.SML = self.sb('SML', [128, S5_NSM, 64], F32)
        self.sm_idx = {}
        self.S5S = Res('S5S')
        self.BLKres = Res('BLK')
        self.BLKQres = Res('BLKQ')
        self.LAM3 = self.sb('LAM3', [128, 3, 64], F32)
        self.H0 = self.sb('H0', [128, 2, 64], F32)
        self.CARRY = self.sb('CARRY', [128, 128], F32)
        self.TRI = self.sb('TRI', [128, 128], F32)
        self.ANTIF = self.sb('ANTIF', [128, 128], F32)
        self.S5Cres = Res('S5C')
        self.PB = self.sb('PB', [128, 2, 8, 16], F32)
        self.PC = self.sb('PC', [128, 2, 8, 16], F32)
        self.PBres = Res('PB')
        self.BLK = self.sb('BLK', [128, 11, 72], F32)
        self.FINALL = self.RC[:, :].rearrange("p (s r g) -> p s r g", s=4, r=2)
        self.FINres = self.RCres
        S = self.SCR
        o = [0]

        def carve(n, dt=BF16):
            a = S[:, o[0]:o[0] + n]
            o[0] += n
            return a if dt == BF16 else a.bitcast(F32)
        self.HTOK = carve(1024).rearrange("p (a b) -> p a b", a=8)
        self.HTOKR = carve(1024).rearrange("p (a b) -> p a b", a=8)
        self.U = carve(1024).rearrange("p (a b) -> p a b", a=8)
        self.UR = carve(1024).rearrange("p (a b) -> p a b", a=8)
        self.EIN = carve(2048).rearrange("p (g r n) -> p g r n", g=8, r=2)
        self.AOUT = carve(2304).rearrange("p (g r n) -> p g r n", g=8, r=2)
        self.GEN7 = carve(2048).rearrange("p (g r n) -> p g r n", g=8, r=2)
        self.MGF = carve(2048).rearrange("p (g r n) -> p g r n", g=8, r=2)
        self.MGB = carve(2048).rearrange("p (g r n) -> p g r n", g=8, r=2)
        self.MINT = carve(2048).rearrange("p (g r n) -> p g r n", g=8, r=2)
        self.HPB = carve(2048).rearrange("p (r g c) -> p r g c", r=2, g=8)
        self.GR = carve(2048, F32).rearrange("p (g c) -> p g c", g=8)
        self.GI = carve(2048, F32).rearrange("p (g c) -> p g c", g=8)
        self.COS = carve(2048, F32).rearrange("p (g c) -> p g c", g=8)
        self.SIN = carve(2048, F32).rearrange("p (g c) -> p g c", g=8)
        self.TBG = S[:, o[0] - 8192:o[0] - 4096].bitcast(F32)
        self.TB2 = S[:, o[0]:o[0] + 4096].bitcast(F32)
        self.T1 = carve(2048, F32)
        self.T2 = carve(2048, F32)
        self.T3 = S[:, 8448:10496].bitcast(F32)
        self.AMt = S[:, 4096:6144].bitcast(F32)
        assert o[0] <= 32768, o[0]
        self.r_htok, self.r_htokr, self.r_u, self.r_ur = Res('htok'), Res('htokr'), Res('u'), Res('ur')
        self.r_ein, self.r_aout, self.r_gen7 = Res('ein'), Res('aout'), Res('gen7')
        self.r_mgf, self.r_mgb, self.r_mint, self.r_hpb = Res('mgf'), Res('mgb'), Res('mint'), Res('hpb')
        self.r_g, self.r_cs, self.r_t1, self.r_t2 = Res('g'), Res('cs'), Res('t1'), Res('t2')

    def sm(self, name):
        if name not in self.sm_idx:
            self.sm_idx[name] = len(self.sm_idx)
            assert len(self.sm_idx) <= S5_NSM - 2, name
        return self.SML[:, self.sm_idx[name], :]

    def _srw(self, extra):
        w = getattr(self, '_s_wres', None) or self.S5S
        return [self.S5S, w] + list(extra), [w]

    def s_tt(self, out, a, b, op, extra=()):
        r, w = self._srw(extra)
        self.s.op(getattr(self, '_s_eng', 'dve'), lambda e: e.tensor_tensor(out, a, b, op), reads=r, writes=w)

    def s_ts(self, out, a, s1, s2, op0, op1=None, extra=()):
        r, w = self._srw(extra)
        if op1 is None:
            self.s.op('dve', lambda e: e.tensor_scalar(out, a, s1, None, op0), reads=r, writes=w)
        else:
            self.s.op('dve', lambda e: e.tensor_scalar(out, a, s1, s2, op0, op1), reads=r, writes=w)

    def s_stt(self, out, a, sc, b, op0, op1, extra=()):
        r, w = self._srw(extra)
        self.s.op('dve', lambda e: e.scalar_tensor_tensor(out, a, sc, b, op0, op1), reads=r, writes=w)

    def s_cmul(self, outr, outi, ar, ai, br, bi, t1, t2, extra=()):
        self.s_tt(t1, ar, br, ALU.mult, extra)
        self.s_tt(t2, ai, bi, ALU.mult, extra)
        self.s_tt(outr, t1, t2, ALU.subtract)
        self.s_tt(t1, ar, bi, ALU.mult, extra)
        self.s_tt(t2, ai, br, ALU.mult, extra)
        self.s_tt(outi, t1, t2, ALU.add)

    def s_expm1(self, dr, di, zr, zi, nsq, deg, pre):
        t1, t2, t3, t4 = (self.sm(pre + n) for n in ('t1', 't2', 't3', 't4'))
        sr, si = self.sm(pre + 'sr'), self.sm(pre + 'si')
        xr, xi = self.sm(pre + 'xr'), self.sm(pre + 'xi')
        sc = 1.0 / (1 << nsq)
        self.s_ts(xr, zr, sc, None, ALU.mult)
        if zi is not None:
            self.s_ts(xi, zi, sc, None, ALU.mult)
        self.s_ts(sr, xr, 1.0 / deg, 1.0, ALU.mult, ALU.add)
        if zi is not None:
            self.s_ts(si, xi, 1.0 / deg, None, ALU.mult)
        for k in range(deg - 1, 1, -1):
            if zi is not None:
                self.s_tt(t1, xr, sr, ALU.mult)
                self.s_tt(t2, xi, si, ALU.mult)
                self.s_tt(t3, xr, si, ALU.mult)
                self.s_tt(t4, xi, sr, ALU.mult)
                self.s_tt(t1, t1, t2, ALU.subtract)
                self.s_tt(t3, t3, t4, ALU.add)
                self.s_ts(sr, t1, 1.0 / k, 1.0, ALU.mult, ALU.add)
                self.s_ts(si, t3, 1.0 / k, None, ALU.mult)
            else:
                self.s_tt(t1, xr, sr, ALU.mult)
                self.s_ts(sr, t1, 1.0 / k, 1.0, ALU.mult, ALU.add)
        if zi is not None:
            self.s_cmul(dr, di, xr, xi, sr, si, t1, t2)
        else:
            self.s_tt(dr, xr, sr, ALU.mult)
        for _ in range(nsq):
            if zi is not None:
                self.s_tt(t1, dr, dr, ALU.mult)
                self.s_tt(t2, di, di, ALU.mult)
                self.s_tt(t3, dr, di, ALU.mult)
                self.s_tt(t1, t1, t2, ALU.subtract)
                self.s_stt(dr, dr, 2.0, t1, ALU.mult, ALU.add)
                self.s_tt(t3, t3, di, ALU.add)
                self.s_ts(di, t3, 2.0, None, ALU.mult)
            else:
                self.s_tt(t1, dr, dr, ALU.mult)
                self.s_stt(dr, dr, 2.0, t1, ALU.mult, ALU.add)

    def s5_layer_params(self, sl):
        sm = self.sm
        self.dma('sp', self.LAM3[:, :, :], self.d_lam[sl].rearrange("a p g -> p a g"), self.S5S, writes=[self.S5S])
        self.dma('sp', self.H0[:, :, :], self.d_h0[sl], self.S5S, writes=[self.S5S])
        lamr, lami, ldt = self.LAM3[:, 0, :], self.LAM3[:, 1, :], self.LAM3[:, 2, :]
        self.s_expm1(sm('dt'), None, ldt, None, 6, 7, 'e_')
        self.s_ts(sm('dt'), sm('dt'), 1.0, None, ALU.add)
        self.s_tt(sm('ar'), lamr, sm('dt'), ALU.mult)
        self.s_tt(sm('ai'), lami, sm('dt'), ALU.mult)
        self.s_expm1(sm('nr'), sm('ni'), sm('ar'), sm('ai'), 7, 8, 'e_')
        self.s_ts(sm('abr'), sm('nr'), 1.0, None, ALU.add)
        t1, t2 = sm('e_t1'), sm('e_t2')
        self.s_tt(t1, lamr, lamr, ALU.mult)
        self.s_tt(t2, lami, lami, ALU.mult)
        self.s_tt(t1, t1, t2, ALU.add)
        self.s.op('dve', lambda e: e.reciprocal(sm('rden'), t1), reads=[self.S5S], writes=[self.S5S])
        self.s_tt(t1, sm('nr'), lamr, ALU.mult)
        self.s_tt(t2, sm('ni'), lami, ALU.mult)
        self.s_tt(t1, t1, t2, ALU.add)
        self.s_tt(sm('fre'), t1, sm('rden'), ALU.mult)
        self.s_tt(t1, sm('ni'), lamr, ALU.mult)
        self.s_tt(t2, sm('nr'), lami, ALU.mult)
        self.s_tt(t1, t1, t2, ALU.subtract)
        self.s_tt(sm('fim'), t1, sm('rden'), ALU.mult)
        self.s_ts(t1, sm('ar'), 2.0, None, ALU.mult)
        self.s_expm1(sm('m2'), None, t1, None, 5, 6, 'e_')
        self.s_ts(sm('m2'), sm('m2'), 1.0, None, ALU.add)
        self.s.op('dve', lambda e: e.reciprocal(sm('im2'), sm('m2')), reads=[self.S5S], writes=[self.S5S])
        self.s_tt(sm('q1r'), sm('abr'), sm('im2'), ALU.mult)
        self.s_stt(sm('q1i'), sm('ni'), -1.0, sm('im2'), ALU.mult, ALU.mult)
        cr, ci = sm('abr'), sm('ni')
        for k, nm in enumerate(('p2', 'p4', 'mu')):
            self.s_cmul(sm(nm + 'r'), sm(nm + 'i'), cr, ci, cr, ci, t1, t2)
            cr, ci = sm(nm + 'r'), sm(nm + 'i')
        self.s_tt(sm('rho8'), sm('m2'), sm('m2'), ALU.mult)
        self.s_tt(sm('rho8'), sm('rho8'), sm('rho8'), ALU.mult)
        self.s.op('dve', lambda e: e.reciprocal(t1, sm('rho8')), reads=[self.S5S], writes=[self.S5S])
        self.s_tt(sm('E0r'), sm('mur'), t1, ALU.mult)
        self.s_tt(sm('E0i'), sm('mui'), t1, ALU.mult)
        for k in range(1, 7):
            self.s_cmul(sm(f'E{k}r'), sm(f'E{k}i'), sm(f'E{k-1}r'), sm(f'E{k-1}i'), sm(f'E{k-1}r'), sm(f'E{k-1}i'), t1, t2)
        self.s_cmul(sm('g0r'), sm('g0i'), sm('mur'), sm('mui'), self.H0[:, 0, :], self.H0[:, 1, :], t1, t2)

    def blk(self, k):
        return self.BLK[:, k, :].rearrange("p (g n) -> p g n", g=8)

    def s5_block_tables(self, sl, j):
        sm = self.sm
        gs = slice(8 * j, 8 * j + 8)
        B = self.blk
        PWR, PWI, QR, QI, WBR, WBI, W7R, W7I, TA, TB = (B(k) for k in range(10))
        self._s_wres = self.BLKres
        self.dma('sp', self.PB[:, :, :, :], self.d_B[sl][:, :, gs, :], self.PBres, writes=[self.PBres])
        self.dma('sp', self.PC[:, :, :, :], self.d_C[sl][:, :, gs, :], self.PBres, writes=[self.PBres])

        def col(ap, n):
            return ap[:, :, n:n + 1]

        def sv(name):
            return sm(name)[:, gs].unsqueeze(2)

        def bc(ap, shape):
            return ap.to_broadcast(shape)

        TA2 = self.SML[:, S5_NSM - 2, :].rearrange("p (g n) -> p g n", g=8)
        TB2 = self.SML[:, S5_NSM - 1, :].rearrange("p (g n) -> p g n", g=8)

        def cmul_tab(outr, outi, ar, ai, br, bi, shape, tmps=None, extra=()):
            ta_, tb_ = tmps or (TA, TB)
            ta = ta_[:, :, 0:shape[2]]
            tb = tb_[:, :, 0:shape[2]]
            self.s_cmul(outr, outi, ar, ai, bc(br, shape), bc(bi, shape), ta, tb, extra)

        for (TR, TI, b1r, b1i, b2r, b2i, b4r, b4i) in (
                (PWR, PWI, 'abr', 'ni', 'p2r', 'p2i', 'p4r', 'p4i'),):
            self.s.op('dve', lambda e, o=col(TR, 0): e.memset(o, 1.0), reads=[self.BLKres], writes=[self.BLKres])
            self.s.op('dve', lambda e, o=col(TI, 0): e.memset(o, 0.0), reads=[self.BLKres], writes=[self.BLKres])
            self.s_tt(col(TR, 1), sv(b1r), sv(b1r), ALU.max)
            self.s_tt(col(TI, 1), sv(b1i), sv(b1i), ALU.max)
            self.s_tt(col(TR, 2), sv(b2r), sv(b2r), ALU.max)
            self.s_tt(col(TI, 2), sv(b2i), sv(b2i), ALU.max)
            cmul_tab(TR[:, :, 3:5], TI[:, :, 3:5], TR[:, :, 1:3], TI[:, :, 1:3], sv(b2r), sv(b2i), [128, 8, 2])
            cmul_tab(TR[:, :, 5:9], TI[:, :, 5:9], TR[:, :, 1:5], TI[:, :, 1:5], sv(b4r), sv(b4i), [128, 8, 4])
        self._s_eng = 'pool'
        self._s_wres = self.BLKQres
        q2 = (TA2, TB2)
        self.s.op('pool', lambda e, o=col(QR, 0): e.memset(o, 1.0), reads=[self.BLKQres], writes=[self.BLKQres])
        self.s.op('pool', lambda e, o=col(QI, 0): e.memset(o, 0.0), reads=[self.BLKQres], writes=[self.BLKQres])
        self.s.op('pool', lambda e, o=col(QR, 1), a=sv('q1r'): e.tensor_copy(o, a), reads=[self.S5S, self.BLKQres],
                  writes=[self.BLKQres])
        self.s.op('pool', lambda e, o=col(QI, 1), a=sv('q1i'): e.tensor_copy(o, a), reads=[self.S5S, self.BLKQres],
                  writes=[self.BLKQres])
        self.s_cmul(col(QR, 2), col(QI, 2), col(QR, 1), col(QI, 1), col(QR, 1), col(QI, 1), col(TA2, 0), col(TB2, 0))
        cmul_tab(QR[:, :, 3:5], QI[:, :, 3:5], QR[:, :, 1:3], QI[:, :, 1:3], col(QR, 2), col(QI, 2), [128, 8, 2], q2)
        cmul_tab(QR[:, :, 5:9], QI[:, :, 5:9], QR[:, :, 1:5], QI[:, :, 1:5], col(QR, 4), col(QI, 4), [128, 8, 4], q2)
        cmul_tab(WBR[:, :, 0:8], WBI[:, :, 0:8], QR[:, :, 0:8], QI[:, :, 0:8], sv('fre'), sv('fim'), [128, 8, 8], q2)
        self._s_eng = 'dve'
        self._s_wres = self.BLKres
        cmul_tab(W7R[:, :, 0:8], W7I[:, :, 0:8], WBR[:, :, 0:8], WBI[:, :, 0:8], col(PWR, 7), col(PWI, 7), [128, 8, 8],
                 None, [self.BLKQres])
        self._s_wres = None

    def s5_block_expand_in(self, sl, j):
        B = self.blk
        PWR, PWI, QR, QI, WBR, WBI, W7R, W7I, TA, TB = (B(k) for k in range(10))
        sh = [128, 8, 8, 16]
        Br = self.PB[:, 0, :, :].unsqueeze(2).to_broadcast(sh)
        Bi = self.PB[:, 1, :, :].unsqueeze(2).to_broadcast(sh)
        for (WR, WI, DST, dres) in ((WBR, WBI, self.EIN, self.r_ein), (W7R, W7I, self.GEN7, self.r_gen7)):
            wr = WR[:, :, 0:8].unsqueeze(3).to_broadcast(sh)
            wi = WI[:, :, 0:8].unsqueeze(3).to_broadcast(sh)
            a1 = self.TBG[:, 0:1024].rearrange("p (g s k) -> p g s k", g=8, s=8)
            a2 = self.TBG[:, 1024:2048].rearrange("p (g s k) -> p g s k", g=8, s=8)
            ex = [self.PBres, self.r_g, self.BLKres, self.BLKQres]
            w_ = [self.r_g]
            self.s.op('dve', lambda e, o=a1, x=wr, y=Br: e.tensor_tensor(o, x, y, ALU.mult), reads=[self.S5S] + ex, writes=w_)
            self.s.op('dve', lambda e, o=a2, x=wi, y=Bi: e.tensor_tensor(o, x, y, ALU.mult), reads=[self.S5S] + ex, writes=w_)
            dre = DST[:, :, 0, :].rearrange("p g (s k) -> p g s k", s=8)
            self.s.op('dve', lambda e, o=dre, x=a1, y=a2: e.tensor_tensor(o, x, y, ALU.subtract), reads=w_, writes=w_ + [dres])
            self.s.op('dve', lambda e, o=a1, x=wr, y=Bi: e.tensor_tensor(o, x, y, ALU.mult), reads=[self.S5S] + ex, writes=w_)
            self.s.op('dve', lambda e, o=a2, x=wi, y=Br: e.tensor_tensor(o, x, y, ALU.mult), reads=[self.S5S] + ex, writes=w_)
            dim_ = DST[:, :, 1, :].rearrange("p g (s k) -> p g s k", s=8)
            self.s.op('dve', lambda e, o=dim_, x=a1, y=a2: e.tensor_tensor(o, x, y, ALU.add), reads=w_, writes=w_ + [dres])

    def s5_block_expand(self, sl, j):
        B = self.blk
        PWR, PWI, QR, QI, WBR, WBI, W7R, W7I, TA, TB = (B(k) for k in range(10))
        t1 = self.TBG[:, 0:1152]
        t2 = self.TB2[:, 0:1152]
        sh9 = [128, 8, 9, 16]
        Cr = self.PC[:, 0, :, :].unsqueeze(2).to_broadcast(sh9)
        Ci = self.PC[:, 1, :, :].unsqueeze(2).to_broadcast(sh9)
        pr = PWR[:, :, 0:9].unsqueeze(3).to_broadcast(sh9)
        pi = PWI[:, :, 0:9].unsqueeze(3).to_broadcast(sh9)
        a1 = t1.rearrange("p (g s k) -> p g s k", g=8, s=9)
        a2 = t2.rearrange("p (g s k) -> p g s k", g=8, s=9)
        ex = [self.PBres, self.r_t1, self.r_t2, self.r_g, self.BLKres, self.BLKQres]
        w_ = [self.r_t1, self.r_t2, self.r_g]
        self.s.op('dve', lambda e: e.tensor_tensor(a1, Cr, pr, ALU.mult), reads=[self.S5S] + ex, writes=w_)
        self.s.op('dve', lambda e: e.tensor_tensor(a2, Ci, pi, ALU.mult), reads=[self.S5S] + ex, writes=w_)
        dre = self.AOUT[:, :, 0, :].rearrange("p g (s k) -> p g s k", s=9)
        self.s.op('dve', lambda e: e.tensor_tensor(dre, a1, a2, ALU.subtract), reads=w_, writes=w_ + [self.r_aout])
        self.s.op('dve', lambda e: e.tensor_tensor(a1, Cr, pi, ALU.mult), reads=[self.S5S] + ex, writes=w_)
        self.s.op('dve', lambda e: e.tensor_tensor(a2, Ci, pr, ALU.mult), reads=[self.S5S] + ex, writes=w_)
        dim_ = self.AOUT[:, :, 1, :].rearrange("p g (s k) -> p g s k", s=9)
        self.s.op('dve', lambda e: e.scalar_tensor_tensor(dim_, a1, -1.0, a2, ALU.mult, ALU.subtract), reads=w_,
                  writes=w_ + [self.r_aout])

    def s5_block(self, i, sl, j):
        sm = self.sm
        gs = slice(8 * j, 8 * j + 8)
        ident, anti = self.cb('ident'), self.cb('anti')
        hsrc = self.HT[:, j, :].rearrange("p (c s) -> p s c", s=8)
        for (dst, dres, rev) in ((self.HTOK, self.r_htok, False), (self.HTOKR, self.r_htokr, True)):
            for half in range(2):
                b = self.newbank()
                for q in range(4):
                    s_ = half * 4 + q
                    src_s = 7 - s_ if rev else s_
                    self.mm(b, b.t[:, q * 128:(q + 1) * 128], hsrc[:, src_s, :], ident, [self.HTres[j], self.CBres])
                dv = dst.rearrange("p g (s k) -> p s g k", s=8)[:, half * 4:half * 4 + 4, :, :]
                self.cp('act', dv, b.t[:, :].rearrange("p (s g k) -> p s g k", s=4, g=8), [b.res], [dres])
        for (src, sres, dst, dres, mat) in ((self.HTOK, self.r_htok, self.U, self.r_u, ident),
                                            (self.HTOKR, self.r_htokr, self.UR, self.r_ur, anti)):
            for half in range(2):
                b = self.newbank()
                for q in range(4):
                    g = half * 4 + q
                    self.mm(b, b.t[:, q * 128:(q + 1) * 128], src[:, g, :], mat, [sres, self.CBres])
                self.cp('act', dst[:, half * 4:half * 4 + 4, :], b.t[:, :].rearrange("p (a b) -> p a b", a=4),
                        [b.res], [dres])
        if j == 0:
            self.s5_block_tables(sl, 0)
            self.s5_block_expand_in(sl, 0)
        self.s5_block_expand(sl, j)
        for ri in range(2):
            for half in range(2):
                b = self.newbank()
                for q in range(4):
                    g = half * 4 + q
                    self.mm(b, b.t[:, q * 128:(q + 1) * 128], self.GEN7[:, g, ri, :], ident, [self.r_gen7, self.CBres])
                v = b.t[:, :].rearrange("p (a b) -> p a b", a=4)
                self.cp('act', self.MGF[:, half * 4:half * 4 + 4, ri, 0:64], v[:, :, 0:64], [b.res], [self.r_mgf])
                self.cp('act', self.MGB[:, half * 4:half * 4 + 4, ri, 64:128], v[:, :, 64:128], [b.res], [self.r_mgb])
        for dr in range(2):
            rows = slice(dr * 64, dr * 64 + 64)
            for half in range(2):
                b = self.newbank()
                for q in range(4):
                    g = half * 4 + q
                    for ri in range(2):
                        self.mm(b, b.t[:, q * 128:(q + 1) * 128], self.EIN[rows, g, ri, :], self.AOUT[rows, g, ri, 0:128],
                                [self.r_ein, self.r_aout])
                self.s.op('dve', lambda e, o=self.MINT[:, half * 4:half * 4 + 4, dr, :],
                          a=b.t[:, :].rearrange("p (a b) -> p a b", a=4),
                          m=self.TRI[:, :].unsqueeze(1).to_broadcast([128, 4, 128]): e.tensor_tensor(o, a, m, ALU.mult),
                          reads=[b.res, self.S5Cres], writes=[self.r_mint])
        gb = {}
        for ri in range(2):
            for half in range(2):
                b = self.newbank()
                gb[(ri, half)] = b
                for q in range(4):
                    g = half * 4 + q
                    self.mm(b, b.t[:, q * 128:(q + 1) * 128], self.MGF[:, g, ri, :], self.U[:, g, :], [self.r_mgf, self.r_u])
                    self.mm(b, b.t[:, q * 128:(q + 1) * 128], self.MGB[:, g, ri, :], self.UR[:, g, :], [self.r_mgb, self.r_ur])
        for ri, GG in ((0, self.GR), (1, self.GI)):
            for half in range(2):
                b = gb[(ri, half)]
                self.cp('act', GG[:, half * 4:half * 4 + 4, :], b.t[:, :].rearrange("p (a b) -> p a b", a=4),
                        [b.res], [self.r_g])
            g0 = sm('g0r' if ri == 0 else 'g0i')[:, gs].unsqueeze(2)
            self.s.op('dve', lambda e, o=GG[:, :, 0:1], a=GG[:, :, 0:1], b_=g0: e.tensor_tensor(o, a, b_, ALU.add),
                      reads=[self.S5S, self.r_g], writes=[self.r_g])
        self.s5_scan(sl, j)
        if j + 1 < NF:
            self.s5_block_tables(sl, j + 1)
            self.s5_block_expand_in(sl, j + 1)
        YF = self.T1.rearrange("p (t f) -> p t f", t=8)
        YB = self.T2.rearrange("p (t f) -> p t f", t=8)
        for dr, (Uc, ures, Y, yres) in enumerate(((self.U, self.r_u, YF, self.r_t1), (self.UR, self.r_ur, YB, self.r_t2))):
            rows = slice(dr * 64, dr * 64 + 64)
            for half in range(2):
                b = self.newbank()
                for q in range(4):
                    g = half * 4 + q
                    cs = slice(q * 128, (q + 1) * 128)
                    self.mm(b, b.t[:, cs], Uc[:, g, :], self.MINT[:, g, dr, :], [ures, self.r_mint])
                    for ri in range(2):
                        self.mm(b, b.t[:, cs], self.HPB[rows, ri, g, :], self.AOUT[rows, g, ri, 16:144],
                                [self.r_hpb, self.r_aout])
                src = b.t[:, :].rearrange("p (g t k) -> p g t k", g=4, t=8)
                dst = Y[:, :, half * 64:half * 64 + 64].rearrange("p t (g k) -> p g t k", g=4)
                self.cp('act', dst, src, [b.res], [yres])
        yfull = self.TMP[j % 2]
        yres = self.TMPres[j % 2]
        for half in range(2):
            b = self.newbank()
            for q in range(4):
                t_ = half * 4 + q
                cs = slice(q * 128, (q + 1) * 128)
                self.mm(b, b.t[:, cs], YF[:, t_, :], self.IDF[:, :], [self.r_t1, self.IDFres])
                self.mm(b, b.t[:, cs], YB[:, 7 - t_, :], self.ANTIF[:, :], [self.r_t2, self.S5Cres])
            dst = yfull[:, :].rearrange("p (c t) -> p t c", t=8)[:, half * 4:half * 4 + 4, :]
            usrc = self.HT[:, j, :].rearrange("p (c t) -> p t c", t=8)[:, half * 4:half * 4 + 4, :]
            self.s.op('dve', lambda e, o=dst, u=usrc, d=self.SPt[:, self.cm.m[f's5d_{sl}'][0] + j:self.cm.m[f's5d_{sl}'][0] + j + 1],
                      y=b.t[:, :].rearrange("p (t c) -> p t c", t=4): e.scalar_tensor_tensor(o, u, d, y, ALU.mult, ALU.add),
                      reads=[b.res, self.HTres[j], self.SPres], writes=[yres])
        self.act(self.HT[:, j, :], yfull[:, :], AF.Gelu_apprx_tanh, [yres], [self.HTres[j]])

    def s5_scan(self, sl, j):
        sm = self.sm
        gs = slice(8 * j, 8 * j + 8)
        GR, GI, COS, SIN = self.GR, self.GI, self.COS, self.SIN
        sh = [128, 8, 128]
        PTMP = self.HPB.rearrange("p r g c -> p (r g c)").bitcast(F32)
        self.s.op('pool', lambda e: e.memset(COS[:, :, 0:1], 1.0), reads=[self.r_cs], writes=[self.r_cs])
        self.s.op('pool', lambda e: e.memset(SIN[:, :, 0:1], 0.0), reads=[self.r_cs], writes=[self.r_cs])
        for k in range(7):
            d = 1 << k
            er = sm(f'E{k}r')[:, gs].unsqueeze(2).to_broadcast([128, 8, d])
            ei = sm(f'E{k}i')[:, gs].unsqueeze(2).to_broadcast([128, 8, d])
            t1 = PTMP[:, 0:8 * d].rearrange("p (g c) -> p g c", g=8)
            t2 = PTMP[:, 512:512 + 8 * d].rearrange("p (g c) -> p g c", g=8)
            rd = [self.S5S, self.r_cs, self.r_hpb]
            wr = [self.r_hpb, self.r_cs]
            o = self.s.op
            o('pool', lambda e, a=t1, x=COS[:, :, 0:d], y=er: e.tensor_tensor(a, x, y, ALU.mult), reads=rd, writes=wr)
            o('pool', lambda e, a=t2, x=SIN[:, :, 0:d], y=ei: e.tensor_tensor(a, x, y, ALU.mult), reads=rd, writes=wr)
            o('pool', lambda e, a=COS[:, :, d:2 * d], x=t1, y=t2: e.tensor_tensor(a, x, y, ALU.subtract), reads=rd, writes=wr)
            o('pool', lambda e, a=t1, x=COS[:, :, 0:d], y=ei: e.tensor_tensor(a, x, y, ALU.mult), reads=rd, writes=wr)
            o('pool', lambda e, a=t2, x=SIN[:, :, 0:d], y=er: e.tensor_tensor(a, x, y, ALU.mult), reads=rd, writes=wr)
            o('pool', lambda e, a=SIN[:, :, d:2 * d], x=t1, y=t2: e.tensor_tensor(a, x, y, ALU.add), reads=rd, writes=wr)
        T1 = self.T1[:, 0:1024].rearrange("p (g c) -> p g c", g=8)
        T2 = self.T2[:, 0:1024].rearrange("p (g c) -> p g c", g=8)
        T3 = self.T3.rearrange("p (g c) -> p g c", g=8)
        AM = self.AMt.rearrange("p (g c) -> p g c", g=8)
        rd = [self.S5S, self.r_cs, self.r_t1, self.r_t2, self.r_g, self.S5Cres, self.r_ein, self.r_gen7]
        wr = [self.r_t1, self.r_t2, self.r_g, self.r_ein, self.r_gen7]
        o = self.s.op
        o('pool', lambda e: e.tensor_tensor(AM, sm('rho8')[:, gs].unsqueeze(2).to_broadcast(sh),
                                            self.CARRY[:, :].unsqueeze(1).to_broadcast(sh), ALU.mult),
          reads=[self.S5S, self.S5Cres, self.r_ein], writes=[self.r_ein])
        o('dve', lambda e: e.tensor_tensor(T1, GR, COS, ALU.mult), reads=rd, writes=wr)
        o('dve', lambda e: e.tensor_tensor(T2, GI, SIN, ALU.mult), reads=rd, writes=wr)
        o('dve', lambda e: e.tensor_tensor(T1, T1, T2, ALU.add), reads=rd, writes=wr)
        o('dve', lambda e: e.tensor_tensor(T3, GI, COS, ALU.mult), reads=rd, writes=wr)
        o('dve', lambda e: e.tensor_tensor(T2, GR, SIN, ALU.mult), reads=rd, writes=wr)
        o('dve', lambda e: e.tensor_tensor(T3, T3, T2, ALU.subtract), reads=rd, writes=wr)
        fl = lambda a: a.rearrange("p g c -> p (g c)")
        o('dve', lambda e: e.tensor_tensor_scan(fl(GR), fl(AM), fl(T1), 0.0, ALU.mult, ALU.add), reads=rd, writes=wr)
        o('dve', lambda e: e.tensor_tensor_scan(fl(GI), fl(AM), fl(T3), 0.0, ALU.mult, ALU.add), reads=rd, writes=wr)
        o('dve', lambda e: e.tensor_tensor(T1, GR, COS, ALU.mult), reads=rd, writes=wr)
        o('dve', lambda e: e.tensor_tensor(T2, GI, SIN, ALU.mult), reads=rd, writes=wr)
        o('dve', lambda e: e.tensor_tensor(T1, T1, T2, ALU.subtract), reads=rd, writes=wr)
        o('dve', lambda e: e.tensor_tensor(T3, GI, COS, ALU.mult), reads=rd, writes=wr)
        o('dve', lambda e: e.tensor_tensor(T2, GR, SIN, ALU.mult), reads=rd, writes=wr)
        o('dve', lambda e: e.tensor_tensor(T3, T3, T2, ALU.add), reads=rd, writes=wr)
        cm_ = self.CARRY[:, 1:128].unsqueeze(1).to_broadcast([128, 8, 127])
        prd = [self.S5S, self.S5Cres, self.r_t1, self.r_gen7]
        for ri, Hf in ((0, T1), (1, T3)):
            o('pool', lambda e, a=self.HPB[:, ri, :, 1:128], x=Hf[:, :, 0:127], y=cm_: e.tensor_tensor(a, x, y, ALU.mult),
              reads=prd, writes=[self.r_hpb])
            o('pool', lambda e, a=self.HPB[:, ri, :, 0:1], x=self.H0[:, ri, gs].unsqueeze(2): e.tensor_copy(a, x),
              reads=[self.S5S], writes=[self.r_hpb])
            fv = self.FINALL[:, :, ri, gs].rearrange("p s g -> p g s")
            hv = Hf.rearrange("p g (s c) -> p g s c", s=4)[:, :, :, 31]
            o('pool', lambda e, a=fv, x=hv: e.tensor_copy(a, x), reads=prd, writes=[self.FINres])

    def s5_prefetch_params(self, sl):
        self.s5_layer_params(sl)
        self._s5_params_ready = sl

    def s5_layer(self, i):
        sl = i // 3
        if not getattr(self, '_s5_const_loaded', False):
            self._s5_const_loaded = True
            self.dma('sp', self.CARRY[:, :], self.d_carry, self.S5Cres, writes=[self.S5Cres])
            self.dma('sp', self.TRI[:, :], self.d_tri, self.S5Cres, writes=[self.S5Cres])
            self.dma('sp', self.ANTIF[:, :], self.d_antif, self.S5Cres, writes=[self.S5Cres])
        self.s.op('pool', lambda e: e.memset(self.MGF[:, :, :, 64:128], 0.0), writes=[self.r_mgf])
        self.s.op('pool', lambda e: e.memset(self.MGB[:, :, :, 0:64], 0.0), writes=[self.r_mgb])
        if getattr(self, '_s5_params_ready', None) != sl:
            self.s5_prefetch_params(sl)
        self._s5_params_ready = None
        for j in range(NF):
            self.s5_block(i, sl, j)
            self.mod_hook(1)
        for seg in range(4):
            for ri in range(2):
                b = self.newbank()
                self.mm(b, b.t[0:64, 0:128], self.FINALL[:, seg, ri, :], self.IDF[:, :], [self.FINres, self.IDFres])
                k = self.vf_i % 2
                self.vf_i += 1
                self.cp('act', self.VF[k][0:64, 0:128], b.t[0:64, 0:128], [b.res], [self.VFres[k]])
                for dr in range(2):
                    oseg = seg if dr == 0 else 3 - seg
                    self.dma('sp', self.o_s5[sl, oseg, dr, ri], self.VF[k][0:64, dr * 64:dr * 64 + 64], self.VFres[k],
                             reads=[self.VFres[k]], is_output=True)
        gb0 = self.cm.m[f'glub_{sl}'][0]
        for ch in range(4):
            w, wres = self.next_w(('glu', i, ch))
            for m2 in range(2):
                mt = ch * 2 + m2
                for h in range(2):
                    hs = slice(h * 512, (h + 1) * 512)
                    ba = self.newbank()
                    bg = self.newbank()
                    for kt in range(8):
                        self.mm(ba, ba.t[:, :], w[:, kt, m2 * 128:(m2 + 1) * 128], self.HT[:, kt, hs], [wres, self.HTres[kt]])
                    for kt in range(8):
                        self.mm(bg, bg.t[:, :], w[:, kt, 256 + m2 * 128:256 + (m2 + 1) * 128], self.HT[:, kt, hs],
                                [wres, self.HTres[kt]])
                    k = self.rl_i % 2
                    self.rl_i += 1
                    self.act(self.RL[k][:, :], bg.t[:, :], AF.Sigmoid, [bg.res, self.SPres], [self.RLres[k]],
                             bias=self.SPt[:, gb0 + 8 + mt:gb0 + 9 + mt], scale=1.0)
                    self.stt('dve', self.RL[k][:, :], ba.t[:, :], self.SPt[:, gb0 + mt:gb0 + mt + 1], self.RL[k][:, :],
                             ALU.add, ALU.mult, [ba.res, self.SPres, self.RLres[k]], [self.RLres[k]])
                    self.stt('dve', self.X[:, mt, hs], self.RL[k][:, :], self.MODT[:, 16 + mt:17 + mt], self.X[:, mt, hs],
                             ALU.mult, ALU.add, [self.RLres[k], self.MODres, self.Xres[mt]], [self.Xres[mt]])
```

```python
import math
import numpy as np
import concourse.bass as bass
import concourse.mybir as mybir
from concourse.bass_utils import run_bass_kernel_spmd

F32 = mybir.dt.float32
BF16 = mybir.dt.bfloat16
AF = mybir.ActivationFunctionType
ALU = mybir.AluOpType

D = 1024
T = 1024
NF = 8
DFF = 4096
DEPTH = 4
EPS = 1e-6
NEG = -30000.0
EPOCH = 30000
ENGS = ('pe', 'act', 'dve', 'pool', 'sp')


class Res:
    __slots__ = ('name', 'w', 'r')

    def __init__(self, name=''):
        self.name = name
        self.w = None
        self.r = []


class DrySched:
    def __init__(self):
        self.count = {e: 0 for e in ENGS}
        self.nsem = 0

    def op(self, *a, **k):
        return None

    def dma(self, *a, **k):
        return None

    def barrier(self):
        pass

    def finish(self):
        pass

    def emit(self):
        pass


class Sched:
    def __init__(self, nc):
        self.nc = nc
        self.ops = {e: [] for e in ENGS}
        self.count = {e: 0 for e in ENGS}
        self.seen = {e: {} for e in ENGS}
        self.esem = {e: [] for e in ENGS}
        self.dsem = {}
        self.out_tokens = []
        self.nsem = 0
        self.pending = {e: [] for e in ENGS}

    def _esem(self, e, epoch):
        while len(self.esem[e]) <= epoch:
            self.esem[e].append(self.nc.alloc_semaphore(f"se_{e}_{len(self.esem[e])}"))
            self.nsem += 1
        return self.esem[e][epoch]

    def _collect(self, eng, reads, writes, extra=()):
        toks = list(extra)
        for r in reads:
            if r.w is not None:
                toks.append(r.w)
        for w in writes:
            if w.w is not None:
                toks.append(w.w)
            toks.extend(w.r)
        need = {}
        for t in toks:
            if t[0] == 'e':
                _, e2, idx = t
                if e2 == eng and eng == 'pe':
                    continue
                key = ('e', e2)
                val = idx
            else:
                key = ('d', t[1])
                val = t[2]
            if self.seen[eng].get(key, 0) >= val:
                continue
            if need.get(key, 0) < val:
                need[key] = val
        waits = []
        for key, val in need.items():
            self.seen[eng][key] = val
            if key[0] == 'e':
                ep = (val - 1) // EPOCH
                waits.append((self._esem(key[1], ep), (val - 1) % EPOCH + 1))
            else:
                waits.append((self.dsem[key[1]][0], val))
        return waits

    def _mark(self, tok, reads, writes):
        for r in reads:
            r.r.append(tok)
        for w in writes:
            w.w = tok
            w.r = []

    def op(self, eng, fn, reads=(), writes=(), extra=()):
        extra = list(extra) + self.pending[eng]
        self.pending[eng] = []
        waits = self._collect(eng, reads, writes, extra)
        self.count[eng] += 1
        idx = self.count[eng]
        sem = self._esem(eng, (idx - 1) // EPOCH)
        self.ops[eng].append((waits, fn, sem, 1))
        tok = ('e', eng, idx)
        self._mark(tok, reads, writes)
        return tok

    def dma(self, eng, fn, key, reads=(), writes=(), is_output=False):
        extra = self.pending[eng]
        self.pending[eng] = []
        waits = self._collect(eng, reads, writes, extra)
        kid = id(key)
        if kid not in self.dsem:
            self.dsem[kid] = [self.nc.alloc_semaphore(f"sd_{len(self.dsem)}"), 0]
            self.nsem += 1
        ent = self.dsem[kid]
        ent[1] += 16
        self.ops[eng].append((waits, fn, ent[0], 16))
        tok = ('d', kid, ent[1])
        self._mark(tok, reads, writes)
        if is_output:
            self.out_tokens.append(tok)
        return tok

    def barrier(self):
        toks = [('e', e, self.count[e]) for e in ENGS if self.count[e] > 0]
        for e in ENGS:
            self.pending[e] = self.pending[e] + toks

    def finish(self):
        toks = list(self.out_tokens)
        waits = self._collect('sp', (), (), toks)
        self.ops['sp'].append((waits, None, None, 0))

    def emit(self):
        nc = self.nc
        ops = self.ops

        def replay(eng_obj, lst):
            for waits, fn, sem, inc in lst:
                for s, v in waits:
                    eng_obj.wait_ge(s, v)
                if fn is None:
                    continue
                ins = fn(eng_obj)
                if sem is not None:
                    ins.then_inc(sem, inc)

        with nc.Block() as block:
            @block.tensor
            def _(e):
                replay(e, ops['pe'])

            @block.scalar
            def _(e):
                replay(e, ops['act'])

            @block.vector
            def _(e):
                replay(e, ops['dve'])

            @block.gpsimd
            def _(e):
                replay(e, ops['pool'])

            @block.sync
            def _(e):
                replay(e, ops['sp'])


class ColMap:
    def __init__(self):
        self.m = {}
        self.n = 0

    def add(self, name, w):
        self.m[name] = (self.n, w)
        self.n += w

    def sl(self, name):
        a, w = self.m[name]
        return slice(a, a + w)


def fm(v):
    v = np.asarray(v, np.float32)
    return np.ascontiguousarray(v.reshape(-1, 128).T)


def small_colmap():
    cm = ColMap()
    cm.add('cvec', 8)
    for i in range(DEPTH):
        cm.add(f'g1_{i}', 8)
        cm.add(f'g2_{i}', 8)
        cm.add(f'adab_{i}', 48)
    for s in range(2):
        cm.add(f's5d_{s}', 8)
        cm.add(f'glub_{s}', 16)
    for nm in ('naq', 'nak', 'gq', 'gk', 'ctxb', 'carry'):
        cm.add(nm, 1)
    return cm


CONST_BF = ColMap()
for _nm in ('ones1024', 'ones128', 'blk64', 'ones', 'ident', 'pswap', 'anti'):
    CONST_BF.add(_nm, 128)


def const_bf_array():
    a = np.zeros((128, CONST_BF.n), np.float32)
    a[:, CONST_BF.sl('ones1024')] = 1.0 / 1024
    a[:, CONST_BF.sl('ones128')] = 1.0 / 128
    b = np.zeros((128, 128), np.float32)
    b[:64, :64] = 1.0 / 64
    b[64:, 64:] = 1.0 / 64
    a[:, CONST_BF.sl('blk64')] = b
    a[:, CONST_BF.sl('ones')] = 1.0
    a[:, CONST_BF.sl('ident')] = np.eye(128, dtype=np.float32)
    a[:, CONST_BF.sl('pswap')] = np.roll(np.eye(128, dtype=np.float32), 64, axis=0)
    a[:, CONST_BF.sl('anti')] = np.eye(128, dtype=np.float32)[::-1]
    return a


class Bank:
    def __init__(self, t, i):
        self.t = t
        self.res = Res(f'bank{i}')
        self.fresh = True


class Builder:
    def __init__(self, layers=(0, 1, 2, 3), mixers=True, plan=None):
        self.layers = layers
        self.mixers = mixers
        self.plan = plan
        self.nc = bass.Bass("TRN2", target_bir_lowering=False)
        self.s = Sched(self.nc) if plan is not None else DrySched()
        self.rec_tags = []
        self.cm = small_colmap()
        self.dram = {}
        self.wq = []
        self.wq_issued = 0

    def din(self, name, shape, dt=F32):
        t = self.nc.dram_tensor(name, list(shape), dt, kind="ExternalInput").ap()
        self.dram[name] = t
        return t

    def dout(self, name, shape, dt=F32):
        t = self.nc.dram_tensor(name, list(shape), dt, kind="ExternalOutput").ap()
        self.dram[name] = t
        return t

    def sb(self, name, shape, dt):
        return self.nc.alloc_sbuf_tensor(name, list(shape), dt)

    def mm(self, bank, out, lhsT, rhs, reads):
        st = bank.fresh
        bank.fresh = False
        return self.s.op('pe', lambda e: e.matmul(out, lhsT, rhs, start=st, stop=True, skip_group_check=True),
                         reads=reads, writes=[bank.res])

    def newbank(self):
        rot = getattr(self, 'rot', None) or list(range(8))
        b = self.banks[rot[self.bank_i % len(rot)]]
        self.bank_i += 1
        b.fresh = True
        return b

    def act(self, out, in_, func, reads, writes, bias=0.0, scale=1.0):
        return self.s.op('act', lambda e: e.activation(out=out, in_=in_, func=func, bias=bias, scale=scale),
                         reads=reads, writes=writes)

    def tt(self, eng, out, a, b, op, reads, writes):
        return self.s.op(eng, lambda e: e.tensor_tensor(out, a, b, op), reads=reads, writes=writes)

    def ts(self, eng, out, a, s1, s2, op0, op1, reads, writes):
        return self.s.op(eng, lambda e: e.tensor_scalar(out, a, s1, s2, op0, op1), reads=reads, writes=writes)

    def stt(self, eng, out, a, sc, b, op0, op1, reads, writes):
        return self.s.op(eng, lambda e: e.scalar_tensor_tensor(out, a, sc, b, op0, op1), reads=reads, writes=writes)

    def rsqrt(self, out, in_, reads, writes):
        self.s.op('act', lambda e: e.activation(out=out, in_=in_, func=AF.Ln, bias=self.EPSC[:, 0:1], scale=1.0),
                  reads=list(reads) + [self.EPSres], writes=writes)
        self.s.op('act', lambda e: e.activation(out=out, in_=out, func=AF.Exp, bias=0.0, scale=-0.5),
                  reads=writes, writes=writes)

    def cp(self, eng, out, in_, reads, writes):
        if eng == 'act':
            return self.s.op('act', lambda e: e.copy(out, in_), reads=reads, writes=writes)
        return self.s.op(eng, lambda e: e.tensor_copy(out, in_), reads=reads, writes=writes)

    def dma(self, eng, out, in_, key, reads=(), writes=(), is_output=False):
        return self.s.dma(eng, lambda e: e.dma_start(out=out, in_=in_), key, reads=reads, writes=writes,
                          is_output=is_output)

    def wq_add(self, src_ap, shape):
        self.wq.append((src_ap, shape))
        return len(self.wq) - 1

    def wq_get(self, idx):
        while self.wq_issued < min(len(self.wq), idx + self.NSLOT):
            j = self.wq_issued
            slot = j % self.NSLOT
            src, shape = self.wq[j]
            n = shape[1] * shape[2]
            dst = self.wslot[slot][:, 0:n].rearrange("p (a b) -> p a b", a=shape[1])
            if isinstance(src, list):
                wpart = shape[2] // len(src)
                for sap, part in src:
                    self.dma('pool', dst[:, :, part * wpart:(part + 1) * wpart], sap, self.wres[slot],
                             writes=[self.wres[slot]])
            else:
                self.dma('pool', dst, src, self.wres[slot], writes=[self.wres[slot]])
            self.wq_issued += 1
        slot = idx % self.NSLOT
        src, shape = self.wq[idx]
        n = shape[1] * shape[2]
        return self.wslot[slot][:, 0:n].rearrange("p (a b) -> p a b", a=shape[1]), self.wres[slot]

    def build(self):
        nc = self.nc
        cm = self.cm
        s = self.s
        xT = self.din('xT', [D, T])
        smallp = self.din('smallp', [128, cm.n])
        cbf = self.din('cbf', [128, CONST_BF.n])
        identf = self.din('identf', [128, 128])
        ada_w = self.din('ada_w', [DEPTH, D, 6 * D])
        mlp_w1 = self.din('mlp_w1', [DEPTH, D, DFF])
        mlp_w2 = self.din('mlp_w2', [DEPTH, DFF, D])
        yT = self.dout('yT', [D, T])
        self.declare_mixer_dram()

        self.X = self.sb('X', [128, NF, T], F32)
        self.Xres = [Res(f'X{j}') for j in range(NF)]
        self.HT = self.sb('HT', [128, NF, T], BF16)
        self.HTres = [Res(f'HT{j}') for j in range(NF)]
        self.SPt = self.sb('SPt', [128, cm.n], F32)
        self.SPres = Res('SP')
        self.CB = self.sb('CB', [128, CONST_BF.n], BF16)
        self.CBres = Res('CB')
        self.IDF = self.sb('IDF', [128, 128], F32)
        self.NSLOT = 3
        self.wslot = [self.sb(f'wslot{k}', [128, 6144], BF16) for k in range(self.NSLOT)]
        self.wres = [Res(f'w{k}') for k in range(self.NSLOT)]
        self.TMP = [self.sb(f'TMP{k}', [128, T], F32) for k in range(2)]
        self.TMPres = [Res(f'TMP{k}') for k in range(2)]
        self.SQ = [self.sb(f'SQ{k}', [128, T], BF16) for k in range(2)]
        self.SQres = [Res(f'SQ{k}') for k in range(2)]
        self.RSTD = self.sb('RSTD', [128, T], F32)
        self.RSTDres = Res('RSTD')
        self.MODTS = [self.sb(f'MODT{k}', [128, 48], F32) for k in range(2)]
        self.MODress = [Res(f'MOD{k}') for k in range(2)]
        self.GSC = self.sb('GSC', [128, 8], F32)
        self.GSCres = Res('GSC')
        self.SIL = self.sb('SIL', [128, 8], BF16)
        self.SILres = Res('SIL')
        self.SCR = self.sb('SCR', [128, 32 * T], BF16)
        self.HID = self.SCR[:, :].rearrange("p (a b) -> p a b", a=32)
        self.HIDres = [Res(f'HID{k}') for k in range(32)]
        self.EPSC = self.sb('EPSC', [128, 1], F32)
        self.EPSres = Res('EPS')
        self.s.op('dve', lambda e: e.memset(self.EPSC[:, :], EPS), writes=[self.EPSres])
        self.RL = [self.sb(f'RL{k}', [128, 512], F32) for k in range(2)]
        self.RLres = [Res(f'RL{k}') for k in range(2)]
        self.rl_i = 0
        self.alloc_mixer_sbuf()
        self.banks = [Bank(nc.alloc_psum_tensor(f'ps{k}', [128, 512], F32), k) for k in range(8)]
        self.bank_i = 0

        self.ada_w, self.mlp_w1, self.mlp_w2 = ada_w, mlp_w1, mlp_w2
        self.wtags = list(self.plan) if self.plan is not None else None
        if self.wtags is not None:
            for tag in self.wtags:
                src, shape = self.wsrc(tag)
                self.wq_add(src, shape)
        self.wnext = 0

        self.dma('sp', self.SPt[:, :], smallp, self.SPres, writes=[self.SPres])
        self.dma('pool', self.CB[:, :], cbf, self.CBres, writes=[self.CBres])
        self.IDFres = Res('IDF')
        self.dma('sp', self.IDF[:, :], identf, self.IDFres, writes=[self.IDFres])
        for j in range(NF):
            self.dma('sp', self.X[:, j, :], xT[j * 128:(j + 1) * 128, :], self.Xres[j], writes=[self.Xres[j]])
        self.act(self.SIL[:, :], self.SPt[:, cm.sl('cvec')], AF.Silu, [self.SPres], [self.SILres])
        self.load_mixer_inputs()

        self.MODT = self.MODTS[0]
        self.MODres = self.MODress[0]
        self.mod_pending = []
        for ch in range(8):
            self.modulation_chunk(self.layers[0], ch, 0)
        if self.mixers and self.layers[0] % 3 == 0:
            self.s5_prefetch_params(self.layers[0] // 3)
        for li, i in enumerate(self.layers):
            self.MODT = self.MODTS[li % 2]
            self.MODres = self.MODress[li % 2]
            if li + 1 < len(self.layers):
                self.mod_pending = [(self.layers[li + 1], ch, (li + 1) % 2) for ch in range(8)]
            self.norm_mod(f'g1_{i}', 8, 0)
            if self.mixers:
                self.mixer(i)
            self.norm_mod(f'g2_{i}', 32 + 0, 24)
            self.mod_hook(8)
            if self.mixers and li + 1 < len(self.layers) and self.layers[li + 1] % 3 == 0:
                self.s5_prefetch_params(self.layers[li + 1] // 3)
            self.mlp(i)

        for j in range(NF):
            self.dma('sp', yT[j * 128:(j + 1) * 128, :], self.X[:, j, :], self.Xres[j], reads=[self.Xres[j]],
                     is_output=True)
        self.finalize_mixer_outputs()
        s.finish()
        s.emit()
        return nc

    def next_w(self, tag):
        if self.wtags is None:
            self.rec_tags.append(tag)
            _, shape = self.wsrc(tag)
            n = shape[1] * shape[2]
            return self.wslot[0][:, 0:n].rearrange("p (a b) -> p a b", a=shape[1]), self.wres[0]
        assert self.wtags[self.wnext] == tag, (self.wtags[self.wnext], tag)
        ap, res = self.wq_get(self.wnext)
        self.wnext += 1
        return ap, res

    def wsrc(self, tag):
        kind, i, ch = tag
        pk = lambda w: w.rearrange("(kt p) n -> p kt n", p=128)
        if kind == 'ada':
            return pk(self.ada_w[i])[:, :, ch * 768:(ch + 1) * 768], [128, 8, 768]
        if kind == 'w1':
            return pk(self.mlp_w1[i])[:, :, ch * 512:(ch + 1) * 512], [128, 8, 512]
        if kind == 'w2':
            return pk(self.mlp_w2[i])[:, :, ch * 128:(ch + 1) * 128], [128, 32, 128]
        return self.wsrc_mixer(tag)

    def modulation_chunk(self, i, ch, buf):
        b = self.newbank()
        w, wres = self.next_w(('ada', i, ch))
        for m6 in range(6):
            for kt in range(8):
                self.mm(b, b.t[:, m6:m6 + 1], w[:, kt, m6 * 128:(m6 + 1) * 128], self.SIL[:, kt:kt + 1],
                        [wres, self.SILres])
        a0 = self.cm.m[f'adab_{i}'][0] + ch * 6
        self.tt('dve', self.MODTS[buf][:, ch * 6:ch * 6 + 6], b.t[:, 0:6], self.SPt[:, a0:a0 + 6], ALU.add,
                [b.res, self.SPres], [self.MODress[buf]])

    def mod_hook(self, n=1):
        for _ in range(n):
            if self.mod_pending:
                self.modulation_chunk(*self.mod_pending.pop(0))

    def cb(self, name):
        return self.CB[:, CONST_BF.sl(name)]

    def norm_mod(self, gname, sc0, sh0):
        cm = self.cm
        self.stt('dve', self.GSC[:, :], self.MODT[:, sc0:sc0 + 8], 1.0, self.SPt[:, cm.sl(gname)], ALU.add, ALU.mult,
                 [self.MODres, self.SPres], [self.GSCres])
        bs = [self.newbank(), self.newbank()]
        for j in range(NF):
            k = j % 2
            self.act(self.SQ[k][:, :], self.X[:, j, :], AF.Square, [self.Xres[j]], [self.SQres[k]])
            for h in range(2):
                self.mm(bs[h], bs[h].t[:, :], self.cb('ones1024'), self.SQ[k][:, h * 512:(h + 1) * 512],
                        [self.SQres[k], self.CBres])
        for h in range(2):
            self.rsqrt(self.RSTD[:, h * 512:(h + 1) * 512], bs[h].t[:, :], [bs[h].res], [self.RSTDres])
        for j in range(NF):
            k = j % 2
            self.tt('dve', self.TMP[k][:, :], self.X[:, j, :], self.RSTD[:, :], ALU.mult,
                    [self.Xres[j], self.RSTDres], [self.TMPres[k]])
            self.act(self.HT[:, j, :], self.TMP[k][:, :], AF.Identity, [self.TMPres[k], self.GSCres, self.MODres],
                     [self.HTres[j]], bias=self.MODT[:, sh0 + j:sh0 + j + 1], scale=self.GSC[:, j:j + 1])

    def mlp(self, i):
        for ch in range(8):
            w, wres = self.next_w(('w1', i, ch))
            for m4 in range(4):
                mt = ch * 4 + m4
                for h in range(2):
                    b = self.newbank()
                    for kt in range(8):
                        self.mm(b, b.t[:, :], w[:, kt, m4 * 128:(m4 + 1) * 128], self.HT[:, kt, h * 512:(h + 1) * 512],
                                [wres, self.HTres[kt]])
                    k = self.rl_i % 2
                    self.rl_i += 1
                    self.act(self.RL[k][:, :], b.t[:, :], AF.Relu, [b.res], [self.RLres[k]])
                    self.tt('dve' if k == 0 else 'pool', self.HID[:, mt, h * 512:(h + 1) * 512], self.RL[k][:, :],
                            self.RL[k][:, :], ALU.mult, [self.RLres[k]], [self.HIDres[mt]])
        for ch in range(8):
            w, wres = self.next_w(('w2', i, ch))
            for m2 in range(1):
                mt = ch
                for h in range(2):
                    b = self.newbank()
                    for kt in range(32):
                        self.mm(b, b.t[:, :], w[:, kt, m2 * 128:(m2 + 1) * 128], self.HID[:, kt, h * 512:(h + 1) * 512],
                                [wres, self.HIDres[kt]])
                    self.stt('dve', self.X[:, mt, h * 512:(h + 1) * 512], b.t[:, :], self.MODT[:, 40 + mt:41 + mt],
                             self.X[:, mt, h * 512:(h + 1) * 512], ALU.mult, ALU.add,
                             [b.res, self.MODres, self.Xres[mt]], [self.Xres[mt]])

    def declare_mixer_dram(self):
        pass

    def alloc_mixer_sbuf(self):
        pass

    def plan_mixer_weights(self, i):
        pass

    def load_mixer_inputs(self):
        pass

    def mixer(self, i):
        pass

    def finalize_mixer_outputs(self):
        pass


def core_tokens(inp, core):
    if core < 4:
        return np.asarray(inp['x_sample'][core], np.float32), np.asarray(inp['c'][core], np.float32)
    b0 = 4 * (core - 4)
    return (np.asarray(inp['x_prompt'][b0:b0 + 4], np.float32).reshape(T, D),
            np.asarray(inp['c_ctx'], np.float32))


def small_array(inp, core, cm):
    a = np.zeros((128, cm.n), np.float32)
    _, cvec = core_tokens(inp, core)
    a[:, cm.sl('cvec')] = fm(cvec)
    for i in range(DEPTH):
        a[:, cm.sl(f'g1_{i}')] = fm(inp['norm_g'][i, 0])
        a[:, cm.sl(f'g2_{i}')] = fm(inp['norm_g'][i, 1])
        a[:, cm.sl(f'adab_{i}')] = fm(inp['ada_b'][i])
    for sl in range(2):
        a[:, cm.sl(f's5d_{sl}')] = fm(inp['s5_d'][sl])
        a[:, cm.sl(f'glub_{sl}')] = fm(inp['s5_glu_b'][sl])
    a[:, cm.sl('naq')] = np.tile(np.asarray(inp['na_q_norm'][0], np.float32), 2)[:, None]
    a[:, cm.sl('nak')] = np.tile(np.asarray(inp['na_k_norm'][0], np.float32), 2)[:, None]
    a[:, cm.sl('gq')] = np.asarray(inp['gqa_q_norm'][0], np.float32)[:, None]
    a[:, cm.sl('gk')] = np.asarray(inp['gqa_k_norm'][0], np.float32)[:, None]
    a[:, cm.sl('ctxb')] = 0.0 if core < 4 else NEG
    a[:, cm.sl('carry')] = 1.0 if core < 4 else 0.0
    return a


def common_inputs(inp, core, cm):
    x, _ = core_tokens(inp, core)
    m = {
        'xT': np.ascontiguousarray(x.T),
        'smallp': small_array(inp, core, cm),
        'cbf': const_bf_array(),
        'identf': np.eye(128, dtype=np.float32),
        'ada_w': np.asarray(inp['ada_w'], np.float32),
        'mlp_w1': np.asarray(inp['mlp_w1'], np.float32),
        'mlp_w2': np.asarray(inp['mlp_w2'], np.float32),
    }
    return m


NA_TILES = [(kt, hf) for kt in range(8) for hf in range(2)
            if not ((kt < 2 and hf == 1) or (kt >= 6 and hf == 0))]
NA_TILE_IDX = {t: n for n, t in enumerate(NA_TILES)}


class FullBuilder(Builder):
    def declare_mixer_dram(self):
        self.na_wqkv = self.din('na_w_qkv', [D, 3 * D])
        self.na_wo = self.din('na_w_o', [D, D])
        self.gqa_wqkv = self.din('gqa_w_qkv', [D, 1536])
        self.gqa_wo = self.din('gqa_w_o', [D, D])
        self.s5_gluw = self.din('s5_glu_w', [2, D, 2 * D])
        self.na_kcT = self.din('na_kcT', [D, 512])
        self.na_vc = self.din('na_vc', [512, D])
        self.gqa_kcT = self.din('gqa_kcT', [256, 512])
        self.gqa_vc = self.din('gqa_vc', [512, 256])
        self.na_bias = self.din('na_bias', [16, len(NA_TILES), 128, 512])
        self.gmask = self.din('gmask', [4, 2048])
        self.rope = self.din('rope', [2, 128, T])
        self.o_nak = self.dout('o_nak', [T, D])
        self.o_nav = self.dout('o_nav', [T, D])
        self.o_gk = self.dout('o_gk', [T, 256])
        self.o_gv = self.dout('o_gv', [T, 256])
        self.declare_s5_dram()

    def alloc_mixer_sbuf(self):
        self.PT = [self.sb(f'PT{k}', [128, 512], BF16) for k in range(3)]
        self.PTres = [Res(f'PT{k}') for k in range(3)]
        self.PT += [self.RL[0][:, :].bitcast(BF16)[:, 0:512], self.RL[1][:, :].bitcast(BF16)[:, 0:512]]
        self.PTres += [self.RLres[0], self.RLres[1]]
        self.BT = [self.sb(f'BT{k}', [128, 512], BF16) for k in range(3)]
        self.BTres = [Res(f'BT{k}') for k in range(3)]
        self.RC = self.sb('RC', [128, 512], F32)
        self.RCres = Res('RC')
        self.VF = [self.sb(f'VF{k}', [128, 1024], F32) for k in range(2)]
        self.VFres = [Res(f'VF{k}') for k in range(2)]
        self.SQt = [self.SQ[0][:, 0:512], self.SQ[0][:, 512:1024], self.SQ[1][:, 0:512], self.SQ[1][:, 512:1024]]
        self.SQtres = [Res(f'SQt{k}') for k in range(4)]
        self.RS = [self.RSTD[:, 0:512], self.RSTD[:, 512:1024]]
        self.RSres = [Res(f'RS{k}') for k in range(2)]
        self.SAres = [[Res(f'SAd{k}'), Res(f'SAp{k}')] for k in range(2)]
        self.pt_i = 0
        self.bt_i = 0
        self.sq_i = 0
        self.rs_i = 0
        self.vf_i = 0
        self.kf_i = 0
        S = self.SCR
        self.QT = S[:, 0:8192].rearrange("p (a b) -> p a b", a=8)
        self.QTres = [Res(f'QT{j}') for j in range(8)]
        self.KT_na = S[:, 8192:20480].rearrange("p (a b) -> p a b", a=8)
        self.V_na = S[:, 20480:32768].rearrange("p (a b) -> p a b", a=12)
        self.KT_g = S[:, 8192:11264].rearrange("p (a b) -> p a b", a=2)
        self.V_g = S[:, 11264:14336].rearrange("p (a b) -> p a b", a=12)
        self.CC = S[:, 14336:16384].bitcast(F32)
        self.SS = S[:, 16384:18432].bitcast(F32)
        self.GM = S[:, 18432:20480]
        self.KTres = [Res(f'KT{j}') for j in range(8)]
        self.KCres = Res('KC')
        self.Vres = [Res(f'V{j}') for j in range(12)]
        self.ROPEres = Res('ROPE')
        self.GMres = Res('GM')
        self.alloc_s5_sbuf()

    def wsrc_mixer(self, tag):
        kind, i, ch = tag
        pk = lambda w: w.rearrange("(kt p) n -> p kt n", p=128)
        if kind == 'naqkv':
            return pk(self.na_wqkv)[:, :, ch * 512:(ch + 1) * 512], [128, 8, 512]
        if kind == 'gqkv':
            return pk(self.gqa_wqkv)[:, :, ch * 512:(ch + 1) * 512], [128, 8, 512]
        if kind == 'wo':
            w = self.na_wo if i % 3 == 1 else self.gqa_wo
            return pk(w)[:, :, ch * 512:(ch + 1) * 512], [128, 8, 512]
        assert kind == 'glu'
        sl = i // 3
        srcs = [(pk(self.s5_gluw[sl])[:, :, half * 1024 + ch * 256:half * 1024 + (ch + 1) * 256], half)
                for half in range(2)]
        return srcs, [128, 8, 512]

    def load_mixer_inputs(self):
        pass

    def mixer(self, i):
        kind = i % 3
        self.s.barrier()
        if kind == 1:
            self.na_layer(i)
        elif kind == 2:
            self.gqa_layer(i)
        else:
            self.s5_layer(i)
        self.s.barrier()

    def proj_fm(self, w, wres, m4, h):
        b = self.newbank()
        for kt in range(8):
            self.mm(b, b.t[:, :], w[:, kt, m4 * 128:(m4 + 1) * 128], self.HT[:, kt, h * 512:(h + 1) * 512],
                    [wres, self.HTres[kt]])
        return b

    def qknorm(self, b, onesname, gain_col, out, out_res, out_reads=()):
        k = self.sq_i % 4
        self.sq_i += 1
        self.act(self.SQt[k], b.t[:, :], AF.Square, [b.res], [self.SQtres[k]])
        bm = self.newbank()
        self.mm(bm, bm.t[:, :], self.cb(onesname), self.SQt[k], [self.SQtres[k], self.CBres])
        r = self.rs_i % 2
        self.rs_i += 1
        self.rsqrt(self.RS[r], bm.t[:, :], [bm.res], [self.RSres[r]])
        self.stt('dve', out, b.t[:, :], self.SPt[:, self.cm.sl(gain_col)], self.RS[r], ALU.mult, ALU.mult,
                 [b.res, self.RSres[r], self.SPres] + list(out_reads), [out_res])

    def qk_pipeline(self, items, onesname, stage3, cur=None):
        st = {}
        N = len(items)
        cur = cur or [None, None, None]
        for n in range(N + 2):
            if n < N:
                tag, m4, h, ctx = items[n]
                if cur[0] != tag:
                    w, wres = self.next_w(tag)
                    cur = [tag, w, wres]
                b = self.proj_fm(cur[1], cur[2], m4, h)
                k = self.sq_i % 4
                self.sq_i += 1
                self.act(self.SQt[k], b.t[:, :], AF.Square, [b.res], [self.SQtres[k]])
                st[n] = [b, k, None]
            if 0 <= n - 1 < N:
                b, k, _ = st[n - 1]
                bm = self.newbank()
                self.mm(bm, bm.t[:, :], self.cb(onesname), self.SQt[k], [self.SQtres[k], self.CBres])
                r = self.rs_i % 2
                self.rs_i += 1
                self.rsqrt(self.RS[r], bm.t[:, :], [bm.res], [self.RSres[r]])
                st[n - 1][2] = r
            if 0 <= n - 2 < N:
                b, k, r = st.pop(n - 2)
                stage3(items[n - 2][3], b, r)
        return cur

    def emit_k_out(self, KF, KFres, odram, j):
        k = self.vf_i % 2
        self.vf_i += 1
        for half in range(2):
            b = self.newbank()
            for t4 in range(4):
                tt_ = half * 4 + t4
                self.mm(b, b.t[:, t4 * 128:(t4 + 1) * 128], KF[:, tt_ * 128:(tt_ + 1) * 128], self.IDF[:, :],
                        [KFres, self.IDFres])
            self.cp('act', self.VF[k][:, half * 512:(half + 1) * 512], b.t[:, :], [b.res], [self.VFres[k]])
        dst = odram.rearrange("(tt p) f -> p tt f", p=128)[:, :, j * 128:(j + 1) * 128]
        self.dma('sp', dst, self.VF[k][:, :].rearrange("p (a b) -> p a b", a=8), self.VFres[k],
                 reads=[self.VFres[k]], is_output=True)

    def v_proj(self, w, wres, c0, ncols, Vt, vcol0, odram):
        for tt_ in range(8):
            b = self.newbank()
            for kt in range(8):
                self.mm(b, b.t[:, 0:ncols], self.HT[:, kt, tt_ * 128:(tt_ + 1) * 128], w[:, kt, c0:c0 + ncols],
                        [wres, self.HTres[kt]])
            k = self.vf_i % 2
            self.vf_i += 1
            self.cp('act', self.VF[k][:, 0:ncols], b.t[:, 0:ncols], [b.res], [self.VFres[k]])
            self.cp('pool', Vt[:, tt_, vcol0:vcol0 + ncols], self.VF[k][:, 0:ncols], [self.VFres[k]], [self.Vres[tt_]])
            self.dma('sp', odram[tt_ * 128:(tt_ + 1) * 128, vcol0:vcol0 + ncols], self.VF[k][:, 0:ncols], self.VFres[k],
                     reads=[self.VFres[k]], is_output=True)

    def wo_proj(self, i, src):
        for ch in range(2):
            w, wres = self.next_w(('wo', i, ch))
            for m4 in range(4):
                mt = ch * 4 + m4
                for h in range(2):
                    b = self.newbank()
                    for kt in range(8):
                        self.mm(b, b.t[:, :], w[:, kt, m4 * 128:(m4 + 1) * 128], src[:, kt, h * 512:(h + 1) * 512],
                                [wres, self.HTres[kt]])
                    self.stt('dve', self.X[:, mt, h * 512:(h + 1) * 512], b.t[:, :], self.MODT[:, 16 + mt:17 + mt],
                             self.X[:, mt, h * 512:(h + 1) * 512], ALU.mult, ALU.add,
                             [b.res, self.MODres, self.Xres[mt]], [self.Xres[mt]])

    def attention(self, kind):
        na = kind == 'na'
        nheads = 16 if na else 8
        dh = 64 if na else 128
        scale = dh ** -0.5
        KT = self.KT_na if na else self.KT_g
        V = self.V_na if na else self.V_g
        self.rot = [0, 1, 2, 3]
        self.bank_i = 0
        steps = []
        for hd in range(nheads):
            for hf in range(2):
                tiles = [kt for kt in range(12) if (not na) or kt >= 8 or (kt, hf) in NA_TILE_IDX]
                for n, kt in enumerate(tiles):
                    steps.append((hd, hf, kt, n == 0, n == len(tiles) - 1))
        bias_steps = [st for st in steps if na and st[2] < 8]
        bias_slot = {}
        self._bias_n = 0

        def issue_bias(upto):
            while self._bias_n < min(len(bias_steps), upto):
                hd, hf, kt = bias_steps[self._bias_n][:3]
                bi = self._bias_n % 3
                self.dma('pool', self.BT[bi][:, :], self.na_bias[hd, NA_TILE_IDX[(kt, hf)]], self.BTres[bi],
                         writes=[self.BTres[bi]])
                bias_slot[(hd, hf, kt)] = bi
                self._bias_n += 1

        LA = 4
        NPT = 5
        st_info = {}
        nbias = [0]
        acc = [0]

        def front(n):
            hd, hf, kt, first, last = steps[n]
            if na:
                ht, pr = hd // 2, slice(64 * (hd % 2), 64 * (hd % 2) + 64)
                ktile = ht
            else:
                ht, pr = hd, slice(0, 128)
                ktile = hd // 4
            qs = slice(hf * 512, hf * 512 + 512)
            bs_ = self.newbank()
            ks = slice(kt * 128, kt * 128 + 128)
            kres = self.KTres[ktile] if kt < 8 else self.KCres
            self.mm(bs_, bs_.t[:, :], KT[pr, ktile, ks], self.QT[pr, ht, qs], [kres, self.QTres[ht]])
            p = self.pt_i % NPT
            self.pt_i += 1
            if kt < 8:
                if na:
                    issue_bias(nbias[0] + 3)
                    bi = bias_slot[(hd, hf, kt)]
                    nbias[0] += 1
                    self.mm(bs_, bs_.t[:, :], self.cb('ident'), self.BT[bi][:, :], [self.BTres[bi], self.CBres])
                else:
                    self.mm(bs_, bs_.t[:, :], self.GM[0:4, ks], self.GM[0:4, 1024 + hf * 512:1536 + hf * 512],
                            [self.GMres])
                self.act(self.PT[p][:, :], bs_.t[:, :], AF.Exp, [bs_.res], [self.PTres[p]], bias=0.0, scale=scale)
            else:
                self.act(self.PT[p][:, :], bs_.t[:, :], AF.Exp, [bs_.res, self.SPres], [self.PTres[p]],
                         bias=self.SPt[:, self.cm.sl('ctxb')], scale=scale)
            st_info[n] = p

        def back(n):
            hd, hf, kt, first, last = steps[n]
            if na:
                ht, pr = hd // 2, slice(64 * (hd % 2), 64 * (hd % 2) + 64)
                vc = slice(ht * 128, ht * 128 + 128)
            else:
                ht, pr = hd, slice(0, 128)
                vc = slice((hd // 4) * 128, (hd // 4) * 128 + 128)
            qs = slice(hf * 512, hf * 512 + 512)
            if first:
                acc[0] += 1
                self._bo = self.banks[4 + 2 * (acc[0] % 2)]
                self._bsum = self.banks[5 + 2 * (acc[0] % 2)]
                self._bo.fresh = True
                self._bsum.fresh = True
            bo, bsum = self._bo, self._bsum
            p = st_info.pop(n)
            self.mm(bo, bo.t[:, :], V[:, kt, vc], self.PT[p][:, :], [self.Vres[kt], self.PTres[p]])
            self.mm(bsum, bsum.t[:, :], self.cb('ones'), self.PT[p][:, :], [self.PTres[p], self.CBres])
            if last:
                self.s.op('dve', lambda e, o=self.RC[pr, :], a=bsum.t[pr, :]: e.reciprocal(o, a),
                          reads=[bsum.res], writes=[self.RCres])
                self.tt('dve', self.HT[pr, ht, qs], bo.t[pr, :], self.RC[pr, :], ALU.mult,
                        [bo.res, self.RCres], [self.HTres[ht]])

        hook_every = max(1, len(steps) // 9)
        for n in range(len(steps) + LA):
            if n < len(steps):
                if n % hook_every == hook_every - 1:
                    self.mod_hook(1)
                front(n)
            if n - LA >= 0:
                back(n - LA)
        self.rot = list(range(8))

    def na_layer(self, i):
        for j in range(8):
            self.dma('pool', self.KT_na[:, j, 1024:1536], self.na_kcT[j * 128:(j + 1) * 128, :], self.KCres,
                     writes=[self.KCres])
        for t in range(4):
            self.dma('pool', self.V_na[:, 8 + t, :], self.na_vc[t * 128:(t + 1) * 128, :], self.Vres[8 + t],
                     writes=[self.Vres[8 + t]])
        gq_col = self.SPt[:, self.cm.sl('naq')]
        gk_col = self.SPt[:, self.cm.sl('nak')]

        def s3_q(ctx, b, r):
            j, h = ctx
            self.stt('dve', self.QT[:, j, h * 512:(h + 1) * 512], b.t[:, :], gq_col, self.RS[r], ALU.mult, ALU.mult,
                     [b.res, self.RSres[r], self.SPres], [self.QTres[j]])

        def s3_k(ctx, b, r):
            j, h, kf = ctx
            self.stt('dve', self.TMP[kf][:, h * 512:(h + 1) * 512], b.t[:, :], gk_col, self.RS[r], ALU.mult, ALU.mult,
                     [b.res, self.RSres[r], self.SPres], [self.TMPres[kf]])
            if h == 1:
                self.cp('pool', self.KT_na[:, j, 0:1024], self.TMP[kf][:, :], [self.TMPres[kf]], [self.KTres[j]])
                self.emit_k_out(self.TMP[kf], self.TMPres[kf], self.o_nak, j)

        items = [(('naqkv', i, ch), m4, h, (ch * 4 + m4, h)) for ch in range(2) for m4 in range(4) for h in range(2)]
        self.qk_pipeline(items, 'blk64', s3_q)
        items = []
        for ch in range(2, 4):
            for m4 in range(4):
                kf = self.kf_i % 2
                self.kf_i += 1
                for h in range(2):
                    items.append((('naqkv', i, ch), m4, h, ((ch - 2) * 4 + m4, h, kf)))
        self.qk_pipeline(items, 'blk64', s3_k)
        for ch in range(4, 6):
            w, wres = self.next_w(('naqkv', i, ch))
            self.v_proj(w, wres, 0, 512, self.V_na, (ch - 4) * 512, self.o_nav)
        self.attention('na')
        self.wo_proj(i, self.HT)

    def rope_norm(self, b, r, gaincol, h, out, outres, kf=None, kfres=None):
        hs = slice(h * 512, (h + 1) * 512)
        QN, QNres = self.RL[0], self.RLres[0]
        T1, T1res = self.RL[1], self.RLres[1]
        T2, T2res = self.RC, self.RCres
        QNb, QNbres = self.PT[0], self.PTres[0]
        self.stt('dve', QN[:, :], b.t[:, :], self.SPt[:, self.cm.sl(gaincol)], self.RS[r], ALU.mult, ALU.mult,
                 [b.res, self.RSres[r], self.SPres], [QNres])
        self.cp('act', QNb[:, :], QN[:, :], [QNres], [QNbres])
        bsw = self.newbank()
        self.mm(bsw, bsw.t[:, :], self.cb('pswap'), QNb[:, :], [QNbres, self.CBres])
        self.tt('pool', T1[:, :], QN[:, :], self.CC[:, hs], ALU.mult, [QNres, self.ROPEres], [T1res])
        self.tt('dve', T2[:, :], bsw.t[:, :], self.SS[:, hs], ALU.mult, [bsw.res, self.ROPEres], [T2res])
        if kf is None:
            self.tt('dve', out, T1[:, :], T2[:, :], ALU.add, [T1res, T2res], [outres])
        else:
            self.tt('dve', kf, T1[:, :], T2[:, :], ALU.add, [T1res, T2res], [kfres])
            self.cp('act', out, kf, [kfres], [outres])

    def gqa_layer(self, i):
        self.dma('sp', self.CC[:, :], self.rope[0], self.ROPEres, writes=[self.ROPEres])
        self.dma('sp', self.SS[:, :], self.rope[1], self.ROPEres, writes=[self.ROPEres])
        self.dma('pool', self.GM[0:4, :], self.gmask, self.GMres, writes=[self.GMres])
        for kv in range(2):
            self.dma('pool', self.KT_g[:, kv, 1024:1536], self.gqa_kcT[kv * 128:(kv + 1) * 128, :], self.KCres,
                     writes=[self.KCres])
        for t in range(4):
            self.dma('pool', self.V_g[:, 8 + t, :], self.gqa_vc[t * 128:(t + 1) * 128, :], self.Vres[8 + t],
                     writes=[self.Vres[8 + t]])
        def s3_q(ctx, b, r):
            j, h = ctx
            self.rope_norm(b, r, 'gq', h, self.QT[:, j, h * 512:(h + 1) * 512], self.QTres[j])

        def s3_k(ctx, b, r):
            kv, h, kf = ctx
            self.rope_norm(b, r, 'gk', h, self.KT_g[:, kv, h * 512:(h + 1) * 512], self.KTres[kv],
                           kf=self.TMP[kf][:, h * 512:(h + 1) * 512], kfres=self.TMPres[kf])
            if h == 1:
                self.emit_k_out(self.TMP[kf], self.TMPres[kf], self.o_gk, kv)

        items = [(('gqkv', i, ch), m4, h, (ch * 4 + m4, h)) for ch in range(2) for m4 in range(4) for h in range(2)]
        cur = None
        for it in items:
            cur = self.qk_pipeline([it], 'ones128', s3_q, cur)
        items = []
        for kv in range(2):
            kf = self.kf_i % 2
            self.kf_i += 1
            for h in range(2):
                items.append((('gqkv', i, 2), kv, h, (kv, h, kf)))
        for it in items:
            cur = self.qk_pipeline([it], 'ones128', s3_k, cur)
        _, w, wres = cur
        self.v_proj(w, wres, 256, 256, self.V_g, 0, self.o_gv)
        self.attention('gqa')
        self.wo_proj(i, self.HT)

    def declare_s5_dram(self):
        pass

    def alloc_s5_sbuf(self):
        pass

    def plan_s5_weights(self, i):
        pass

    def s5_layer(self, i):
        pass


_TABLE_CACHE = {}


def na_bias_table(rpb, sample):
    if sample:
        q = np.arange(T)
        k = np.arange(T)
        qr, qc = (q // 64)[:, None], (q % 64)[:, None]
        kr, kc = (k // 64)[None, :], (k % 64)[None, :]
        rs = np.clip(qr - 4, 0, 8)
        cs = np.clip(qc - 8, 0, 48)
        ok = (kr >= rs) & (kr < rs + 8) & (kc >= cs) & (kc < cs + 16)
        drow = np.clip(kr - qr + 7, 0, 14)
        dc = np.clip(kc - qc + 15, 0, 30)
        full = np.where(ok[None], np.asarray(rpb, np.float32)[:, drow, dc], np.float32(NEG))
    else:
        q = np.arange(T)
        same = (q[:, None] // 256) == (q[None, :] // 256)
        full = np.broadcast_to(np.where(same, np.float32(0.0), np.float32(NEG))[None], (16, T, T))
    out = np.empty((16, len(NA_TILES), 128, 512), np.float32)
    for n, (kt, hf) in enumerate(NA_TILES):
        out[:, n] = np.transpose(full[:, hf * 512:(hf + 1) * 512, kt * 128:(kt + 1) * 128], (0, 2, 1))
    return out


def rope_tables(sample):
    if not sample:
        return np.stack([np.ones((128, T), np.float32), np.zeros((128, T), np.float32)])
    t = np.arange(T)
    row = (t // 64).astype(np.float32)
    col = (t % 64).astype(np.float32)
    half = 64
    inv = (np.float32(10000.0) ** (-np.arange(0, half, 2, dtype=np.float32) / np.float32(half))).astype(np.float32)
    ang = np.concatenate([row[:, None] * inv, col[:, None] * inv], axis=-1).astype(np.float32)
    c, s_ = np.cos(ang).astype(np.float32).T, np.sin(ang).astype(np.float32).T
    return np.stack([np.concatenate([c, c], 0), np.concatenate([-s_, s_], 0)])


def gmask_table(sample):
    g = np.zeros((4, 2048), np.float32)
    k = np.arange(T)
    for j in range(4):
        g[j, :T] = (k // 256 == j)
        if not sample:
            g[j, T:] = np.where(k // 256 == j, 0.0, NEG)
    return g


def mixer_inputs(inp, core):
    sample = core < 4
    m = {
        'na_w_qkv': np.asarray(inp['na_w_qkv'][0], np.float32),
        'na_w_o': np.asarray(inp['na_w_o'][0], np.float32),
        'gqa_w_qkv': np.asarray(inp['gqa_w_qkv'][0], np.float32),
        'gqa_w_o': np.asarray(inp['gqa_w_o'][0], np.float32),
        's5_glu_w': np.asarray(inp['s5_glu_w'], np.float32),
    }
    if sample:
        m['na_kcT'] = np.ascontiguousarray(np.asarray(inp['cache_na_k'][core, 0], np.float32).reshape(512, D).T)
        m['na_vc'] = np.ascontiguousarray(np.asarray(inp['cache_na_v'][core, 0], np.float32).reshape(512, D))
        m['gqa_kcT'] = np.ascontiguousarray(np.asarray(inp['cache_gqa_k'][core, 0], np.float32).reshape(512, 256).T)
        m['gqa_vc'] = np.ascontiguousarray(np.asarray(inp['cache_gqa_v'][core, 0], np.float32).reshape(512, 256))
    else:
        m['na_kcT'] = np.zeros((D, 512), np.float32)
        m['na_vc'] = np.zeros((512, D), np.float32)
        m['gqa_kcT'] = np.zeros((256, 512), np.float32)
        m['gqa_vc'] = np.zeros((512, 256), np.float32)
    key = ('nab', sample)
    if key not in _TABLE_CACHE:
        _TABLE_CACHE[key] = na_bias_table(inp['na_rpb'][0], sample)
        _TABLE_CACHE[('rope', sample)] = rope_tables(sample)
        _TABLE_CACHE[('gm', sample)] = gmask_table(sample)
    m['na_bias'] = _TABLE_CACHE[key]
    m['rope'] = _TABLE_CACHE[('rope', sample)]
    m['gmask'] = _TABLE_CACHE[('gm', sample)]
    return m


def make_builder(layers, mixers=True):
    dry = S5Builder(layers=layers, mixers=mixers, plan=None)
    dry.build()
    b = S5Builder(layers=layers, mixers=mixers, plan=dry.rec_tags)
    nc = b.build()
    return b, nc


def run(inp, layers=(0, 1, 2, 3), cores=None):
    b, nc = make_builder(layers)
    maps = []
    if cores is not None:
        for core in cores:
            m = common_inputs(inp, core, b.cm)
            m.update(mixer_inputs(inp, core))
            m.update(s5_inputs(inp, core))
            maps.append({k: v for k, v in m.items() if k in b.dram})
        res = run_bass_kernel_spmd(nc, maps, core_ids=list(range(len(cores))))
        return res.results
    for core in range(8):
        m = common_inputs(inp, core, b.cm)
        m.update(mixer_inputs(inp, core))
        m.update(s5_inputs(inp, core))
        maps.append({k: v for k, v in m.items() if k in b.dram})
    res = run_bass_kernel_spmd(nc, maps, core_ids=list(range(8)))
    R = res.results
    y_sample = np.stack([R[c]['yT'].T for c in range(4)]).astype(np.float32)
    y_prompt = np.concatenate([R[c]['yT'].T.reshape(4, 256, D) for c in range(4, 8)]).astype(np.float32)
    nak = np.concatenate([R[c]['o_nak'].reshape(4, 256, 16, 64) for c in range(4, 8)])[:, None]
    nav = np.concatenate([R[c]['o_nav'].reshape(4, 256, 16, 64) for c in range(4, 8)])[:, None]
    gk = np.concatenate([R[c]['o_gk'].reshape(4, 256, 2, 128) for c in range(4, 8)])[:, None]
    gv = np.concatenate([R[c]['o_gv'].reshape(4, 256, 2, 128) for c in range(4, 8)])[:, None]
    s5 = s5_assemble(R)
    return (y_prompt, y_sample, s5, nak.astype(np.float32), nav.astype(np.float32), gk.astype(np.float32),
            gv.astype(np.float32))


def kernel(**inputs):
    return run(inputs)


S5_NSM = 48


def s5_host_arrays(inp, core):
    sample = core < 4
    lam = np.zeros((2, 3, 128, 64), np.float32)
    Bm = np.zeros((2, 128, 2, 64, 16), np.float32)
    Cm = np.zeros((2, 128, 2, 64, 16), np.float32)
    h0 = np.zeros((2, 128, 2, 64), np.float32)
    for sl in range(2):
        for dr in range(2):
            rows = slice(dr * 64, dr * 64 + 64)
            lam[sl, 0, rows] = np.asarray(inp['s5_lam_re'][sl, dr], np.float32).T
            lam[sl, 1, rows] = np.asarray(inp['s5_lam_im'][sl, dr], np.float32).T
            lam[sl, 2, rows] = np.asarray(inp['s5_log_dt'][sl, dr], np.float32)[None, :]
            Bm[sl, rows, 0] = np.transpose(np.asarray(inp['s5_b_re'][sl, dr], np.float32), (1, 0, 2))
            Bm[sl, rows, 1] = np.transpose(np.asarray(inp['s5_b_im'][sl, dr], np.float32), (1, 0, 2))
            Cm[sl, rows, 0] = np.transpose(np.asarray(inp['s5_c_re'][sl, dr], np.float32), (2, 0, 1))
            Cm[sl, rows, 1] = np.transpose(np.asarray(inp['s5_c_im'][sl, dr], np.float32), (2, 0, 1))
            if sample:
                for ri in range(2):
                    h0[sl, rows, ri] = np.asarray(inp['state_s5'][core, sl, dr, ri], np.float32).T
    carry = np.ones((128, 128), np.float32)
    carry[:, 0] = 0.0
    carry[:, [32, 64, 96]] = 1.0 if sample else 0.0
    sg = np.arange(128) // 16
    tri = (sg[None, :] >= sg[:, None]).astype(np.float32)
    return {'s5_lam': lam, 's5_B': Bm, 's5_C': Cm, 's5_h0': h0, 's5_carry': carry, 's5_tri': tri,
            's5_antif': np.ascontiguousarray(np.eye(128, dtype=np.float32)[::-1])}


def s5_inputs(inp, core):
    return s5_host_arrays(inp, core)


def s5_core_states(res):
    return np.transpose(res['o_s5'], (1, 0, 2, 3, 4, 5))


def s5_assemble(R):
    return np.concatenate([s5_core_states(R[c]) for c in range(4, 8)]).astype(np.float32)


class S5Builder(FullBuilder):
    def declare_s5_dram(self):
        self.d_lam = self.din('s5_lam', [2, 3, 128, 64])
        self.d_B = self.din('s5_B', [2, 128, 2, 64, 16])
        self.d_C = self.din('s5_C', [2, 128, 2, 64, 16])
        self.d_h0 = self.din('s5_h0', [2, 128, 2, 64])
        self.d_carry = self.din('s5_carry', [128, 128])
        self.d_tri = self.din('s5_tri', [128, 128])
        self.d_antif = self.din('s5_antif', [128, 128])
        self.o_s5 = self.dout('o_s5', [2, 4, 2, 2, 64, 64])

    def alloc_s5_sbuf(self):
        self.SML = self.sb('SML', [128, S5_NSM, 64], F32)
        self.sm_idx = {}
        self.S5S = Res('S5S')
        self.BLKres = Res('BLK')
        self.BLKQres = Res('BLKQ')
        self.LAM3 = self.sb('LAM3', [128, 3, 64], F32)
        self.H0 = self.sb('H0', [128, 2, 64], F32)
        self.CARRY = self.sb('CARRY', [128, 128], F32)
        self.TRI = self.sb('TRI', [128, 128], F32)
        self.ANTIF = self.sb('ANTIF', [128, 128], F32)
        self.S5Cres = Res('S5C')
        self.PB = self.sb('PB', [128, 2, 8, 16], F32)
        self.PC = self.sb('PC', [128, 2, 8, 16], F32)
        self.PBres = Res('PB')
        self.BLK = self.sb('BLK', [128, 11, 72], F32)
        self.FINALL = self.RC[:, :].rearrange("p (s r g) -> p s r g", s=4, r=2)
        self.FINres = self.RCres
        S = self.SCR
        o = [0]

        def carve(n, dt=BF16):
            a = S[:, o[0]:o[0] + n]
            o[0] += n
            return a if dt == BF16 else a.bitcast(F32)
        self.HTOK = carve(1024).rearrange("p (a b) -> p a b", a=8)
        self.HTOKR = carve(1024).rearrange("p (a b) -> p a b", a=8)
        self.U = carve(1024).rearrange("p (a b) -> p a b", a=8)
        self.UR = carve(1024).rearrange("p (a b) -> p a b", a=8)
        self.EIN = carve(2048).rearrange("p (g r n) -> p g r n", g=8, r=2)
        self.AOUT = carve(2304).rearrange("p (g r n) -> p g r n", g=8, r=2)
        self.GEN7 = carve(2048).rearrange("p (g r n) -> p g r n", g=8, r=2)
        self.MGF = carve(2048).rearrange("p (g r n) -> p g r n", g=8, r=2)
        self.MGB = carve(2048).rearrange("p (g r n) -> p g r n", g=8, r=2)
        self.MINT = carve(2048).rearrange("p (g r n) -> p g r n", g=8, r=2)
        self.HPB = carve(2048).rearrange("p (r g c) -> p r g c", r=2, g=8)
        self.GR = carve(2048, F32).rearrange("p (g c) -> p g c", g=8)
        self.GI = carve(2048, F32).rearrange("p (g c) -> p g c", g=8)
        self.COS = carve(2048, F32).rearrange("p (g c) -> p g c", g=8)
        self.SIN = carve(2048, F32).rearrange("p (g c) -> p g c", g=8)
        self.TBG = S[:, o[0] - 8192:o[0] - 4096].bitcast(F32)
        self.TB2 = S[:, o[0]:o[0] + 4096].bitcast(F32)
        self.T1 = carve(2048, F32)
        self.T2 = carve(2048, F32)
        self.T3 = S[:, 8448:10496].bitcast(F32)
        self.AMt = S[:, 4096:6144].bitcast(F32)
        assert o[0] <= 32768, o[0]
        self.r_htok, self.r_htokr, self.r_u, self.r_ur = Res('htok'), Res('htokr'), Res('u'), Res('ur')
        self.r_ein, self.r_aout, self.r_gen7 = Res('ein'), Res('aout'), Res('gen7')
        self.r_mgf, self.r_mgb, self.r_mint, self.r_hpb = Res('mgf'), Res('mgb'), Res('mint'), Res('hpb')
        self.r_g, self.r_cs, self.r_t1, self.r_t2 = Res('g'), Res('cs'), Res('t1'), Res('t2')

    def sm(self, name):
        if name not in self.sm_idx:
            self.sm_idx[name] = len(self.sm_idx)
            assert len(self.sm_idx) <= S5_NSM - 2, name
        return self.SML[:, self.sm_idx[name], :]

    def _srw(self, extra):
        w = getattr(self, '_s_wres', None) or self.S5S
        return [self.S5S, w] + list(extra), [w]

    def s_tt(self, out, a, b, op, extra=()):
        r, w = self._srw(extra)
        self.s.op(getattr(self, '_s_eng', 'dve'), lambda e: e.tensor_tensor(out, a, b, op), reads=r, writes=w)

    def s_ts(self, out, a, s1, s2, op0, op1=None, extra=()):
        r, w = self._srw(extra)
        if op1 is None:
            self.s.op('dve', lambda e: e.tensor_scalar(out, a, s1, None, op0), reads=r, writes=w)
        else:
            self.s.op('dve', lambda e: e.tensor_scalar(out, a, s1, s2, op0, op1), reads=r, writes=w)

    def s_stt(self, out, a, sc, b, op0, op1, extra=()):
        r, w = self._srw(extra)
        self.s.op('dve', lambda e: e.scalar_tensor_tensor(out, a, sc, b, op0, op1), reads=r, writes=w)

    def s_cmul(self, outr, outi, ar, ai, br, bi, t1, t2, extra=()):
        self.s_tt(t1, ar, br, ALU.mult, extra)
        self.s_tt(t2, ai, bi, ALU.mult, extra)
        self.s_tt(outr, t1, t2, ALU.subtract)
        self.s_tt(t1, ar, bi, ALU.mult, extra)
        self.s_tt(t2, ai, br, ALU.mult, extra)
        self.s_tt(outi, t1, t2, ALU.add)

    def s_expm1(self, dr, di, zr, zi, nsq, deg, pre):
        t1, t2, t3, t4 = (self.sm(pre + n) for n in ('t1', 't2', 't3', 't4'))
        sr, si = self.sm(pre + 'sr'), self.sm(pre + 'si')
        xr, xi = self.sm(pre + 'xr'), self.sm(pre + 'xi')
        sc = 1.0 / (1 << nsq)
        self.s_ts(xr, zr, sc, None, ALU.mult)
        if zi is not None:
            self.s_ts(xi, zi, sc, None, ALU.mult)
        self.s_ts(sr, xr, 1.0 / deg, 1.0, ALU.mult, ALU.add)
        if zi is not None:
            self.s_ts(si, xi, 1.0 / deg, None, ALU.mult)
        for k in range(deg - 1, 1, -1):
            if zi is not None:
                self.s_tt(t1, xr, sr, ALU.mult)
                self.s_tt(t2, xi, si, ALU.mult)
                self.s_tt(t3, xr, si, ALU.mult)
                self.s_tt(t4, xi, sr, ALU.mult)
                self.s_tt(t1, t1, t2, ALU.subtract)
                self.s_tt(t3, t3, t4, ALU.add)
                self.s_ts(sr, t1, 1.0 / k, 1.0, ALU.mult, ALU.add)
                self.s_ts(si, t3, 1.0 / k, None, ALU.mult)
            else:
                self.s_tt(t1, xr, sr, ALU.mult)
                self.s_ts(sr, t1, 1.0 / k, 1.0, ALU.mult, ALU.add)
        if zi is not None:
            self.s_cmul(dr, di, xr, xi, sr, si, t1, t2)
        else:
            self.s_tt(dr, xr, sr, ALU.mult)
        for _ in range(nsq):
            if zi is not None:
                self.s_tt(t1, dr, dr, ALU.mult)
                self.s_tt(t2, di, di, ALU.mult)
                self.s_tt(t3, dr, di, ALU.mult)
                self.s_tt(t1, t1, t2, ALU.subtract)
                self.s_stt(dr, dr, 2.0, t1, ALU.mult, ALU.add)
                self.s_tt(t3, t3, di, ALU.add)
                self.s_ts(di, t3, 2.0, None, ALU.mult)
            else:
                self.s_tt(t1, dr, dr, ALU.mult)
                self.s_stt(dr, dr, 2.0, t1, ALU.mult, ALU.add)

    def s5_layer_params(self, sl):
        sm = self.sm
        self.dma('sp', self.LAM3[:, :, :], self.d_lam[sl].rearrange("a p g -> p a g"), self.S5S, writes=[self.S5S])
        self.dma('sp', self.H0[:, :, :], self.d_h0[sl], self.S5S, writes=[self.S5S])
        lamr, lami, ldt = self.LAM3[:, 0, :], self.LAM3[:, 1, :], self.LAM3[:, 2, :]
        self.s_expm1(sm('dt'), None, ldt, None, 6, 7, 'e_')
        self.s_ts(sm('dt'), sm('dt'), 1.0, None, ALU.add)
        self.s_tt(sm('ar'), lamr, sm('dt'), ALU.mult)
        self.s_tt(sm('ai'), lami, sm('dt'), ALU.mult)
        self.s_expm1(sm('nr'), sm('ni'), sm('ar'), sm('ai'), 7, 8, 'e_')
        self.s_ts(sm('abr'), sm('nr'), 1.0, None, ALU.add)
        t1, t2 = sm('e_t1'), sm('e_t2')
        self.s_tt(t1, lamr, lamr, ALU.mult)
        self.s_tt(t2, lami, lami, ALU.mult)
        self.s_tt(t1, t1, t2, ALU.add)
        self.s.op('dve', lambda e: e.reciprocal(sm('rden'), t1), reads=[self.S5S], writes=[self.S5S])
        self.s_tt(t1, sm('nr'), lamr, ALU.mult)
        self.s_tt(t2, sm('ni'), lami, ALU.mult)
        self.s_tt(t1, t1, t2, ALU.add)
        self.s_tt(sm('fre'), t1, sm('rden'), ALU.mult)
        self.s_tt(t1, sm('ni'), lamr, ALU.mult)
        self.s_tt(t2, sm('nr'), lami, ALU.mult)
        self.s_tt(t1, t1, t2, ALU.subtract)
        self.s_tt(sm('fim'), t1, sm('rden'), ALU.mult)
        self.s_ts(t1, sm('ar'), 2.0, None, ALU.mult)
        self.s_expm1(sm('m2'), None, t1, None, 5, 6, 'e_')
        self.s_ts(sm('m2'), sm('m2'), 1.0, None, ALU.add)
        self.s.op('dve', lambda e: e.reciprocal(sm('im2'), sm('m2')), reads=[self.S5S], writes=[self.S5S])
        self.s_tt(sm('q1r'), sm('abr'), sm('im2'), ALU.mult)
        self.s_stt(sm('q1i'), sm('ni'), -1.0, sm('im2'), ALU.mult, ALU.mult)
        cr, ci = sm('abr'), sm('ni')
        for k, nm in enumerate(('p2', 'p4', 'mu')):
            self.s_cmul(sm(nm + 'r'), sm(nm + 'i'), cr, ci, cr, ci, t1, t2)
            cr, ci = sm(nm + 'r'), sm(nm + 'i')
        self.s_tt(sm('rho8'), sm('m2'), sm('m2'), ALU.mult)
        self.s_tt(sm('rho8'), sm('rho8'), sm('rho8'), ALU.mult)
        self.s.op('dve', lambda e: e.reciprocal(t1, sm('rho8')), reads=[self.S5S], writes=[self.S5S])
        self.s_tt(sm('E0r'), sm('mur'), t1, ALU.mult)
        self.s_tt(sm('E0i'), sm('mui'), t1, ALU.mult)
        for k in range(1, 7):
            self.s_cmul(sm(f'E{k}r'), sm(f'E{k}i'), sm(f'E{k-1}r'), sm(f'E{k-1}i'), sm(f'E{k-1}r'), sm(f'E{k-1}i'), t1, t2)
        self.s_cmul(sm('g0r'), sm('g0i'), sm('mur'), sm('mui'), self.H0[:, 0, :], self.H0[:, 1, :], t1, t2)

    def blk(self, k):
        return self.BLK[:, k, :].rearrange("p (g n) -> p g n", g=8)

    def s5_block_tables(self, sl, j):
        sm = self.sm
        gs = slice(8 * j, 8 * j + 8)
        B = self.blk
        PWR, PWI, QR, QI, WBR, WBI, W7R, W7I, TA, TB = (B(k) for k in range(10))
        self._s_wres = self.BLKres
        self.dma('sp', self.PB[:, :, :, :], self.d_B[sl][:, :, gs, :], self.PBres, writes=[self.PBres])
        self.dma('sp', self.PC[:, :, :, :], self.d_C[sl][:, :, gs, :], self.PBres, writes=[self.PBres])

        def col(ap, n):
            return ap[:, :, n:n + 1]

        def sv(name):
            return sm(name)[:, gs].unsqueeze(2)

        def bc(ap, shape):
            return ap.to_broadcast(shape)

        TA2 = self.SML[:, S5_NSM - 2, :].rearrange("p (g n) -> p g n", g=8)
        TB2 = self.SML[:, S5_NSM - 1, :].rearrange("p (g n) -> p g n", g=8)

        def cmul_tab(outr, outi, ar, ai, br, bi, shape, tmps=None, extra=()):
            ta_, tb_ = tmps or (TA, TB)
            ta = ta_[:, :, 0:shape[2]]
            tb = tb_[:, :, 0:shape[2]]
            self.s_cmul(outr, outi, ar, ai, bc(br, shape), bc(bi, shape), ta, tb, extra)

        for (TR, TI, b1r, b1i, b2r, b2i, b4r, b4i) in (
                (PWR, PWI, 'abr', 'ni', 'p2r', 'p2i', 'p4r', 'p4i'),):
            self.s.op('dve', lambda e, o=col(TR, 0): e.memset(o, 1.0), reads=[self.BLKres], writes=[self.BLKres])
            self.s.op('dve', lambda e, o=col(TI, 0): e.memset(o, 0.0), reads=[self.BLKres], writes=[self.BLKres])
            self.s_tt(col(TR, 1), sv(b1r), sv(b1r), ALU.max)
            self.s_tt(col(TI, 1), sv(b1i), sv(b1i), ALU.max)
            self.s_tt(col(TR, 2), sv(b2r), sv(b2r), ALU.max)
            self.s_tt(col(TI, 2), sv(b2i), sv(b2i), ALU.max)
            cmul_tab(TR[:, :, 3:5], TI[:, :, 3:5], TR[:, :, 1:3], TI[:, :, 1:3], sv(b2r), sv(b2i), [128, 8, 2])
            cmul_tab(TR[:, :, 5:9], TI[:, :, 5:9], TR[:, :, 1:5], TI[:, :, 1:5], sv(b4r), sv(b4i), [128, 8, 4])
        self._s_eng = 'pool'
        self._s_wres = self.BLKQres
        q2 = (TA2, TB2)
        self.s.op('pool', lambda e, o=col(QR, 0): e.memset(o, 1.0), reads=[self.BLKQres], writes=[self.BLKQres])
        self.s.op('pool', lambda e, o=col(QI, 0): e.memset(o, 0.0), reads=[self.BLKQres], writes=[self.BLKQres])
        self.s.op('pool', lambda e, o=col(QR, 1), a=sv('q1r'): e.tensor_copy(o, a), reads=[self.S5S, self.BLKQres],
                  writes=[self.BLKQres])
        self.s.op('pool', lambda e, o=col(QI, 1), a=sv('q1i'): e.tensor_copy(o, a), reads=[self.S5S, self.BLKQres],
                  writes=[self.BLKQres])
        self.s_cmul(col(QR, 2), col(QI, 2), col(QR, 1), col(QI, 1), col(QR, 1), col(QI, 1), col(TA2, 0), col(TB2, 0))
        cmul_tab(QR[:, :, 3:5], QI[:, :, 3:5], QR[:, :, 1:3], QI[:, :, 1:3], col(QR, 2), col(QI, 2), [128, 8, 2], q2)
        cmul_tab(QR[:, :, 5:9], QI[:, :, 5:9], QR[:, :, 1:5], QI[:, :, 1:5], col(QR, 4), col(QI, 4), [128, 8, 4], q2)
        cmul_tab(WBR[:, :, 0:8], WBI[:, :, 0:8], QR[:, :, 0:8], QI[:, :, 0:8], sv('fre'), sv('fim'), [128, 8, 8], q2)
        self._s_eng = 'dve'
        self._s_wres = self.BLKres
        cmul_tab(W7R[:, :, 0:8], W7I[:, :, 0:8], WBR[:, :, 0:8], WBI[:, :, 0:8], col(PWR, 7), col(PWI, 7), [128, 8, 8],
                 None, [self.BLKQres])
        self._s_wres = None

    def s5_block_expand_in(self, sl, j):
        B = self.blk
        PWR, PWI, QR, QI, WBR, WBI, W7R, W7I, TA, TB = (B(k) for k in range(10))
        sh = [128, 8, 8, 16]
        Br = self.PB[:, 0, :, :].unsqueeze(2).to_broadcast(sh)
        Bi = self.PB[:, 1, :, :].unsqueeze(2).to_broadcast(sh)
        for (WR, WI, DST, dres) in ((WBR, WBI, self.EIN, self.r_ein), (W7R, W7I, self.GEN7, self.r_gen7)):
            wr = WR[:, :, 0:8].unsqueeze(3).to_broadcast(sh)
            wi = WI[:, :, 0:8].unsqueeze(3).to_broadcast(sh)
            a1 = self.TBG[:, 0:1024].rearrange("p (g s k) -> p g s k", g=8, s=8)
            a2 = self.TBG[:, 1024:2048].rearrange("p (g s k) -> p g s k", g=8, s=8)
            ex = [self.PBres, self.r_g, self.BLKres, self.BLKQres]
            w_ = [self.r_g]
            self.s.op('dve', lambda e, o=a1, x=wr, y=Br: e.tensor_tensor(o, x, y, ALU.mult), reads=[self.S5S] + ex, writes=w_)
            self.s.op('dve', lambda e, o=a2, x=wi, y=Bi: e.tensor_tensor(o, x, y, ALU.mult), reads=[self.S5S] + ex, writes=w_)
            dre = DST[:, :, 0, :].rearrange("p g (s k) -> p g s k", s=8)
            self.s.op('dve', lambda e, o=dre, x=a1, y=a2: e.tensor_tensor(o, x, y, ALU.subtract), reads=w_, writes=w_ + [dres])
            self.s.op('dve', lambda e, o=a1, x=wr, y=Bi: e.tensor_tensor(o, x, y, ALU.mult), reads=[self.S5S] + ex, writes=w_)
            self.s.op('dve', lambda e, o=a2, x=wi, y=Br: e.tensor_tensor(o, x, y, ALU.mult), reads=[self.S5S] + ex, writes=w_)
            dim_ = DST[:, :, 1, :].rearrange("p g (s k) -> p g s k", s=8)
            self.s.op('dve', lambda e, o=dim_, x=a1, y=a2: e.tensor_tensor(o, x, y, ALU.add), reads=w_, writes=w_ + [dres])

    def s5_block_expand(self, sl, j):
        B = self.blk
        PWR, PWI, QR, QI, WBR, WBI, W7R, W7I, TA, TB = (B(k) for k in range(10))
        t1 = self.TBG[:, 0:1152]
        t2 = self.TB2[:, 0:1152]
        sh9 = [128, 8, 9, 16]
        Cr = self.PC[:, 0, :, :].unsqueeze(2).to_broadcast(sh9)
        Ci = self.PC[:, 1, :, :].unsqueeze(2).to_broadcast(sh9)
        pr = PWR[:, :, 0:9].unsqueeze(3).to_broadcast(sh9)
        pi = PWI[:, :, 0:9].unsqueeze(3).to_broadcast(sh9)
        a1 = t1.rearrange("p (g s k) -> p g s k", g=8, s=9)
        a2 = t2.rearrange("p (g s k) -> p g s k", g=8, s=9)
        ex = [self.PBres, self.r_t1, self.r_t2, self.r_g, self.BLKres, self.BLKQres]
        w_ = [self.r_t1, self.r_t2, self.r_g]
        self.s.op('dve', lambda e: e.tensor_tensor(a1, Cr, pr, ALU.mult), reads=[self.S5S] + ex, writes=w_)
        self.s.op('dve', lambda e: e.tensor_tensor(a2, Ci, pi, ALU.mult), reads=[self.S5S] + ex, writes=w_)
        dre = self.AOUT[:, :, 0, :].rearrange("p g (s k) -> p g s k", s=9)
        self.s.op('dve', lambda e: e.tensor_tensor(dre, a1, a2, ALU.subtract), reads=w_, writes=w_ + [self.r_aout])
        self.s.op('dve', lambda e: e.tensor_tensor(a1, Cr, pi, ALU.mult), reads=[self.S5S] + ex, writes=w_)
        self.s.op('dve', lambda e: e.tensor_tensor(a2, Ci, pr, ALU.mult), reads=[self.S5S] + ex, writes=w_)
        dim_ = self.AOUT[:, :, 1, :].rearrange("p g (s k) -> p g s k", s=9)
        self.s.op('dve', lambda e: e.scalar_tensor_tensor(dim_, a1, -1.0, a2, ALU.mult, ALU.subtract), reads=w_,
                  writes=w_ + [self.r_aout])

    def s5_block(self, i, sl, j):
        sm = self.sm
        gs = slice(8 * j, 8 * j + 8)
        ident, anti = self.cb('ident'), self.cb('anti')
        hsrc = self.HT[:, j, :].rearrange("p (c s) -> p s c", s=8)
        for (dst, dres, rev) in ((self.HTOK, self.r_htok, False), (self.HTOKR, self.r_htokr, True)):
            for half in range(2):
                b = self.newbank()
                for q in range(4):
                    s_ = half * 4 + q
                    src_s = 7 - s_ if rev else s_
                    self.mm(b, b.t[:, q * 128:(q + 1) * 128], hsrc[:, src_s, :], ident, [self.HTres[j], self.CBres])
                dv = dst.rearrange("p g (s k) -> p s g k", s=8)[:, half * 4:half * 4 + 4, :, :]
                self.cp('act', dv, b.t[:, :].rearrange("p (s g k) -> p s g k", s=4, g=8), [b.res], [dres])
        for (src, sres, dst, dres, mat) in ((self.HTOK, self.r_htok, self.U, self.r_u, ident),
                                            (self.HTOKR, self.r_htokr, self.UR, self.r_ur, anti)):
            for half in range(2):
                b = self.newbank()
                for q in range(4):
                    g = half * 4 + q
                    self.mm(b, b.t[:, q * 128:(q + 1) * 128], src[:, g, :], mat, [sres, self.CBres])
                self.cp('act', dst[:, half * 4:half * 4 + 4, :], b.t[:, :].rearrange("p (a b) -> p a b", a=4),
                        [b.res], [dres])
        if j == 0:
            self.s5_block_tables(sl, 0)
            self.s5_block_expand_in(sl, 0)
        self.s5_block_expand(sl, j)
        for ri in range(2):
            for half in range(2):
                b = self.newbank()
                for q in range(4):
                    g = half * 4 + q
                    self.mm(b, b.t[:, q * 128:(q + 1) * 128], self.GEN7[:, g, ri, :], ident, [self.r_gen7, self.CBres])
                v = b.t[:, :].rearrange("p (a b) -> p a b", a=4)
                self.cp('act', self.MGF[:, half * 4:half * 4 + 4, ri, 0:64], v[:, :, 0:64], [b.res], [self.r_mgf])
                self.cp('act', self.MGB[:, half * 4:half * 4 + 4, ri, 64:128], v[:, :, 64:128], [b.res], [self.r_mgb])
        for dr in range(2):
            rows = slice(dr * 64, dr * 64 + 64)
            for half in range(2):
                b = self.newbank()
                for q in range(4):
                    g = half * 4 + q
                    for ri in range(2):
                        self.mm(b, b.t[:, q * 128:(q + 1) * 128], self.EIN[rows, g, ri, :], self.AOUT[rows, g, ri, 0:128],
                                [self.r_ein, self.r_aout])
                self.s.op('dve', lambda e, o=self.MINT[:, half * 4:half * 4 + 4, dr, :],
                          a=b.t[:, :].rearrange("p (a b) -> p a b", a=4),
                          m=self.TRI[:, :].unsqueeze(1).to_broadcast([128, 4, 128]): e.tensor_tensor(o, a, m, ALU.mult),
                          reads=[b.res, self.S5Cres], writes=[self.r_mint])
        gb = {}
        for ri in range(2):
            for half in range(2):
                b = self.newbank()
                gb[(ri, half)] = b
                for q in range(4):
                    g = half * 4 + q
                    self.mm(b, b.t[:, q * 128:(q + 1) * 128], self.MGF[:, g, ri, :], self.U[:, g, :], [self.r_mgf, self.r_u])
                    self.mm(b, b.t[:, q * 128:(q + 1) * 128], self.MGB[:, g, ri, :], self.UR[:, g, :], [self.r_mgb, self.r_ur])
        for ri, GG in ((0, self.GR), (1, self.GI)):
            for half in range(2):
                b = gb[(ri, half)]
                self.cp('act', GG[:, half * 4:half * 4 + 4, :], b.t[:, :].rearrange("p (a b) -> p a b", a=4),
                        [b.res], [self.r_g])
            g0 = sm('g0r' if ri == 0 else 'g0i')[:, gs].unsqueeze(2)
            self.s.op('dve', lambda e, o=GG[:, :, 0:1], a=GG[:, :, 0:1], b_=g0: e.tensor_tensor(o, a, b_, ALU.add),
                      reads=[self.S5S, self.r_g], writes=[self.r_g])
        self.s5_scan(sl, j)
        if j + 1 < NF:
            self.s5_block_tables(sl, j + 1)
            self.s5_block_expand_in(sl, j + 1)
        YF = self.T1.rearrange("p (t f) -> p t f", t=8)
        YB = self.T2.rearrange("p (t f) -> p t f", t=8)
        for dr, (Uc, ures, Y, yres) in enumerate(((self.U, self.r_u, YF, self.r_t1), (self.UR, self.r_ur, YB, self.r_t2))):
            rows = slice(dr * 64, dr * 64 + 64)
            for half in range(2):
                b = self.newbank()
                for q in range(4):
                    g = half * 4 + q
                    cs = slice(q * 128, (q + 1) * 128)
                    self.mm(b, b.t[:, cs], Uc[:, g, :], self.MINT[:, g, dr, :], [ures, self.r_mint])
                    for ri in range(2):
                        self.mm(b, b.t[:, cs], self.HPB[rows, ri, g, :], self.AOUT[rows, g, ri, 16:144],
                                [self.r_hpb, self.r_aout])
                src = b.t[:, :].rearrange("p (g t k) -> p g t k", g=4, t=8)
                dst = Y[:, :, half * 64:half * 64 + 64].rearrange("p t (g k) -> p g t k", g=4)
                self.cp('act', dst, src, [b.res], [yres])
        yfull = self.TMP[j % 2]
        yres = self.TMPres[j % 2]
        for half in range(2):
            b = self.newbank()
            for q in range(4):
                t_ = half * 4 + q
                cs = slice(q * 128, (q + 1) * 128)
                self.mm(b, b.t[:, cs], YF[:, t_, :], self.IDF[:, :], [self.r_t1, self.IDFres])
                self.mm(b, b.t[:, cs], YB[:, 7 - t_, :], self.ANTIF[:, :], [self.r_t2, self.S5Cres])
            dst = yfull[:, :].rearrange("p (c t) -> p t c", t=8)[:, half * 4:half * 4 + 4, :]
            usrc = self.HT[:, j, :].rearrange("p (c t) -> p t c", t=8)[:, half * 4:half * 4 + 4, :]
            self.s.op('dve', lambda e, o=dst, u=usrc, d=self.SPt[:, self.cm.m[f's5d_{sl}'][0] + j:self.cm.m[f's5d_{sl}'][0] + j + 1],
                      y=b.t[:, :].rearrange("p (t c) -> p t c", t=4): e.scalar_tensor_tensor(o, u, d, y, ALU.mult, ALU.add),
                      reads=[b.res, self.HTres[j], self.SPres], writes=[yres])
        self.act(self.HT[:, j, :], yfull[:, :], AF.Gelu_apprx_tanh, [yres], [self.HTres[j]])

    def s5_scan(self, sl, j):
        sm = self.sm
        gs = slice(8 * j, 8 * j + 8)
        GR, GI, COS, SIN = self.GR, self.GI, self.COS, self.SIN
        sh = [128, 8, 128]
        PTMP = self.HPB.rearrange("p r g c -> p (r g c)").bitcast(F32)
        self.s.op('pool', lambda e: e.memset(COS[:, :, 0:1], 1.0), reads=[self.r_cs], writes=[self.r_cs])
        self.s.op('pool', lambda e: e.memset(SIN[:, :, 0:1], 0.0), reads=[self.r_cs], writes=[self.r_cs])
        for k in range(7):
            d = 1 << k
            er = sm(f'E{k}r')[:, gs].unsqueeze(2).to_broadcast([128, 8, d])
            ei = sm(f'E{k}i')[:, gs].unsqueeze(2).to_broadcast([128, 8, d])
            t1 = PTMP[:, 0:8 * d].rearrange("p (g c) -> p g c", g=8)
            t2 = PTMP[:, 512:512 + 8 * d].rearrange("p (g c) -> p g c", g=8)
            rd = [self.S5S, self.r_cs, self.r_hpb]
            wr = [self.r_hpb, self.r_cs]
            o = self.s.op
            o('pool', lambda e, a=t1, x=COS[:, :, 0:d], y=er: e.tensor_tensor(a, x, y, ALU.mult), reads=rd, writes=wr)
            o('pool', lambda e, a=t2, x=SIN[:, :, 0:d], y=ei: e.tensor_tensor(a, x, y, ALU.mult), reads=rd, writes=wr)
            o('pool', lambda e, a=COS[:, :, d:2 * d], x=t1, y=t2: e.tensor_tensor(a, x, y, ALU.subtract), reads=rd, writes=wr)
            o('pool', lambda e, a=t1, x=COS[:, :, 0:d], y=ei: e.tensor_tensor(a, x, y, ALU.mult), reads=rd, writes=wr)
            o('pool', lambda e, a=t2, x=SIN[:, :, 0:d], y=er: e.tensor_tensor(a, x, y, ALU.mult), reads=rd, writes=wr)
            o('pool', lambda e, a=SIN[:, :, d:2 * d], x=t1, y=t2: e.tensor_tensor(a, x, y, ALU.add), reads=rd, writes=wr)
        T1 = self.T1[:, 0:1024].rearrange("p (g c) -> p g c", g=8)
        T2 = self.T2[:, 0:1024].rearrange("p (g c) -> p g c", g=8)
        T3 = self.T3.rearrange("p (g c) -> p g c", g=8)
        AM = self.AMt.rearrange("p (g c) -> p g c", g=8)
        rd = [self.S5S, self.r_cs, self.r_t1, self.r_t2, self.r_g, self.S5Cres, self.r_ein, self.r_gen7]
        wr = [self.r_t1, self.r_t2, self.r_g, self.r_ein, self.r_gen7]
        o = self.s.op
        o('pool', lambda e: e.tensor_tensor(AM, sm('rho8')[:, gs].unsqueeze(2).to_broadcast(sh),
                                            self.CARRY[:, :].unsqueeze(1).to_broadcast(sh), ALU.mult),
          reads=[self.S5S, self.S5Cres, self.r_ein], writes=[self.r_ein])
        o('dve', lambda e: e.tensor_tensor(T1, GR, COS, ALU.mult), reads=rd, writes=wr)
        o('dve', lambda e: e.tensor_tensor(T2, GI, SIN, ALU.mult), reads=rd, writes=wr)
        o('dve', lambda e: e.tensor_tensor(T1, T1, T2, ALU.add), reads=rd, writes=wr)
        o('dve', lambda e: e.tensor_tensor(T3, GI, COS, ALU.mult), reads=rd, writes=wr)
        o('dve', lambda e: e.tensor_tensor(T2, GR, SIN, ALU.mult), reads=rd, writes=wr)
        o('dve', lambda e: e.tensor_tensor(T3, T3, T2, ALU.subtract), reads=rd, writes=wr)
        fl = lambda a: a.rearrange("p g c -> p (g c)")
        o('dve', lambda e: e.tensor_tensor_scan(fl(GR), fl(AM), fl(T1), 0.0, ALU.mult, ALU.add), reads=rd, writes=wr)
        o('dve', lambda e: e.tensor_tensor_scan(fl(GI), fl(AM), fl(T3), 0.0, ALU.mult, ALU.add), reads=rd, writes=wr)
        o('dve', lambda e: e.tensor_tensor(T1, GR, COS, ALU.mult), reads=rd, writes=wr)
        o('dve', lambda e: e.tensor_tensor(T2, GI, SIN, ALU.mult), reads=rd, writes=wr)
        o('dve', lambda e: e.tensor_tensor(T1, T1, T2, ALU.subtract), reads=rd, writes=wr)
        o('dve', lambda e: e.tensor_tensor(T3, GI, COS, ALU.mult), reads=rd, writes=wr)
        o('dve', lambda e: e.tensor_tensor(T2, GR, SIN, ALU.mult), reads=rd, writes=wr)
        o('dve', lambda e: e.tensor_tensor(T3, T3, T2, ALU.add), reads=rd, writes=wr)
        cm_ = self.CARRY[:, 1:128].unsqueeze(1).to_broadcast([128, 8, 127])
        prd = [self.S5S, self.S5Cres, self.r_t1, self.r_gen7]
        for ri, Hf in ((0, T1), (1, T3)):
            o('pool', lambda e, a=self.HPB[:, ri, :, 1:128], x=Hf[:, :, 0:127], y=cm_: e.tensor_tensor(a, x, y, ALU.mult),
              reads=prd, writes=[self.r_hpb])
            o('pool', lambda e, a=self.HPB[:, ri, :, 0:1], x=self.H0[:, ri, gs].unsqueeze(2): e.tensor_copy(a, x),
              reads=[self.S5S], writes=[self.r_hpb])
            fv = self.FINALL[:, :, ri, gs].rearrange("p s g -> p g s")
            hv = Hf.rearrange("p g (s c) -> p g s c", s=4)[:, :, :, 31]
            o('pool', lambda e, a=fv, x=hv: e.tensor_copy(a, x), reads=prd, writes=[self.FINres])

    def s5_prefetch_params(self, sl):
        self.s5_layer_params(sl)
        self._s5_params_ready = sl

    def s5_layer(self, i):
        sl = i // 3
        if not getattr(self, '_s5_const_loaded', False):
            self._s5_const_loaded = True
            self.dma('sp', self.CARRY[:, :], self.d_carry, self.S5Cres, writes=[self.S5Cres])
            self.dma('sp', self.TRI[:, :], self.d_tri, self.S5Cres, writes=[self.S5Cres])
            self.dma('sp', self.ANTIF[:, :], self.d_antif, self.S5Cres, writes=[self.S5Cres])
        self.s.op('pool', lambda e: e.memset(self.MGF[:, :, :, 64:128], 0.0), writes=[self.r_mgf])
        self.s.op('pool', lambda e: e.memset(self.MGB[:, :, :, 0:64], 0.0), writes=[self.r_mgb])
        if getattr(self, '_s5_params_ready', None) != sl:
            self.s5_prefetch_params(sl)
        self._s5_params_ready = None
        for j in range(NF):
            self.s5_block(i, sl, j)
            self.mod_hook(1)
        for seg in range(4):
            for ri in range(2):
                b = self.newbank()
                self.mm(b, b.t[0:64, 0:128], self.FINALL[:, seg, ri, :], self.IDF[:, :], [self.FINres, self.IDFres])
                k = self.vf_i % 2
                self.vf_i += 1
                self.cp('act', self.VF[k][0:64, 0:128], b.t[0:64, 0:128], [b.res], [self.VFres[k]])
                for dr in range(2):
                    oseg = seg if dr == 0 else 3 - seg
                    self.dma('sp', self.o_s5[sl, oseg, dr, ri], self.VF[k][0:64, dr * 64:dr * 64 + 64], self.VFres[k],
                             reads=[self.VFres[k]], is_output=True)
        gb0 = self.cm.m[f'glub_{sl}'][0]
        for ch in range(4):
            w, wres = self.next_w(('glu', i, ch))
            for m2 in range(2):
                mt = ch * 2 + m2
                for h in range(2):
                    hs = slice(h * 512, (h + 1) * 512)
                    ba = self.newbank()
                    bg = self.newbank()
                    for kt in range(8):
                        self.mm(ba, ba.t[:, :], w[:, kt, m2 * 128:(m2 + 1) * 128], self.HT[:, kt, hs], [wres, self.HTres[kt]])
                    for kt in range(8):
                        self.mm(bg, bg.t[:, :], w[:, kt, 256 + m2 * 128:256 + (m2 + 1) * 128], self.HT[:, kt, hs],
                                [wres, self.HTres[kt]])
                    k = self.rl_i % 2
                    self.rl_i += 1
                    self.act(self.RL[k][:, :], bg.t[:, :], AF.Sigmoid, [bg.res, self.SPres], [self.RLres[k]],
                             bias=self.SPt[:, gb0 + 8 + mt:gb0 + 9 + mt], scale=1.0)
                    self.stt('dve', self.RL[k][:, :], ba.t[:, :], self.SPt[:, gb0 + mt:gb0 + mt + 1], self.RL[k][:, :],
                             ALU.add, ALU.mult, [ba.res, self.SPres, self.RLres[k]], [self.RLres[k]])
                    self.stt('dve', self.X[:, mt, hs], self.RL[k][:, :], self.MODT[:, 16 + mt:17 + mt], self.X[:, mt, hs],
                             ALU.mult, ALU.add, [self.RLres[k], self.MODres, self.Xres[mt]], [self.Xres[mt]])
```

```python
import math
import numpy as np
import concourse.bass as bass
import concourse.mybir as mybir
from concourse.bass_utils import run_bass_kernel_spmd

F32 = mybir.dt.float32
BF16 = mybir.dt.bfloat16
AF = mybir.ActivationFunctionType
ALU = mybir.AluOpType

D = 1024
T = 1024
NF = 8
DFF = 4096
DEPTH = 4
EPS = 1e-6
NEG = -30000.0
EPOCH = 30000
ENGS = ('pe', 'act', 'dve', 'pool', 'sp')


class Res:
    __slots__ = ('name', 'w', 'r')

    def __init__(self, name=''):
        self.name = name
        self.w = None
        self.r = []


class DrySched:
    def __init__(self):
        self.count = {e: 0 for e in ENGS}
        self.nsem = 0

    def op(self, *a, **k):
        return None

    def dma(self, *a, **k):
        return None

    def barrier(self):
        pass

    def finish(self):
        pass

    def emit(self):
        pass


class Sched:
    def __init__(self, nc):
        self.nc = nc
        self.ops = {e: [] for e in ENGS}
        self.count = {e: 0 for e in ENGS}
        self.seen = {e: {} for e in ENGS}
        self.esem = {e: [] for e in ENGS}
        self.dsem = {}
        self.out_tokens = []
        self.nsem = 0
        self.pending = {e: [] for e in ENGS}

    def _esem(self, e, epoch):
        while len(self.esem[e]) <= epoch:
            self.esem[e].append(self.nc.alloc_semaphore(f"se_{e}_{len(self.esem[e])}"))
            self.nsem += 1
        return self.esem[e][epoch]

    def _collect(self, eng, reads, writes, extra=()):
        toks = list(extra)
        for r in reads:
            if r.w is not None:
                toks.append(r.w)
        for w in writes:
            if w.w is not None:
                toks.append(w.w)
            toks.extend(w.r)
        need = {}
        for t in toks:
            if t[0] == 'e':
                _, e2, idx = t
                if e2 == eng and eng == 'pe':
                    continue
                key = ('e', e2)
                val = idx
            else:
                key = ('d', t[1])
                val = t[2]
            if self.seen[eng].get(key, 0) >= val:
                continue
            if need.get(key, 0) < val:
                need[key] = val
        waits = []
        for key, val in need.items():
            self.seen[eng][key] = val
            if key[0] == 'e':
                ep = (val - 1) // EPOCH
                waits.append((self._esem(key[1], ep), (val - 1) % EPOCH + 1))
            else:
                waits.append((self.dsem[key[1]][0], val))
        return waits

    def _mark(self, tok, reads, writes):
        for r in reads:
            r.r.append(tok)
        for w in writes:
            w.w = tok
            w.r = []

    def op(self, eng, fn, reads=(), writes=(), extra=()):
        extra = list(extra) + self.pending[eng]
        self.pending[eng] = []
        waits = self._collect(eng, reads, writes, extra)
        self.count[eng] += 1
        idx = self.count[eng]
        sem = self._esem(eng, (idx - 1) // EPOCH)
        self.ops[eng].append((waits, fn, sem, 1))
        tok = ('e', eng, idx)
        self._mark(tok, reads, writes)
        return tok

    def dma(self, eng, fn, key, reads=(), writes=(), is_output=False):
        extra = self.pending[eng]
        self.pending[eng] = []
        waits = self._collect(eng, reads, writes, extra)
        kid = id(key)
        if kid not in self.dsem:
            self.dsem[kid] = [self.nc.alloc_semaphore(f"sd_{len(self.dsem)}"), 0]
            self.nsem += 1
        ent = self.dsem[kid]
        ent[1] += 16
        self.ops[eng].append((waits, fn, ent[0], 16))
        tok = ('d', kid, ent[1])
        self._mark(tok, reads, writes)
        if is_output:
            self.out_tokens.append(tok)
        return tok

    def barrier(self):
        toks = [('e', e, self.count[e]) for e in ENGS if self.count[e] > 0]
        for e in ENGS:
            self.pending[e] = self.pending[e] + toks

    def finish(self):
        toks = list(self.out_tokens)
        waits = self._collect('sp', (), (), toks)
        self.ops['sp'].append((waits, None, None, 0))

    def emit(self):
        nc = self.nc
        ops = self.ops

        def replay(eng_obj, lst):
            for waits, fn, sem, inc in lst:
                for s, v in waits:
                    eng_obj.wait_ge(s, v)
                if fn is None:
                    continue
                ins = fn(eng_obj)
                if sem is not None:
                    ins.then_inc(sem, inc)

        with nc.Block() as block:
            @block.tensor
            def _(e):
                replay(e, ops['pe'])

            @block.scalar
            def _(e):
                replay(e, ops['act'])

            @block.vector
            def _(e):
                replay(e, ops['dve'])

            @block.gpsimd
            def _(e):
                replay(e, ops['pool'])

            @block.sync
            def _(e):
                replay(e, ops['sp'])


class ColMap:
    def __init__(self):
        self.m = {}
        self.n = 0

    def add(self, name, w):
        self.m[name] = (self.n, w)
        self.n += w

    def sl(self, name):
        a, w = self.m[name]
        return slice(a, a + w)


def fm(v):
    v = np.asarray(v, np.float32)
    return np.ascontiguousarray(v.reshape(-1, 128).T)


def small_colmap():
    cm = ColMap()
    cm.add('cvec', 8)
    for i in range(DEPTH):
        cm.add(f'g1_{i}', 8)
        cm.add(f'g2_{i}', 8)
        cm.add(f'adab_{i}', 48)
    for s in range(2):
        cm.add(f's5d_{s}', 8)
        cm.add(f'glub_{s}', 16)
    for nm in ('naq', 'nak', 'gq', 'gk', 'ctxb', 'carry'):
        cm.add(nm, 1)
    return cm


CONST_BF = ColMap()
for _nm in ('ones1024', 'ones128', 'blk64', 'ones', 'ident', 'pswap', 'anti'):
    CONST_BF.add(_nm, 128)


def const_bf_array():
    a = np.zeros((128, CONST_BF.n), np.float32)
    a[:, CONST_BF.sl('ones1024')] = 1.0 / 1024
    a[:, CONST_BF.sl('ones128')] = 1.0 / 128
    b = np.zeros((128, 128), np.float32)
    b[:64, :64] = 1.0 / 64
    b[64:, 64:] = 1.0 / 64
    a[:, CONST_BF.sl('blk64')] = b
    a[:, CONST_BF.sl('ones')] = 1.0
    a[:, CONST_BF.sl('ident')] = np.eye(128, dtype=np.float32)
    a[:, CONST_BF.sl('pswap')] = np.roll(np.eye(128, dtype=np.float32), 64, axis=0)
    a[:, CONST_BF.sl('anti')] = np.eye(128, dtype=np.float32)[::-1]
    return a


class Bank:
    def __init__(self, t, i):
        self.t = t
        self.res = Res(f'bank{i}')
        self.fresh = True


class Builder:
    def __init__(self, layers=(0, 1, 2, 3), mixers=True, plan=None):
        self.layers = layers
        self.mixers = mixers
        self.plan = plan
        self.nc = bass.Bass("TRN2", target_bir_lowering=False)
        self.s = Sched(self.nc) if plan is not None else DrySched()
        self.rec_tags = []
        self.cm = small_colmap()
        self.dram = {}
        self.wq = []
        self.wq_issued = 0

    def din(self, name, shape, dt=F32):
        t = self.nc.dram_tensor(name, list(shape), dt, kind="ExternalInput").ap()
        self.dram[name] = t
        return t

    def dout(self, name, shape, dt=F32):
        t = self.nc.dram_tensor(name, list(shape), dt, kind="ExternalOutput").ap()
        self.dram[name] = t
        return t

    def sb(self, name, shape, dt):
        return self.nc.alloc_sbuf_tensor(name, list(shape), dt)

    def mm(self, bank, out, lhsT, rhs, reads):
        st = bank.fresh
        bank.fresh = False
        return self.s.op('pe', lambda e: e.matmul(out, lhsT, rhs, start=st, stop=True, skip_group_check=True),
                         reads=reads, writes=[bank.res])

    def newbank(self):
        rot = getattr(self, 'rot', None) or list(range(8))
        b = self.banks[rot[self.bank_i % len(rot)]]
        self.bank_i += 1
        b.fresh = True
        return b

    def act(self, out, in_, func, reads, writes, bias=0.0, scale=1.0):
        return self.s.op('act', lambda e: e.activation(out=out, in_=in_, func=func, bias=bias, scale=scale),
                         reads=reads, writes=writes)

    def tt(self, eng, out, a, b, op, reads, writes):
        return self.s.op(eng, lambda e: e.tensor_tensor(out, a, b, op), reads=reads, writes=writes)

    def ts(self, eng, out, a, s1, s2, op0, op1, reads, writes):
        return self.s.op(eng, lambda e: e.tensor_scalar(out, a, s1, s2, op0, op1), reads=reads, writes=writes)

    def stt(self, eng, out, a, sc, b, op0, op1, reads, writes):
        return self.s.op(eng, lambda e: e.scalar_tensor_tensor(out, a, sc, b, op0, op1), reads=reads, writes=writes)

    def rsqrt(self, out, in_, reads, writes):
        self.s.op('act', lambda e: e.activation(out=out, in_=in_, func=AF.Ln, bias=self.EPSC[:, 0:1], scale=1.0),
                  reads=list(reads) + [self.EPSres], writes=writes)
        self.s.op('act', lambda e: e.activation(out=out, in_=out, func=AF.Exp, bias=0.0, scale=-0.5),
                  reads=writes, writes=writes)

    def cp(self, eng, out, in_, reads, writes):
        if eng == 'act':
            return self.s.op('act', lambda e: e.copy(out, in_), reads=reads, writes=writes)
        return self.s.op(eng, lambda e: e.tensor_copy(out, in_), reads=reads, writes=writes)

    def dma(self, eng, out, in_, key, reads=(), writes=(), is_output=False):
        return self.s.dma(eng, lambda e: e.dma_start(out=out, in_=in_), key, reads=reads, writes=writes,
                          is_output=is_output)

    def wq_add(self, src_ap, shape):
        self.wq.append((src_ap, shape))
        return len(self.wq) - 1

    def wq_get(self, idx):
        while self.wq_issued < min(len(self.wq), idx + self.NSLOT):
            j = self.wq_issued
            slot = j % self.NSLOT
            src, shape = self.wq[j]
            n = shape[1] * shape[2]
            dst = self.wslot[slot][:, 0:n].rearrange("p (a b) -> p a b", a=shape[1])
            if isinstance(src, list):
                wpart = shape[2] // len(src)
                for sap, part in src:
                    self.dma('pool', dst[:, :, part * wpart:(part + 1) * wpart], sap, self.wres[slot],
                             writes=[self.wres[slot]])
            else:
                self.dma('pool', dst, src, self.wres[slot], writes=[self.wres[slot]])
            self.wq_issued += 1
        slot = idx % self.NSLOT
        src, shape = self.wq[idx]
        n = shape[1] * shape[2]
        return self.wslot[slot][:, 0:n].rearrange("p (a b) -> p a b", a=shape[1]), self.wres[slot]

    def build(self):
        nc = self.nc
        cm = self.cm
        s = self.s
        xT = self.din('xT', [D, T])
        smallp = self.din('smallp', [128, cm.n])
        cbf = self.din('cbf', [128, CONST_BF.n])
        identf = self.din('identf', [128, 128])
        ada_w = self.din('ada_w', [DEPTH, D, 6 * D])
        mlp_w1 = self.din('mlp_w1', [DEPTH, D, DFF])
        mlp_w2 = self.din('mlp_w2', [DEPTH, DFF, D])
        yT = self.dout('yT', [D, T])
        self.declare_mixer_dram()

        self.X = self.sb('X', [128, NF, T], F32)
        self.Xres = [Res(f'X{j}') for j in range(NF)]
        self.HT = self.sb('HT', [128, NF, T], BF16)
        self.HTres = [Res(f'HT{j}') for j in range(NF)]
        self.SPt = self.sb('SPt', [128, cm.n], F32)
        self.SPres = Res('SP')
        self.CB = self.sb('CB', [128, CONST_BF.n], BF16)
        self.CBres = Res('CB')
        self.IDF = self.sb('IDF', [128, 128], F32)
        self.NSLOT = 3
        self.wslot = [self.sb(f'wslot{k}', [128, 6144], BF16) for k in range(self.NSLOT)]
        self.wres = [Res(f'w{k}') for k in range(self.NSLOT)]
        self.TMP = [self.sb(f'TMP{k}', [128, T], F32) for k in range(2)]
        self.TMPres = [Res(f'TMP{k}') for k in range(2)]
        self.SQ = [self.sb(f'SQ{k}', [128, T], BF16) for k in range(2)]
        self.SQres = [Res(f'SQ{k}') for k in range(2)]
        self.RSTD = self.sb('RSTD', [128, T], F32)
        self.RSTDres = Res('RSTD')
        self.MODTS = [self.sb(f'MODT{k}', [128, 48], F32) for k in range(2)]
        self.MODress = [Res(f'MOD{k}') for k in range(2)]
        self.GSC = self.sb('GSC', [128, 8], F32)
        self.GSCres = Res('GSC')
        self.SIL = self.sb('SIL', [128, 8], BF16)
        self.SILres = Res('SIL')
        self.SCR = self.sb('SCR', [128, 32 * T], BF16)
        self.HID = self.SCR[:, :].rearrange("p (a b) -> p a b", a=32)
        self.HIDres = [Res(f'HID{k}') for k in range(32)]
        self.EPSC = self.sb('EPSC', [128, 1], F32)
        self.EPSres = Res('EPS')
        self.s.op('dve', lambda e: e.memset(self.EPSC[:, :], EPS), writes=[self.EPSres])
        self.RL = [self.sb(f'RL{k}', [128, 512], F32) for k in range(2)]
        self.RLres = [Res(f'RL{k}') for k in range(2)]
        self.rl_i = 0
        self.alloc_mixer_sbuf()
        self.banks = [Bank(nc.alloc_psum_tensor(f'ps{k}', [128, 512], F32), k) for k in range(8)]
        self.bank_i = 0

        self.ada_w, self.mlp_w1, self.mlp_w2 = ada_w, mlp_w1, mlp_w2
        self.wtags = list(self.plan) if self.plan is not None else None
        if self.wtags is not None:
            for tag in self.wtags:
                src, shape = self.wsrc(tag)
                self.wq_add(src, shape)
        self.wnext = 0

        self.dma('sp', self.SPt[:, :], smallp, self.SPres, writes=[self.SPres])
        self.dma('pool', self.CB[:, :], cbf, self.CBres, writes=[self.CBres])
        self.IDFres = Res('IDF')
        self.dma('sp', self.IDF[:, :], identf, self.IDFres, writes=[self.IDFres])
        for j in range(NF):
            self.dma('sp', self.X[:, j, :], xT[j * 128:(j + 1) * 128, :], self.Xres[j], writes=[self.Xres[j]])
        self.act(self.SIL[:, :], self.SPt[:, cm.sl('cvec')], AF.Silu, [self.SPres], [self.SILres])
        self.load_mixer_inputs()

        self.MODT = self.MODTS[0]
        self.MODres = self.MODress[0]
        self.mod_pending = []
        for ch in range(8):
            self.modulation_chunk(self.layers[0], ch, 0)
        if self.mixers and self.layers[0] % 3 == 0:
            self.s5_prefetch_params(self.layers[0] // 3)
        for li, i in enumerate(self.layers):
            self.MODT = self.MODTS[li % 2]
            self.MODres = self.MODress[li % 2]
            if li + 1 < len(self.layers):
                self.mod_pending = [(self.layers[li + 1], ch, (li + 1) % 2) for ch in range(8)]
            self.norm_mod(f'g1_{i}', 8, 0)
            if self.mixers:
                self.mixer(i)
            self.norm_mod(f'g2_{i}', 32 + 0, 24)
            self.mod_hook(8)
            if self.mixers and li + 1 < len(self.layers) and self.layers[li + 1] % 3 == 0:
                self.s5_prefetch_params(self.layers[li + 1] // 3)
            self.mlp(i)

        for j in range(NF):
            self.dma('sp', yT[j * 128:(j + 1) * 128, :], self.X[:, j, :], self.Xres[j], reads=[self.Xres[j]],
                     is_output=True)
        self.finalize_mixer_outputs()
        s.finish()
        s.emit()
        return nc

    def next_w(self, tag):
        if self.wtags is None:
            self.rec_tags.append(tag)
            _, shape = self.wsrc(tag)
            n = shape[1] * shape[2]
            return self.wslot[0][:, 0:n].rearrange("p (a b) -> p a b", a=shape[1]), self.wres[0]
        assert self.wtags[self.wnext] == tag, (self.wtags[self.wnext], tag)
        ap, res = self.wq_get(self.wnext)
        self.wnext += 1
        return ap, res

    def wsrc(self, tag):
        kind, i, ch = tag
        pk = lambda w: w.rearrange("(kt p) n -> p kt n", p=128)
        if kind == 'ada':
            return pk(self.ada_w[i])[:, :, ch * 768:(ch + 1) * 768], [128, 8, 768]
        if kind == 'w1':
            return pk(self.mlp_w1[i])[:, :, ch * 512:(ch + 1) * 512], [128, 8, 512]
        if kind == 'w2':
            return pk(self.mlp_w2[i])[:, :, ch * 128:(ch + 1) * 128], [128, 32, 128]
        return self.wsrc_mixer(tag)

    def modulation_chunk(self, i, ch, buf):
        b = self.newbank()
        w, wres = self.next_w(('ada', i, ch))
        for m6 in range(6):
            for kt in range(8):
                self.mm(b, b.t[:, m6:m6 + 1], w[:, kt, m6 * 128:(m6 + 1) * 128], self.SIL[:, kt:kt + 1],
                        [wres, self.SILres])
        a0 = self.cm.m[f'adab_{i}'][0] + ch * 6
        self.tt('dve', self.MODTS[buf][:, ch * 6:ch * 6 + 6], b.t[:, 0:6], self.SPt[:, a0:a0 + 6], ALU.add,
                [b.res, self.SPres], [self.MODress[buf]])

    def mod_hook(self, n=1):
        for _ in range(n):
            if self.mod_pending:
                self.modulation_chunk(*self.mod_pending.pop(0))

    def cb(self, name):
        return self.CB[:, CONST_BF.sl(name)]

    def norm_mod(self, gname, sc0, sh0):
        cm = self.cm
        self.stt('dve', self.GSC[:, :], self.MODT[:, sc0:sc0 + 8], 1.0, self.SPt[:, cm.sl(gname)], ALU.add, ALU.mult,
                 [self.MODres, self.SPres], [self.GSCres])
        bs = [self.newbank(), self.newbank()]
        for j in range(NF):
            k = j % 2
            self.act(self.SQ[k][:, :], self.X[:, j, :], AF.Square, [self.Xres[j]], [self.SQres[k]])
            for h in range(2):
                self.mm(bs[h], bs[h].t[:, :], self.cb('ones1024'), self.SQ[k][:, h * 512:(h + 1) * 512],
                        [self.SQres[k], self.CBres])
        for h in range(2):
            self.rsqrt(self.RSTD[:, h * 512:(h + 1) * 512], bs[h].t[:, :], [bs[h].res], [self.RSTDres])
        for j in range(NF):
            k = j % 2
            self.tt('dve', self.TMP[k][:, :], self.X[:, j, :], self.RSTD[:, :], ALU.mult,
                    [self.Xres[j], self.RSTDres], [self.TMPres[k]])
            self.act(self.HT[:, j, :], self.TMP[k][:, :], AF.Identity, [self.TMPres[k], self.GSCres, self.MODres],
                     [self.HTres[j]], bias=self.MODT[:, sh0 + j:sh0 + j + 1], scale=self.GSC[:, j:j + 1])

    def mlp(self, i):
        for ch in range(8):
            w, wres = self.next_w(('w1', i, ch))
            for m4 in range(4):
                mt = ch * 4 + m4
                for h in range(2):
                    b = self.newbank()
                    for kt in range(8):
                        self.mm(b, b.t[:, :], w[:, kt, m4 * 128:(m4 + 1) * 128], self.HT[:, kt, h * 512:(h + 1) * 512],
                                [wres, self.HTres[kt]])
                    k = self.rl_i % 2
                    self.rl_i += 1
                    self.act(self.RL[k][:, :], b.t[:, :], AF.Relu, [b.res], [self.RLres[k]])
                    self.tt('dve' if k == 0 else 'pool', self.HID[:, mt, h * 512:(h + 1) * 512], self.RL[k][:, :],
                            self.RL[k][:, :], ALU.mult, [self.RLres[k]], [self.HIDres[mt]])
        for ch in range(8):
            w, wres = self.next_w(('w2', i, ch))
            for m2 in range(1):
                mt = ch
                for h in range(2):
                    b = self.newbank()
                    for kt in range(32):
                        self.mm(b, b.t[:, :], w[:, kt, m2 * 128:(m2 + 1) * 128], self.HID[:, kt, h * 512:(h + 1) * 512],
                                [wres, self.HIDres[kt]])
                    self.stt('dve', self.X[:, mt, h * 512:(h + 1) * 512], b.t[:, :], self.MODT[:, 40 + mt:41 + mt],
                             self.X[:, mt, h * 512:(h + 1) * 512], ALU.mult, ALU.add,
                             [b.res, self.MODres, self.Xres[mt]], [self.Xres[mt]])

    def declare_mixer_dram(self):
        pass

    def alloc_mixer_sbuf(self):
        pass

    def plan_mixer_weights(self, i):
        pass

    def load_mixer_inputs(self):
        pass

    def mixer(self, i):
        pass

    def finalize_mixer_outputs(self):
        pass


def core_tokens(inp, core):
    if core < 4:
        return np.asarray(inp['x_sample'][core], np.float32), np.asarray(inp['c'][core], np.float32)
    b0 = 4 * (core - 4)
    return (np.asarray(inp['x_prompt'][b0:b0 + 4], np.float32).reshape(T, D),
            np.asarray(inp['c_ctx'], np.float32))


def small_array(inp, core, cm):
    a = np.zeros((128, cm.n), np.float32)
    _, cvec = core_tokens(inp, core)
    a[:, cm.sl('cvec')] = fm(cvec)
    for i in range(DEPTH):
        a[:, cm.sl(f'g1_{i}')] = fm(inp['norm_g'][i, 0])
        a[:, cm.sl(f'g2_{i}')] = fm(inp['norm_g'][i, 1])
        a[:, cm.sl(f'adab_{i}')] = fm(inp['ada_b'][i])
    for sl in range(2):
        a[:, cm.sl(f's5d_{sl}')] = fm(inp['s5_d'][sl])
        a[:, cm.sl(f'glub_{sl}')] = fm(inp['s5_glu_b'][sl])
    a[:, cm.sl('naq')] = np.tile(np.asarray(inp['na_q_norm'][0], np.float32), 2)[:, None]
    a[:, cm.sl('nak')] = np.tile(np.asarray(inp['na_k_norm'][0], np.float32), 2)[:, None]
    a[:, cm.sl('gq')] = np.asarray(inp['gqa_q_norm'][0], np.float32)[:, None]
    a[:, cm.sl('gk')] = np.asarray(inp['gqa_k_norm'][0], np.float32)[:, None]
    a[:, cm.sl('ctxb')] = 0.0 if core < 4 else NEG
    a[:, cm.sl('carry')] = 1.0 if core < 4 else 0.0
    return a


def common_inputs(inp, core, cm):
    x, _ = core_tokens(inp, core)
    m = {
        'xT': np.ascontiguousarray(x.T),
        'smallp': small_array(inp, core, cm),
        'cbf': const_bf_array(),
        'identf': np.eye(128, dtype=np.float32),
        'ada_w': np.asarray(inp['ada_w'], np.float32),
        'mlp_w1': np.asarray(inp['mlp_w1'], np.float32),
        'mlp_w2': np.asarray(inp['mlp_w2'], np.float32),
    }
    return m


NA_TILES = [(kt, hf) for kt in range(8) for hf in range(2)
            if not ((kt < 2 and hf == 1) or (kt >= 6 and hf == 0))]
NA_TILE_IDX = {t: n for n, t in enumerate(NA_TILES)}


class FullBuilder(Builder):
    def declare_mixer_dram(self):
        self.na_wqkv = self.din('na_w_qkv', [D, 3 * D])
        self.na_wo = self.din('na_w_o', [D, D])
        self.gqa_wqkv = self.din('gqa_w_qkv', [D, 1536])
        self.gqa_wo = self.din('gqa_w_o', [D, D])
        self.s5_gluw = self.din('s5_glu_w', [2, D, 2 * D])
        self.na_kcT = self.din('na_kcT', [D, 512])
        self.na_vc = self.din('na_vc', [512, D])
        self.gqa_kcT = self.din('gqa_kcT', [256, 512])
        self.gqa_vc = self.din('gqa_vc', [512, 256])
        self.na_bias = self.din('na_bias', [16, len(NA_TILES), 128, 512])
        self.gmask = self.din('gmask', [4, 2048])
        self.rope = self.din('rope', [2, 128, T])
        self.o_nak = self.dout('o_nak', [T, D])
        self.o_nav = self.dout('o_nav', [T, D])
        self.o_gk = self.dout('o_gk', [T, 256])
        self.o_gv = self.dout('o_gv', [T, 256])
        self.declare_s5_dram()

    def alloc_mixer_sbuf(self):
        self.PT = [self.sb(f'PT{k}', [128, 512], BF16) for k in range(3)]
        self.PTres = [Res(f'PT{k}') for k in range(3)]
        self.PT += [self.RL[0][:, :].bitcast(BF16)[:, 0:512], self.RL[1][:, :].bitcast(BF16)[:, 0:512]]
        self.PTres += [self.RLres[0], self.RLres[1]]
        self.BT = [self.sb(f'BT{k}', [128, 512], BF16) for k in range(3)]
        self.BTres = [Res(f'BT{k}') for k in range(3)]
        self.RC = self.sb('RC', [128, 512], F32)
        self.RCres = Res('RC')
        self.VF = [self.sb(f'VF{k}', [128, 1024], F32) for k in range(2)]
        self.VFres = [Res(f'VF{k}') for k in range(2)]
        self.SQt = [self.SQ[0][:, 0:512], self.SQ[0][:, 512:1024], self.SQ[1][:, 0:512], self.SQ[1][:, 512:1024]]
        self.SQtres = [Res(f'SQt{k}') for k in range(4)]
        self.RS = [self.RSTD[:, 0:512], self.RSTD[:, 512:1024]]
        self.RSres = [Res(f'RS{k}') for k in range(2)]
        self.SAres = [[Res(f'SAd{k}'), Res(f'SAp{k}')] for k in range(2)]
        self.pt_i = 0
        self.bt_i = 0
        self.sq_i = 0
        self.rs_i = 0
        self.vf_i = 0
        self.kf_i = 0
        S = self.SCR
        self.QT = S[:, 0:8192].rearrange("p (a b) -> p a b", a=8)
        self.QTres = [Res(f'QT{j}') for j in range(8)]
        self.KT_na = S[:, 8192:20480].rearrange("p (a b) -> p a b", a=8)
        self.V_na = S[:, 20480:32768].rearrange("p (a b) -> p a b", a=12)
        self.KT_g = S[:, 8192:11264].rearrange("p (a b) -> p a b", a=2)
        self.V_g = S[:, 11264:14336].rearrange("p (a b) -> p a b", a=12)
        self.CC = S[:, 14336:16384].bitcast(F32)
        self.SS = S[:, 16384:18432].bitcast(F32)
        self.GM = S[:, 18432:20480]
        self.KTres = [Res(f'KT{j}') for j in range(8)]
        self.KCres = Res('KC')
        self.Vres = [Res(f'V{j}') for j in range(12)]
        self.ROPEres = Res('ROPE')
        self.GMres = Res('GM')
        self.alloc_s5_sbuf()

    def wsrc_mixer(self, tag):
        kind, i, ch = tag
        pk = lambda w: w.rearrange("(kt p) n -> p kt n", p=128)
        if kind == 'naqkv':
            return pk(self.na_wqkv)[:, :, ch * 512:(ch + 1) * 512], [128, 8, 512]
        if kind == 'gqkv':
            return pk(self.gqa_wqkv)[:, :, ch * 512:(ch + 1) * 512], [128, 8, 512]
        if kind == 'wo':
            w = self.na_wo if i % 3 == 1 else self.gqa_wo
            return pk(w)[:, :, ch * 512:(ch + 1) * 512], [128, 8, 512]
        assert kind == 'glu'
        sl = i // 3
        srcs = [(pk(self.s5_gluw[sl])[:, :, half * 1024 + ch * 256:half * 1024 + (ch + 1) * 256], half)
                for half in range(2)]
        return srcs, [128, 8, 512]

    def load_mixer_inputs(self):
        pass

    def mixer(self, i):
        kind = i % 3
        self.s.barrier()
        if kind == 1:
            self.na_layer(i)
        elif kind == 2:
            self.gqa_layer(i)
        else:
            self.s5_layer(i)
        self.s.barrier()

    def proj_fm(self, w, wres, m4, h):
        b = self.newbank()
        for kt in range(8):
            self.mm(b, b.t[:, :], w[:, kt, m4 * 128:(m4 + 1) * 128], self.HT[:, kt, h * 512:(h + 1) * 512],
                    [wres, self.HTres[kt]])
        return b

    def qknorm(self, b, onesname, gain_col, out, out_res, out_reads=()):
        k = self.sq_i % 4
        self.sq_i += 1
        self.act(self.SQt[k], b.t[:, :], AF.Square, [b.res], [self.SQtres[k]])
        bm = self.newbank()
        self.mm(bm, bm.t[:, :], self.cb(onesname), self.SQt[k], [self.SQtres[k], self.CBres])
        r = self.rs_i % 2
        self.rs_i += 1
        self.rsqrt(self.RS[r], bm.t[:, :], [bm.res], [self.RSres[r]])
        self.stt('dve', out, b.t[:, :], self.SPt[:, self.cm.sl(gain_col)], self.RS[r], ALU.mult, ALU.mult,
                 [b.res, self.RSres[r], self.SPres] + list(out_reads), [out_res])

    def qk_pipeline(self, items, onesname, stage3, cur=None):
        st = {}
        N = len(items)
        cur = cur or [None, None, None]
        for n in range(N + 2):
            if n < N:
                tag, m4, h, ctx = items[n]
                if cur[0] != tag:
                    w, wres = self.next_w(tag)
                    cur = [tag, w, wres]
                b = self.proj_fm(cur[1], cur[2], m4, h)
                k = self.sq_i % 4
                self.sq_i += 1
                self.act(self.SQt[k], b.t[:, :], AF.Square, [b.res], [self.SQtres[k]])
                st[n] = [b, k, None]
            if 0 <= n - 1 < N:
                b, k, _ = st[n - 1]
                bm = self.newbank()
                self.mm(bm, bm.t[:, :], self.cb(onesname), self.SQt[k], [self.SQtres[k], self.CBres])
                r = self.rs_i % 2
                self.rs_i += 1
                self.rsqrt(self.RS[r], bm.t[:, :], [bm.res], [self.RSres[r]])
                st[n - 1][2] = r
            if 0 <= n - 2 < N:
                b, k, r = st.pop(n - 2)
                stage3(items[n - 2][3], b, r)
        return cur

    def emit_k_out(self, KF, KFres, odram, j):
        k = self.vf_i % 2
        self.vf_i += 1
        for half in range(2):
            b = self.newbank()
            for t4 in range(4):
                tt_ = half * 4 + t4
                self.mm(b, b.t[:, t4 * 128:(t4 + 1) * 128], KF[:, tt_ * 128:(tt_ + 1) * 128], self.IDF[:, :],
                        [KFres, self.IDFres])
            self.cp('act', self.VF[k][:, half * 512:(half + 1) * 512], b.t[:, :], [b.res], [self.VFres[k]])
        dst = odram.rearrange("(tt p) f -> p tt f", p=128)[:, :, j * 128:(j + 1) * 128]
        self.dma('sp', dst, self.VF[k][:, :].rearrange("p (a b) -> p a b", a=8), self.VFres[k],
                 reads=[self.VFres[k]], is_output=True)

    def v_proj(self, w, wres, c0, ncols, Vt, vcol0, odram):
        for tt_ in range(8):
            b = self.newbank()
            for kt in range(8):
                self.mm(b, b.t[:, 0:ncols], self.HT[:, kt, tt_ * 128:(tt_ + 1) * 128], w[:, kt, c0:c0 + ncols],
                        [wres, self.HTres[kt]])
            k = self.vf_i % 2
            self.vf_i += 1
            self.cp('act', self.VF[k][:, 0:ncols], b.t[:, 0:ncols], [b.res], [self.VFres[k]])
            self.cp('pool', Vt[:, tt_, vcol0:vcol0 + ncols], self.VF[k][:, 0:ncols], [self.VFres[k]], [self.Vres[tt_]])
            self.dma('sp', odram[tt_ * 128:(tt_ + 1) * 128, vcol0:vcol0 + ncols], self.VF[k][:, 0:ncols], self.VFres[k],
                     reads=[self.VFres[k]], is_output=True)

    def wo_proj(self, i, src):
        for ch in range(2):
            w, wres = self.next_w(('wo', i, ch))
            for m4 in range(4):
                mt = ch * 4 + m4
                for h in range(2):
                    b = self.newbank()
                    for kt in range(8):
                        self.mm(b, b.t[:, :], w[:, kt, m4 * 128:(m4 + 1) * 128], src[:, kt, h * 512:(h + 1) * 512],
                                [wres, self.HTres[kt]])
                    self.stt('dve', self.X[:, mt, h * 512:(h + 1) * 512], b.t[:, :], self.MODT[:, 16 + mt:17 + mt],
                             self.X[:, mt, h * 512:(h + 1) * 512], ALU.mult, ALU.add,
                             [b.res, self.MODres, self.Xres[mt]], [self.Xres[mt]])

    def attention(self, kind):
        na = kind == 'na'
        nheads = 16 if na else 8
        dh = 64 if na else 128
        scale = dh ** -0.5
        KT = self.KT_na if na else self.KT_g
        V = self.V_na if na else self.V_g
        self.rot = [0, 1, 2, 3]
        self.bank_i = 0
        steps = []
        for hd in range(nheads):
            for hf in range(2):
                tiles = [kt for kt in range(12) if (not na) or kt >= 8 or (kt, hf) in NA_TILE_IDX]
                for n, kt in enumerate(tiles):
                    steps.append((hd, hf, kt, n == 0, n == len(tiles) - 1))
        bias_steps = [st for st in steps if na and st[2] < 8]
        bias_slot = {}
        self._bias_n = 0

        def issue_bias(upto):
            while self._bias_n < min(len(bias_steps), upto):
                hd, hf, kt = bias_steps[self._bias_n][:3]
                bi = self._bias_n % 3
                self.dma('pool', self.BT[bi][:, :], self.na_bias[hd, NA_TILE_IDX[(kt, hf)]], self.BTres[bi],
                         writes=[self.BTres[bi]])
                bias_slot[(hd, hf, kt)] = bi
                self._bias_n += 1

        LA = 4
        NPT = 5
        st_info = {}
        nbias = [0]
        acc = [0]

        def front(n):
            hd, hf, kt, first, last = steps[n]
            if na:
                ht, pr = hd // 2, slice(64 * (hd % 2), 64 * (hd % 2) + 64)
                ktile = ht
            else:
                ht, pr = hd, slice(0, 128)
                ktile = hd // 4
            qs = slice(hf * 512, hf * 512 + 512)
            bs_ = self.newbank()
            ks = slice(kt * 128, kt * 128 + 128)
            kres = self.KTres[ktile] if kt < 8 else self.KCres
            self.mm(bs_, bs_.t[:, :], KT[pr, ktile, ks], self.QT[pr, ht, qs], [kres, self.QTres[ht]])
            p = self.pt_i % NPT
            self.pt_i += 1
            if kt < 8:
                if na:
                    issue_bias(nbias[0] + 3)
                    bi = bias_slot[(hd, hf, kt)]
                    nbias[0] += 1
                    self.mm(bs_, bs_.t[:, :], self.cb('ident'), self.BT[bi][:, :], [self.BTres[bi], self.CBres])
                else:
                    self.mm(bs_, bs_.t[:, :], self.GM[0:4, ks], self.GM[0:4, 1024 + hf * 512:1536 + hf * 512],
                            [self.GMres])
                self.act(self.PT[p][:, :], bs_.t[:, :], AF.Exp, [bs_.res], [self.PTres[p]], bias=0.0, scale=scale)
            else:
                self.act(self.PT[p][:, :], bs_.t[:, :], AF.Exp, [bs_.res, self.SPres], [self.PTres[p]],
                         bias=self.SPt[:, self.cm.sl('ctxb')], scale=scale)
            st_info[n] = p

        def back(n):
            hd, hf, kt, first, last = steps[n]
            if na:
                ht, pr = hd // 2, slice(64 * (hd % 2), 64 * (hd % 2) + 64)
                vc = slice(ht * 128, ht * 128 + 128)
            else:
                ht, pr = hd, slice(0, 128)
                vc = slice((hd // 4) * 128, (hd // 4) * 128 + 128)
            qs = slice(hf * 512, hf * 512 + 512)
            if first:
                acc[0] += 1
                self._bo = self.banks[4 + 2 * (acc[0] % 2)]
                self._bsum = self.banks[5 + 2 * (acc[0] % 2)]
                self._bo.fresh = True
                self._bsum.fresh = True
            bo, bsum = self._bo, self._bsum
            p = st_info.pop(n)
            self.mm(bo, bo.t[:, :], V[:, kt, vc], self.PT[p][:, :], [self.Vres[kt], self.PTres[p]])
            self.mm(bsum, bsum.t[:, :], self.cb('ones'), self.PT[p][:, :], [self.PTres[p], self.CBres])
            if last:
                self.s.op('dve', lambda e, o=self.RC[pr, :], a=bsum.t[pr, :]: e.reciprocal(o, a),
                          reads=[bsum.res], writes=[self.RCres])
                self.tt('dve', self.HT[pr, ht, qs], bo.t[pr, :], self.RC[pr, :], ALU.mult,
                        [bo.res, self.RCres], [self.HTres[ht]])

        hook_every = max(1, len(steps) // 9)
        for n in range(len(steps) + LA):
            if n < len(steps):
                if n % hook_every == hook_every - 1:
                    self.mod_hook(1)
                front(n)
            if n - LA >= 0:
                back(n - LA)
        self.rot = list(range(8))

    def na_layer(self, i):
        for j in range(8):
            self.dma('pool', self.KT_na[:, j, 1024:1536], self.na_kcT[j * 128:(j + 1) * 128, :], self.KCres,
                     writes=[self.KCres])
        for t in range(4):
            self.dma('pool', self.V_na[:, 8 + t, :], self.na_vc[t * 128:(t + 1) * 128, :], self.Vres[8 + t],
                     writes=[self.Vres[8 + t]])
        gq_col = self.SPt[:, self.cm.sl('naq')]
        gk_col = self.SPt[:, self.cm.sl('nak')]

        def s3_q(ctx, b, r):
            j, h = ctx
            self.stt('dve', self.QT[:, j, h * 512:(h + 1) * 512], b.t[:, :], gq_col, self.RS[r], ALU.mult, ALU.mult,
                     [b.res, self.RSres[r], self.SPres], [self.QTres[j]])

        def s3_k(ctx, b, r):
            j, h, kf = ctx
            self.stt('dve', self.TMP[kf][:, h * 512:(h + 1) * 512], b.t[:, :], gk_col, self.RS[r], ALU.mult, ALU.mult,
                     [b.res, self.RSres[r], self.SPres], [self.TMPres[kf]])
            if h == 1:
                self.cp('pool', self.KT_na[:, j, 0:1024], self.TMP[kf][:, :], [self.TMPres[kf]], [self.KTres[j]])
                self.emit_k_out(self.TMP[kf], self.TMPres[kf], self.o_nak, j)

        items = [(('naqkv', i, ch), m4, h, (ch * 4 + m4, h)) for ch in range(2) for m4 in range(4) for h in range(2)]
        self.qk_pipeline(items, 'blk64', s3_q)
        items = []
        for ch in range(2, 4):
            for m4 in range(4):
                kf = self.kf_i % 2
                self.kf_i += 1
                for h in range(2):
                    items.append((('naqkv', i, ch), m4, h, ((ch - 2) * 4 + m4, h, kf)))
        self.qk_pipeline(items, 'blk64', s3_k)
        for ch in range(4, 6):
            w, wres = self.next_w(('naqkv', i, ch))
            self.v_proj(w, wres, 0, 512, self.V_na, (ch - 4) * 512, self.o_nav)
        self.attention('na')
        self.wo_proj(i, self.HT)

    def rope_norm(self, b, r, gaincol, h, out, outres, kf=None, kfres=None):
        hs = slice(h * 512, (h + 1) * 512)
        QN, QNres = self.RL[0], self.RLres[0]
        T1, T1res = self.RL[1], self.RLres[1]
        T2, T2res = self.RC, self.RCres
        QNb, QNbres = self.PT[0], self.PTres[0]
        self.stt('dve', QN[:, :], b.t[:, :], self.SPt[:, self.cm.sl(gaincol)], self.RS[r], ALU.mult, ALU.mult,
                 [b.res, self.RSres[r], self.SPres], [QNres])
        self.cp('act', QNb[:, :], QN[:, :], [QNres], [QNbres])
        bsw = self.newbank()
        self.mm(bsw, bsw.t[:, :], self.cb('pswap'), QNb[:, :], [QNbres, self.CBres])
        self.tt('pool', T1[:, :], QN[:, :], self.CC[:, hs], ALU.mult, [QNres, self.ROPEres], [T1res])
        self.tt('dve', T2[:, :], bsw.t[:, :], self.SS[:, hs], ALU.mult, [bsw.res, self.ROPEres], [T2res])
        if kf is None:
            self.tt('dve', out, T1[:, :], T2[:, :], ALU.add, [T1res, T2res], [outres])
        else:
            self.tt('dve', kf, T1[:, :], T2[:, :], ALU.add, [T1res, T2res], [kfres])
            self.cp('act', out, kf, [kfres], [outres])

    def gqa_layer(self, i):
        self.dma('sp', self.CC[:, :], self.rope[0], self.ROPEres, writes=[self.ROPEres])
        self.dma('sp', self.SS[:, :], self.rope[1], self.ROPEres, writes=[self.ROPEres])
        self.dma('pool', self.GM[0:4, :], self.gmask, self.GMres, writes=[self.GMres])
        for kv in range(2):
            self.dma('pool', self.KT_g[:, kv, 1024:1536], self.gqa_kcT[kv * 128:(kv + 1) * 128, :], self.KCres,
                     writes=[self.KCres])
        for t in range(4):
            self.dma('pool', self.V_g[:, 8 + t, :], self.gqa_vc[t * 128:(t + 1) * 128, :], self.Vres[8 + t],
                     writes=[self.Vres[8 + t]])
        def s3_q(ctx, b, r):
            j, h = ctx
            self.rope_norm(b, r, 'gq', h, self.QT[:, j, h * 512:(h + 1) * 512], self.QTres[j])

        def s3_k(ctx, b, r):
            kv, h, kf = ctx
            self.rope_norm(b, r, 'gk', h, self.KT_g[:, kv, h * 512:(h + 1) * 512], self.KTres[kv],
                           kf=self.TMP[kf][:, h * 512:(h + 1) * 512], kfres=self.TMPres[kf])
            if h == 1:
                self.emit_k_out(self.TMP[kf], self.TMPres[kf], self.o_gk, kv)

        items = [(('gqkv', i, ch), m4, h, (ch * 4 + m4, h)) for ch in range(2) for m4 in range(4) for h in range(2)]
        cur = None
        for it in items:
            cur = self.qk_pipeline([it], 'ones128', s3_q, cur)
        items = []
        for kv in range(2):
            kf = self.kf_i % 2
            self.kf_i += 1
            for h in range(2):
                items.append((('gqkv', i, 2), kv, h, (kv, h, kf)))
        for it in items:
            cur = self.qk_pipeline([it], 'ones128', s3_k, cur)
        _, w, wres = cur
        self.v_proj(w, wres, 256, 256, self.V_g, 0, self.o_gv)
        self.attention('gqa')
        self.wo_proj(i, self.HT)

    def declare_s5_dram(self):
        pass

    def alloc_s5_sbuf(self):
        pass

    def plan_s5_weights(self, i):
        pass

    def s5_layer(self, i):
        pass


_TABLE_CACHE = {}


def na_bias_table(rpb, sample):
    if sample:
        q = np.arange(T)
        k = np.arange(T)
        qr, qc = (q // 64)[:, None], (q % 64)[:, None]
        kr, kc = (k // 64)[None, :], (k % 64)[None, :]
        rs = np.clip(qr - 4, 0, 8)
        cs = np.clip(qc - 8, 0, 48)
        ok = (kr >= rs) & (kr < rs + 8) & (kc >= cs) & (kc < cs + 16)
        drow = np.clip(kr - qr + 7, 0, 14)
        dc = np.clip(kc - qc + 15, 0, 30)
        full = np.where(ok[None], np.asarray(rpb, np.float32)[:, drow, dc], np.float32(NEG))
    else:
        q = np.arange(T)
        same = (q[:, None] // 256) == (q[None, :] // 256)
        full = np.broadcast_to(np.where(same, np.float32(0.0), np.float32(NEG))[None], (16, T, T))
    out = np.empty((16, len(NA_TILES), 128, 512), np.float32)
    for n, (kt, hf) in enumerate(NA_TILES):
        out[:, n] = np.transpose(full[:, hf * 512:(hf + 1) * 512, kt * 128:(kt + 1) * 128], (0, 2, 1))
    return out


def rope_tables(sample):
    if not sample:
        return np.stack([np.ones((128, T), np.float32), np.zeros((128, T), np.float32)])
    t = np.arange(T)
    row = (t // 64).astype(np.float32)
    col = (t % 64).astype(np.float32)
    half = 64
    inv = (np.float32(10000.0) ** (-np.arange(0, half, 2, dtype=np.float32) / np.float32(half))).astype(np.float32)
    ang = np.concatenate([row[:, None] * inv, col[:, None] * inv], axis=-1).astype(np.float32)
    c, s_ = np.cos(ang).astype(np.float32).T, np.sin(ang).astype(np.float32).T
    return np.stack([np.concatenate([c, c], 0), np.concatenate([-s_, s_], 0)])


def gmask_table(sample):
    g = np.zeros((4, 2048), np.float32)
    k = np.arange(T)
    for j in range(4):
        g[j, :T] = (k // 256 == j)
        if not sample:
            g[j, T:] = np.where(k // 256 == j, 0.0, NEG)
    return g


def mixer_inputs(inp, core):
    sample = core < 4
    m = {
        'na_w_qkv': np.asarray(inp['na_w_qkv'][0], np.float32),
        'na_w_o': np.asarray(inp['na_w_o'][0], np.float32),
        'gqa_w_qkv': np.asarray(inp['gqa_w_qkv'][0], np.float32),
        'gqa_w_o': np.asarray(inp['gqa_w_o'][0], np.float32),
        's5_glu_w': np.asarray(inp['s5_glu_w'], np.float32),
    }
    if sample:
        m['na_kcT'] = np.ascontiguousarray(np.asarray(inp['cache_na_k'][core, 0], np.float32).reshape(512, D).T)
        m['na_vc'] = np.ascontiguousarray(np.asarray(inp['cache_na_v'][core, 0], np.float32).reshape(512, D))
        m['gqa_kcT'] = np.ascontiguousarray(np.asarray(inp['cache_gqa_k'][core, 0], np.float32).reshape(512, 256).T)
        m['gqa_vc'] = np.ascontiguousarray(np.asarray(inp['cache_gqa_v'][core, 0], np.float32).reshape(512, 256))
    else:
        m['na_kcT'] = np.zeros((D, 512), np.float32)
        m['na_vc'] = np.zeros((512, D), np.float32)
        m['gqa_kcT'] = np.zeros((256, 512), np.float32)
        m['gqa_vc'] = np.zeros((512, 256), np.float32)
    key = ('nab', sample)
    if key not in _TABLE_CACHE:
        _TABLE_CACHE[key] = na_bias_table(inp['na_rpb'][0], sample)
        _TABLE_CACHE[('rope', sample)] = rope_tables(sample)
        _TABLE_CACHE[('gm', sample)] = gmask_table(sample)
    m['na_bias'] = _TABLE_CACHE[key]
    m['rope'] = _TABLE_CACHE[('rope', sample)]
    m['gmask'] = _TABLE_CACHE[('gm', sample)]
    return m


def make_builder(layers, mixers=True):
    dry = S5Builder(layers=layers, mixers=mixers, plan=None)
    dry.build()
    b = S5Builder(layers=layers, mixers=mixers, plan=dry.rec_tags)
    nc = b.build()
    return b, nc


def run(inp, layers=(0, 1, 2, 3), cores=None):
    _TABLE_CACHE.clear()
    b, nc = make_builder(layers)
    maps = []
    if cores is not None:
        for core in cores:
            m = common_inputs(inp, core, b.cm)
            m.update(mixer_inputs(inp, core))
            m.update(s5_inputs(inp, core))
            maps.append({k: v for k, v in m.items() if k in b.dram})
        res = run_bass_kernel_spmd(nc, maps, core_ids=list(range(len(cores))))
        return res.results
    for core in range(8):
        m = common_inputs(inp, core, b.cm)
        m.update(mixer_inputs(inp, core))
        m.update(s5_inputs(inp, core))
        maps.append({k: v for k, v in m.items() if k in b.dram})
    res = run_bass_kernel_spmd(nc, maps, core_ids=list(range(8)))
    R = res.results
    y_sample = np.stack([R[c]['yT'].T for c in range(4)]).astype(np.float32)
    y_prompt = np.concatenate([R[c]['yT'].T.reshape(4, 256, D) for c in range(4, 8)]).astype(np.float32)
    nak = np.concatenate([R[c]['o_nak'].reshape(4, 256, 16, 64) for c in range(4, 8)])[:, None]
    nav = np.concatenate([R[c]['o_nav'].reshape(4, 256, 16, 64) for c in range(4, 8)])[:, None]
    gk = np.concatenate([R[c]['o_gk'].reshape(4, 256, 2, 128) for c in range(4, 8)])[:, None]
    gv = np.concatenate([R[c]['o_gv'].reshape(4, 256, 2, 128) for c in range(4, 8)])[:, None]
    s5 = s5_assemble(R)
    return (y_prompt, y_sample, s5, nak.astype(np.float32), nav.astype(np.float32), gk.astype(np.float32),
            gv.astype(np.float32))


def kernel(**inputs):
    return run(inputs)


S5_NSM = 48


def s5_host_arrays(inp, core):
    sample = core < 4
    lam = np.zeros((2, 3, 128, 64), np.float32)
    Bm = np.zeros((2, 128, 2, 64, 16), np.float32)
    Cm = np.zeros((2, 128, 2, 64, 16), np.float32)
    h0 = np.zeros((2, 128, 2, 64), np.float32)
    for sl in range(2):
        for dr in range(2):
            rows = slice(dr * 64, dr * 64 + 64)
            lam[sl, 0, rows] = np.asarray(inp['s5_lam_re'][sl, dr], np.float32).T
            lam[sl, 1, rows] = np.asarray(inp['s5_lam_im'][sl, dr], np.float32).T
            lam[sl, 2, rows] = np.asarray(inp['s5_log_dt'][sl, dr], np.float32)[None, :]
            Bm[sl, rows, 0] = np.transpose(np.asarray(inp['s5_b_re'][sl, dr], np.float32), (1, 0, 2))
            Bm[sl, rows, 1] = np.transpose(np.asarray(inp['s5_b_im'][sl, dr], np.float32), (1, 0, 2))
            Cm[sl, rows, 0] = np.transpose(np.asarray(inp['s5_c_re'][sl, dr], np.float32), (2, 0, 1))
            Cm[sl, rows, 1] = np.transpose(np.asarray(inp['s5_c_im'][sl, dr], np.float32), (2, 0, 1))
            if sample:
                for ri in range(2):
                    h0[sl, rows, ri] = np.asarray(inp['state_s5'][core, sl, dr, ri], np.float32).T
    carry = np.ones((128, 128), np.float32)
    carry[:, 0] = 0.0
    carry[:, [32, 64, 96]] = 1.0 if sample else 0.0
    sg = np.arange(128) // 16
    tri = (sg[None, :] >= sg[:, None]).astype(np.float32)
    return {'s5_lam': lam, 's5_B': Bm, 's5_C': Cm, 's5_h0': h0, 's5_carry': carry, 's5_tri': tri,
            's5_antif': np.ascontiguousarray(np.eye(128, dtype=np.float32)[::-1])}


def s5_inputs(inp, core):
    return s5_host_arrays(inp, core)


def s5_core_states(res):
    return np.transpose(res['o_s5'], (1, 0, 2, 3, 4, 5))


def s5_assemble(R):
    return np.concatenate([s5_core_states(R[c]) for c in range(4, 8)]).astype(np.float32)


class S5Builder(FullBuilder):
    def declare_s5_dram(self):
        self.d_lam = self.din('s5_lam', [2, 3, 128, 64])
        self.d_B = self.din('s5_B', [2, 128, 2, 64, 16])
        self.d_C = self.din('s5_C', [2, 128, 2, 64, 16])
        self.d_h0 = self.din('s5_h0', [2, 128, 2, 64])
        self.d_carry = self.din('s5_carry', [128, 128])
        self.d_tri = self.din('s5_tri', [128, 128])
        self.d_antif = self.din('s5_antif', [128, 128])
        self.o_s5 = self.dout('o_s5', [2, 4, 2, 2, 64, 64])

    def alloc_s5_sbuf(self):
        self.SML = self.sb('SML', [128, S5_NSM, 64], F32)
        self.sm_idx = {}
        self.S5S = Res('S5S')
        self.BLKres = Res('BLK')
        self.BLKQres = Res('BLKQ')
        self.LAM3 = self.sb('LAM3', [128, 3, 64], F32)
        self.H0 = self.sb('H0', [128, 2, 64], F32)
        self.CARRY = self.sb('CARRY', [128, 128], F32)
        self.TRI = self.sb('TRI', [128, 128], F32)
        self.ANTIF = self.sb('ANTIF', [128, 128], F32)
        self.S5Cres = Res('S5C')
        self.PB = self.sb('PB', [128, 2, 8, 16], F32)
        self.PC = self.sb('PC', [128, 2, 8, 16], F32)
        self.PBres = Res('PB')
        self.BLK = self.sb('BLK', [128, 11, 72], F32)
        self.FINALL = self.RC[:, :].rearrange("p (s r g) -> p s r g", s=4, r=2)
        self.FINres = self.RCres
        S = self.SCR
        o = [0]

        def carve(n, dt=BF16):
            a = S[:, o[0]:o[0] + n]
            o[0] += n
            return a if dt == BF16 else a.bitcast(F32)
        self.HTOK = carve(1024).rearrange("p (a b) -> p a b", a=8)
        self.HTOKR = carve(1024).rearrange("p (a b) -> p a b", a=8)
        self.U = carve(1024).rearrange("p (a b) -> p a b", a=8)
        self.UR = carve(1024).rearrange("p (a b) -> p a b", a=8)
        self.EIN = carve(2048).rearrange("p (g r n) -> p g r n", g=8, r=2)
        self.AOUT = carve(2304).rearrange("p (g r n) -> p g r n", g=8, r=2)
        self.GEN7 = carve(2048).rearrange("p (g r n) -> p g r n", g=8, r=2)
        self.MGF = carve(2048).rearrange("p (g r n) -> p g r n", g=8, r=2)
        self.MGB = carve(2048).rearrange("p (g r n) -> p g r n", g=8, r=2)
        self.MINT = carve(2048).rearrange("p (g r n) -> p g r n", g=8, r=2)
        self.HPB = carve(2048).rearrange("p (r g c) -> p r g c", r=2, g=8)
        self.GR = carve(2048, F32).rearrange("p (g c) -> p g c", g=8)
        self.GI = carve(2048, F32).rearrange("p (g c) -> p g c", g=8)
        self.COS = carve(2048, F32).rearrange("p (g c) -> p g c", g=8)
        self.SIN = carve(2048, F32).rearrange("p (g c) -> p g c", g=8)
        self.TBG = S[:, o[0] - 8192:o[0] - 4096].bitcast(F32)
        self.TB2 = S[:, o[0]:o[0] + 4096].bitcast(F32)
        self.T1 = carve(2048, F32)
        self.T2 = carve(2048, F32)
        self.T3 = S[:, 8448:10496].bitcast(F32)
        self.AMt = S[:, 4096:6144].bitcast(F32)
        assert o[0] <= 32768, o[0]
        self.r_htok, self.r_htokr, self.r_u, self.r_ur = Res('htok'), Res('htokr'), Res('u'), Res('ur')
        self.r_ein, self.r_aout, self.r_gen7 = Res('ein'), Res('aout'), Res('gen7')
        self.r_mgf, self.r_mgb, self.r_mint, self.r_hpb = Res('mgf'), Res('mgb'), Res('mint'), Res('hpb')
        self.r_g, self.r_cs, self.r_t1, self.r_t2 = Res('g'), Res('cs'), Res('t1'), Res('t2')

    def sm(self, name):
        if name not in self.sm_idx:
            self.sm_idx[name] = len(self.sm_idx)
            assert len(self.sm_idx) <= S5_NSM - 2, name
        return self.SML[:, self.sm_idx[name], :]

    def _srw(self, extra):
        w = getattr(self, '_s_wres', None) or self.S5S
        return [self.S5S, w] + list(extra), [w]

    def s_tt(self, out, a, b, op, extra=()):
        r, w = self._srw(extra)
        self.s.op(getattr(self, '_s_eng', 'dve'), lambda e: e.tensor_tensor(out, a, b, op), reads=r, writes=w)

    def s_ts(self, out, a, s1, s2, op0, op1=None, extra=()):
        r, w = self._srw(extra)
        if op1 is None:
            self.s.op('dve', lambda e: e.tensor_scalar(out, a, s1, None, op0), reads=r, writes=w)
        else:
            self.s.op('dve', lambda e: e.tensor_scalar(out, a, s1, s2, op0, op1), reads=r, writes=w)

    def s_stt(self, out, a, sc, b, op0, op1, extra=()):
        r, w = self._srw(extra)
        self.s.op('dve', lambda e: e.scalar_tensor_tensor(out, a, sc, b, op0, op1), reads=r, writes=w)

    def s_cmul(self, outr, outi, ar, ai, br, bi, t1, t2, extra=()):
        self.s_tt(t1, ar, br, ALU.mult, extra)
        self.s_tt(t2, ai, bi, ALU.mult, extra)
        self.s_tt(outr, t1, t2, ALU.subtract)
        self.s_tt(t1, ar, bi, ALU.mult, extra)
        self.s_tt(t2, ai, br, ALU.mult, extra)
        self.s_tt(outi, t1, t2, ALU.add)

    def s_expm1(self, dr, di, zr, zi, nsq, deg, pre):
        t1, t2, t3, t4 = (self.sm(pre + n) for n in ('t1', 't2', 't3', 't4'))
        sr, si = self.sm(pre + 'sr'), self.sm(pre + 'si')
        xr, xi = self.sm(pre + 'xr'), self.sm(pre + 'xi')
        sc = 1.0 / (1 << nsq)
        self.s_ts(xr, zr, sc, None, ALU.mult)
        if zi is not None:
            self.s_ts(xi, zi, sc, None, ALU.mult)
        self.s_ts(sr, xr, 1.0 / deg, 1.0, ALU.mult, ALU.add)
        if zi is not None:
            self.s_ts(si, xi, 1.0 / deg, None, ALU.mult)
        for k in range(deg - 1, 1, -1):
            if zi is not None:
                self.s_tt(t1, xr, sr, ALU.mult)
                self.s_tt(t2, xi, si, ALU.mult)
                self.s_tt(t3, xr, si, ALU.mult)
                self.s_tt(t4, xi, sr, ALU.mult)
                self.s_tt(t1, t1, t2, ALU.subtract)
                self.s_tt(t3, t3, t4, ALU.add)
                self.s_ts(sr, t1, 1.0 / k, 1.0, ALU.mult, ALU.add)
                self.s_ts(si, t3, 1.0 / k, None, ALU.mult)
            else:
                self.s_tt(t1, xr, sr, ALU.mult)
                self.s_ts(sr, t1, 1.0 / k, 1.0, ALU.mult, ALU.add)
        if zi is not None:
            self.s_cmul(dr, di, xr, xi, sr, si, t1, t2)
        else:
            self.s_tt(dr, xr, sr, ALU.mult)
        for _ in range(nsq):
            if zi is not None:
                self.s_tt(t1, dr, dr, ALU.mult)
                self.s_tt(t2, di, di, ALU.mult)
                self.s_tt(t3, dr, di, ALU.mult)
                self.s_tt(t1, t1, t2, ALU.subtract)
                self.s_stt(dr, dr, 2.0, t1, ALU.mult, ALU.add)
                self.s_tt(t3, t3, di, ALU.add)
                self.s_ts(di, t3, 2.0, None, ALU.mult)
            else:
                self.s_tt(t1, dr, dr, ALU.mult)
                self.s_stt(dr, dr, 2.0, t1, ALU.mult, ALU.add)

    def s5_layer_params(self, sl):
        sm = self.sm
        self.dma('sp', self.LAM3[:, :, :], self.d_lam[sl].rearrange("a p g -> p a g"), self.S5S, writes=[self.S5S])
        self.dma('sp', self.H0[:, :, :], self.d_h0[sl], self.S5S, writes=[self.S5S])
        lamr, lami, ldt = self.LAM3[:, 0, :], self.LAM3[:, 1, :], self.LAM3[:, 2, :]
        self.s_expm1(sm('dt'), None, ldt, None, 6, 7, 'e_')
        self.s_ts(sm('dt'), sm('dt'), 1.0, None, ALU.add)
        self.s_tt(sm('ar'), lamr, sm('dt'), ALU.mult)
        self.s_tt(sm('ai'), lami, sm('dt'), ALU.mult)
        self.s_expm1(sm('nr'), sm('ni'), sm('ar'), sm('ai'), 7, 8, 'e_')
        self.s_ts(sm('abr'), sm('nr'), 1.0, None, ALU.add)
        t1, t2 = sm('e_t1'), sm('e_t2')
        self.s_tt(t1, lamr, lamr, ALU.mult)
        self.s_tt(t2, lami, lami, ALU.mult)
        self.s_tt(t1, t1, t2, ALU.add)
        self.s.op('dve', lambda e: e.reciprocal(sm('rden'), t1), reads=[self.S5S], writes=[self.S5S])
        self.s_tt(t1, sm('nr'), lamr, ALU.mult)
        self.s_tt(t2, sm('ni'), lami, ALU.mult)
        self.s_tt(t1, t1, t2, ALU.add)
        self.s_tt(sm('fre'), t1, sm('rden'), ALU.mult)
        self.s_tt(t1, sm('ni'), lamr, ALU.mult)
        self.s_tt(t2, sm('nr'), lami, ALU.mult)
        self.s_tt(t1, t1, t2, ALU.subtract)
        self.s_tt(sm('fim'), t1, sm('rden'), ALU.mult)
        self.s_ts(t1, sm('ar'), 2.0, None, ALU.mult)
        self.s_expm1(sm('m2'), None, t1, None, 5, 6, 'e_')
        self.s_ts(sm('m2'), sm('m2'), 1.0, None, ALU.add)
        self.s.op('dve', lambda e: e.reciprocal(sm('im2'), sm('m2')), reads=[self.S5S], writes=[self.S5S])
        self.s_tt(sm('q1r'), sm('abr'), sm('im2'), ALU.mult)
        self.s_stt(sm('q1i'), sm('ni'), -1.0, sm('im2'), ALU.mult, ALU.mult)
        cr, ci = sm('abr'), sm('ni')
        for k, nm in enumerate(('p2', 'p4', 'mu')):
            self.s_cmul(sm(nm + 'r'), sm(nm + 'i'), cr, ci, cr, ci, t1, t2)
            cr, ci = sm(nm + 'r'), sm(nm + 'i')
        self.s_tt(sm('rho8'), sm('m2'), sm('m2'), ALU.mult)
        self.s_tt(sm('rho8'), sm('rho8'), sm('rho8'), ALU.mult)
        self.s.op('dve', lambda e: e.reciprocal(t1, sm('rho8')), reads=[self.S5S], writes=[self.S5S])
        self.s_tt(sm('E0r'), sm('mur'), t1, ALU.mult)
        self.s_tt(sm('E0i'), sm('mui'), t1, ALU.mult)
        for k in range(1, 7):
            self.s_cmul(sm(f'E{k}r'), sm(f'E{k}i'), sm(f'E{k-1}r'), sm(f'E{k-1}i'), sm(f'E{k-1}r'), sm(f'E{k-1}i'), t1, t2)
        self.s_cmul(sm('g0r'), sm('g0i'), sm('mur'), sm('mui'), self.H0[:, 0, :], self.H0[:, 1, :], t1, t2)

    def blk(self, k):
        return self.BLK[:, k, :].rearrange("p (g n) -> p g n", g=8)

    def s5_block_tables(self, sl, j):
        sm = self.sm
        gs = slice(8 * j, 8 * j + 8)
        B = self.blk
        PWR, PWI, QR, QI, WBR, WBI, W7R, W7I, TA, TB = (B(k) for k in range(10))
        self._s_wres = self.BLKres
        self.dma('sp', self.PB[:, :, :, :], self.d_B[sl][:, :, gs, :], self.PBres, writes=[self.PBres])
        self.dma('sp', self.PC[:, :, :, :], self.d_C[sl][:, :, gs, :], self.PBres, writes=[self.PBres])

        def col(ap, n):
            return ap[:, :, n:n + 1]

        def sv(name):
            return sm(name)[:, gs].unsqueeze(2)

        def bc(ap, shape):
            return ap.to_broadcast(shape)

        TA2 = self.SML[:, S5_NSM - 2, :].rearrange("p (g n) -> p g n", g=8)
        TB2 = self.SML[:, S5_NSM - 1, :].rearrange("p (g n) -> p g n", g=8)

        def cmul_tab(outr, outi, ar, ai, br, bi, shape, tmps=None, extra=()):
            ta_, tb_ = tmps or (TA, TB)
            ta = ta_[:, :, 0:shape[2]]
            tb = tb_[:, :, 0:shape[2]]
            self.s_cmul(outr, outi, ar, ai, bc(br, shape), bc(bi, shape), ta, tb, extra)

        for (TR, TI, b1r, b1i, b2r, b2i, b4r, b4i) in (
                (PWR, PWI, 'abr', 'ni', 'p2r', 'p2i', 'p4r', 'p4i'),):
            self.s.op('dve', lambda e, o=col(TR, 0): e.memset(o, 1.0), reads=[self.BLKres], writes=[self.BLKres])
            self.s.op('dve', lambda e, o=col(TI, 0): e.memset(o, 0.0), reads=[self.BLKres], writes=[self.BLKres])
            self.s_tt(col(TR, 1), sv(b1r), sv(b1r), ALU.max)
            self.s_tt(col(TI, 1), sv(b1i), sv(b1i), ALU.max)
            self.s_tt(col(TR, 2), sv(b2r), sv(b2r), ALU.max)
            self.s_tt(col(TI, 2), sv(b2i), sv(b2i), ALU.max)
            cmul_tab(TR[:, :, 3:5], TI[:, :, 3:5], TR[:, :, 1:3], TI[:, :, 1:3], sv(b2r), sv(b2i), [128, 8, 2])
            cmul_tab(TR[:, :, 5:9], TI[:, :, 5:9], TR[:, :, 1:5], TI[:, :, 1:5], sv(b4r), sv(b4i), [128, 8, 4])
        self._s_eng = 'pool'
        self._s_wres = self.BLKQres
        q2 = (TA2, TB2)
        self.s.op('pool', lambda e, o=col(QR, 0): e.memset(o, 1.0), reads=[self.BLKQres], writes=[self.BLKQres])
        self.s.op('pool', lambda e, o=col(QI, 0): e.memset(o, 0.0), reads=[self.BLKQres], writes=[self.BLKQres])
        self.s.op('pool', lambda e, o=col(QR, 1), a=sv('q1r'): e.tensor_copy(o, a), reads=[self.S5S, self.BLKQres],
                  writes=[self.BLKQres])
        self.s.op('pool', lambda e, o=col(QI, 1), a=sv('q1i'): e.tensor_copy(o, a), reads=[self.S5S, self.BLKQres],
                  writes=[self.BLKQres])
        self.s_cmul(col(QR, 2), col(QI, 2), col(QR, 1), col(QI, 1), col(QR, 1), col(QI, 1), col(TA2, 0), col(TB2, 0))
        cmul_tab(QR[:, :, 3:5], QI[:, :, 3:5], QR[:, :, 1:3], QI[:, :, 1:3], col(QR, 2), col(QI, 2), [128, 8, 2], q2)
        cmul_tab(QR[:, :, 5:9], QI[:, :, 5:9], QR[:, :, 1:5], QI[:, :, 1:5], col(QR, 4), col(QI, 4), [128, 8, 4], q2)
        cmul_tab(WBR[:, :, 0:8], WBI[:, :, 0:8], QR[:, :, 0:8], QI[:, :, 0:8], sv('fre'), sv('fim'), [128, 8, 8], q2)
        self._s_eng = 'dve'
        self._s_wres = self.BLKres
        cmul_tab(W7R[:, :, 0:8], W7I[:, :, 0:8], WBR[:, :, 0:8], WBI[:, :, 0:8], col(PWR, 7), col(PWI, 7), [128, 8, 8],
                 None, [self.BLKQres])
        self._s_wres = None

    def s5_block_expand_in(self, sl, j):
        B = self.blk
        PWR, PWI, QR, QI, WBR, WBI, W7R, W7I, TA, TB = (B(k) for k in range(10))
        sh = [128, 8, 8, 16]
        Br = self.PB[:, 0, :, :].unsqueeze(2).to_broadcast(sh)
        Bi = self.PB[:, 1, :, :].unsqueeze(2).to_broadcast(sh)
        for (WR, WI, DST, dres) in ((WBR, WBI, self.EIN, self.r_ein), (W7R, W7I, self.GEN7, self.r_gen7)):
            wr = WR[:, :, 0:8].unsqueeze(3).to_broadcast(sh)
            wi = WI[:, :, 0:8].unsqueeze(3).to_broadcast(sh)
            a1 = self.TBG[:, 0:1024].rearrange("p (g s k) -> p g s k", g=8, s=8)
            a2 = self.TBG[:, 1024:2048].rearrange("p (g s k) -> p g s k", g=8, s=8)
            ex = [self.PBres, self.r_g, self.BLKres, self.BLKQres]
            w_ = [self.r_g]
            self.s.op('dve', lambda e, o=a1, x=wr, y=Br: e.tensor_tensor(o, x, y, ALU.mult), reads=[self.S5S] + ex, writes=w_)
            self.s.op('dve', lambda e, o=a2, x=wi, y=Bi: e.tensor_tensor(o, x, y, ALU.mult), reads=[self.S5S] + ex, writes=w_)
            dre = DST[:, :, 0, :].rearrange("p g (s k) -> p g s k", s=8)
            self.s.op('dve', lambda e, o=dre, x=a1, y=a2: e.tensor_tensor(o, x, y, ALU.subtract), reads=w_, writes=w_ + [dres])
            self.s.op('dve', lambda e, o=a1, x=wr, y=Bi: e.tensor_tensor(o, x, y, ALU.mult), reads=[self.S5S] + ex, writes=w_)
            self.s.op('dve', lambda e, o=a2, x=wi, y=Br: e.tensor_tensor(o, x, y, ALU.mult), reads=[self.S5S] + ex, writes=w_)
            dim_ = DST[:, :, 1, :].rearrange("p g (s k) -> p g s k", s=8)
            self.s.op('dve', lambda e, o=dim_, x=a1, y=a2: e.tensor_tensor(o, x, y, ALU.add), reads=w_, writes=w_ + [dres])

    def s5_block_expand(self, sl, j):
        B = self.blk
        PWR, PWI, QR, QI, WBR, WBI, W7R, W7I, TA, TB = (B(k) for k in range(10))
        t1 = self.TBG[:, 0:1152]
        t2 = self.TB2[:, 0:1152]
        sh9 = [128, 8, 9, 16]
        Cr = self.PC[:, 0, :, :].unsqueeze(2).to_broadcast(sh9)
        Ci = self.PC[:, 1, :, :].unsqueeze(2).to_broadcast(sh9)
        pr = PWR[:, :, 0:9].unsqueeze(3).to_broadcast(sh9)
        pi = PWI[:, :, 0:9].unsqueeze(3).to_broadcast(sh9)
        a1 = t1.rearrange("p (g s k) -> p g s k", g=8, s=9)
        a2 = t2.rearrange("p (g s k) -> p g s k", g=8, s=9)
        ex = [self.PBres, self.r_t1, self.r_t2, self.r_g, self.BLKres, self.BLKQres]
        w_ = [self.r_t1, self.r_t2, self.r_g]
        self.s.op('dve', lambda e: e.tensor_tensor(a1, Cr, pr, ALU.mult), reads=[self.S5S] + ex, writes=w_)
        self.s.op('dve', lambda e: e.tensor_tensor(a2, Ci, pi, ALU.mult), reads=[self.S5S] + ex, writes=w_)
        dre = self.AOUT[:, :, 0, :].rearrange("p g (s k) -> p g s k", s=9)
        self.s.op('dve', lambda e: e.tensor_tensor(dre, a1, a2, ALU.subtract), reads=w_, writes=w_ + [self.r_aout])
        self.s.op('dve', lambda e: e.tensor_tensor(a1, Cr, pi, ALU.mult), reads=[self.S5S] + ex, writes=w_)
        self.s.op('dve', lambda e: e.tensor_tensor(a2, Ci, pr, ALU.mult), reads=[self.S5S] + ex, writes=w_)
        dim_ = self.AOUT[:, :, 1, :].rearrange("p g (s k) -> p g s k", s=9)
        self.s.op('dve', lambda e: e.scalar_tensor_tensor(dim_, a1, -1.0, a2, ALU.mult, ALU.subtract), reads=w_,
                  writes=w_ + [self.r_aout])

    def s5_block(self, i, sl, j):
        sm = self.sm
        gs = slice(8 * j, 8 * j + 8)
        ident, anti = self.cb('ident'), self.cb('anti')
        hsrc = self.HT[:, j, :].rearrange("p (c s) -> p s c", s=8)
        for (dst, dres, rev) in ((self.HTOK, self.r_htok, False), (self.HTOKR, self.r_htokr, True)):
            for half in range(2):
                b = self.newbank()
                for q in range(4):
                    s_ = half * 4 + q
                    src_s = 7 - s_ if rev else s_
                    self.mm(b, b.t[:, q * 128:(q + 1) * 128], hsrc[:, src_s, :], ident, [self.HTres[j], self.CBres])
                dv = dst.rearrange("p g (s k) -> p s g k", s=8)[:, half * 4:half * 4 + 4, :, :]
                self.cp('act', dv, b.t[:, :].rearrange("p (s g k) -> p s g k", s=4, g=8), [b.res], [dres])
        for (src, sres, dst, dres, mat) in ((self.HTOK, self.r_htok, self.U, self.r_u, ident),
                                            (self.HTOKR, self.r_htokr, self.UR, self.r_ur, anti)):
            for half in range(2):
                b = self.newbank()
                for q in range(4):
                    g = half * 4 + q
                    self.mm(b, b.t[:, q * 128:(q + 1) * 128], src[:, g, :], mat, [sres, self.CBres])
                self.cp('act', dst[:, half * 4:half * 4 + 4, :], b.t[:, :].rearrange("p (a b) -> p a b", a=4),
                        [b.res], [dres])
        if j == 0:
            self.s5_block_tables(sl, 0)
            self.s5_block_expand_in(sl, 0)
        self.s5_block_expand(sl, j)
        for ri in range(2):
            for half in range(2):
                b = self.newbank()
                for q in range(4):
                    g = half * 4 + q
                    self.mm(b, b.t[:, q * 128:(q + 1) * 128], self.GEN7[:, g, ri, :], ident, [self.r_gen7, self.CBres])
                v = b.t[:, :].rearrange("p (a b) -> p a b", a=4)
                self.cp('act', self.MGF[:, half * 4:half * 4 + 4, ri, 0:64], v[:, :, 0:64], [b.res], [self.r_mgf])
                self.cp('act', self.MGB[:, half * 4:half * 4 + 4, ri, 64:128], v[:, :, 64:128], [b.res], [self.r_mgb])
        for dr in range(2):
            rows = slice(dr * 64, dr * 64 + 64)
            for half in range(2):
                b = self.newbank()
                for q in range(4):
                    g = half * 4 + q
                    for ri in range(2):
                        self.mm(b, b.t[:, q * 128:(q + 1) * 128], self.EIN[rows, g, ri, :], self.AOUT[rows, g, ri, 0:128],
                                [self.r_ein, self.r_aout])
                self.s.op('dve', lambda e, o=self.MINT[:, half * 4:half * 4 + 4, dr, :],
                          a=b.t[:, :].rearrange("p (a b) -> p a b", a=4),
                          m=self.TRI[:, :].unsqueeze(1).to_broadcast([128, 4, 128]): e.tensor_tensor(o, a, m, ALU.mult),
                          reads=[b.res, self.S5Cres], writes=[self.r_mint])
        gb = {}
        for ri in range(2):
            for half in range(2):
                b = self.newbank()
                gb[(ri, half)] = b
                for q in range(4):
                    g = half * 4 + q
                    self.mm(b, b.t[:, q * 128:(q + 1) * 128], self.MGF[:, g, ri, :], self.U[:, g, :], [self.r_mgf, self.r_u])
                    self.mm(b, b.t[:, q * 128:(q + 1) * 128], self.MGB[:, g, ri, :], self.UR[:, g, :], [self.r_mgb, self.r_ur])
        for ri, GG in ((0, self.GR), (1, self.GI)):
            for half in range(2):
                b = gb[(ri, half)]
                self.cp('act', GG[:, half * 4:half * 4 + 4, :], b.t[:, :].rearrange("p (a b) -> p a b", a=4),
                        [b.res], [self.r_g])
            g0 = sm('g0r' if ri == 0 else 'g0i')[:, gs].unsqueeze(2)
            self.s.op('dve', lambda e, o=GG[:, :, 0:1], a=GG[:, :, 0:1], b_=g0: e.tensor_tensor(o, a, b_, ALU.add),
                      reads=[self.S5S, self.r_g], writes=[self.r_g])
        self.s5_scan(sl, j)
        if j + 1 < NF:
            self.s5_block_tables(sl, j + 1)
            self.s5_block_expand_in(sl, j + 1)
        YF = self.T1.rearrange("p (t f) -> p t f", t=8)
        YB = self.T2.rearrange("p (t f) -> p t f", t=8)
        for dr, (Uc, ures, Y, yres) in enumerate(((self.U, self.r_u, YF, self.r_t1), (self.UR, self.r_ur, YB, self.r_t2))):
            rows = slice(dr * 64, dr * 64 + 64)
            for half in range(2):
                b = self.newbank()
                for q in range(4):
                    g = half * 4 + q
                    cs = slice(q * 128, (q + 1) * 128)
                    self.mm(b, b.t[:, cs], Uc[:, g, :], self.MINT[:, g, dr, :], [ures, self.r_mint])
                    for ri in range(2):
                        self.mm(b, b.t[:, cs], self.HPB[rows, ri, g, :], self.AOUT[rows, g, ri, 16:144],
                                [self.r_hpb, self.r_aout])
                src = b.t[:, :].rearrange("p (g t k) -> p g t k", g=4, t=8)
                dst = Y[:, :, half * 64:half * 64 + 64].rearrange("p t (g k) -> p g t k", g=4)
                self.cp('act', dst, src, [b.res], [yres])
        yfull = self.TMP[j % 2]
        yres = self.TMPres[j % 2]
        for half in range(2):
            b = self.newbank()
            for q in range(4):
                t_ = half * 4 + q
                cs = slice(q * 128, (q + 1) * 128)
                self.mm(b, b.t[:, cs], YF[:, t_, :], self.IDF[:, :], [self.r_t1, self.IDFres])
                self.mm(b, b.t[:, cs], YB[:, 7 - t_, :], self.ANTIF[:, :], [self.r_t2, self.S5Cres])
            dst = yfull[:, :].rearrange("p (c t) -> p t c", t=8)[:, half * 4:half * 4 + 4, :]
            usrc = self.HT[:, j, :].rearrange("p (c t) -> p t c", t=8)[:, half * 4:half * 4 + 4, :]
            self.s.op('dve', lambda e, o=dst, u=usrc, d=self.SPt[:, self.cm.m[f's5d_{sl}'][0] + j:self.cm.m[f's5d_{sl}'][0] + j + 1],
                      y=b.t[:, :].rearrange("p (t c) -> p t c", t=4): e.scalar_tensor_tensor(o, u, d, y, ALU.mult, ALU.add),
                      reads=[b.res, self.HTres[j], self.SPres], writes=[yres])
        self.act(self.HT[:, j, :], yfull[:, :], AF.Gelu_apprx_tanh, [yres], [self.HTres[j]])

    def s5_scan(self, sl, j):
        sm = self.sm
        gs = slice(8 * j, 8 * j + 8)
        GR, GI, COS, SIN = self.GR, self.GI, self.COS, self.SIN
        sh = [128, 8, 128]
        PTMP = self.HPB.rearrange("p r g c -> p (r g c)").bitcast(F32)
        self.s.op('pool', lambda e: e.memset(COS[:, :, 0:1], 1.0), reads=[self.r_cs], writes=[self.r_cs])
        self.s.op('pool', lambda e: e.memset(SIN[:, :, 0:1], 0.0), reads=[self.r_cs], writes=[self.r_cs])
        for k in range(7):
            d = 1 << k
            er = sm(f'E{k}r')[:, gs].unsqueeze(2).to_broadcast([128, 8, d])
            ei = sm(f'E{k}i')[:, gs].unsqueeze(2).to_broadcast([128, 8, d])
            t1 = PTMP[:, 0:8 * d].rearrange("p (g c) -> p g c", g=8)
            t2 = PTMP[:, 512:512 + 8 * d].rearrange("p (g c) -> p g c", g=8)
            rd = [self.S5S, self.r_cs, self.r_hpb]
            wr = [self.r_hpb, self.r_cs]
            o = self.s.op
            o('pool', lambda e, a=t1, x=COS[:, :, 0:d], y=er: e.tensor_tensor(a, x, y, ALU.mult), reads=rd, writes=wr)
            o('pool', lambda e, a=t2, x=SIN[:, :, 0:d], y=ei: e.tensor_tensor(a, x, y, ALU.mult), reads=rd, writes=wr)
            o('pool', lambda e, a=COS[:, :, d:2 * d], x=t1, y=t2: e.tensor_tensor(a, x, y, ALU.subtract), reads=rd, writes=wr)
            o('pool', lambda e, a=t1, x=COS[:, :, 0:d], y=ei: e.tensor_tensor(a, x, y, ALU.mult), reads=rd, writes=wr)
            o('pool', lambda e, a=t2, x=SIN[:, :, 0:d], y=er: e.tensor_tensor(a, x, y, ALU.mult), reads=rd, writes=wr)
            o('pool', lambda e, a=SIN[:, :, d:2 * d], x=t1, y=t2: e.tensor_tensor(a, x, y, ALU.add), reads=rd, writes=wr)
        T1 = self.T1[:, 0:1024].rearrange("p (g c) -> p g c", g=8)
        T2 = self.T2[:, 0:1024].rearrange("p (g c) -> p g c", g=8)
        T3 = self.T3.rearrange("p (g c) -> p g c", g=8)
        AM = self.AMt.rearrange("p (g c) -> p g c", g=8)
        rd = [self.S5S, self.r_cs, self.r_t1, self.r_t2, self.r_g, self.S5Cres, self.r_ein, self.r_gen7]
        wr = [self.r_t1, self.r_t2, self.r_g, self.r_ein, self.r_gen7]
        o = self.s.op
        o('pool', lambda e: e.tensor_tensor(AM, sm('rho8')[:, gs].unsqueeze(2).to_broadcast(sh),
                                            self.CARRY[:, :].unsqueeze(1).to_broadcast(sh), ALU.mult),
          reads=[self.S5S, self.S5Cres, self.r_ein], writes=[self.r_ein])
        o('dve', lambda e: e.tensor_tensor(T1, GR, COS, ALU.mult), reads=rd, writes=wr)
        o('dve', lambda e: e.tensor_tensor(T2, GI, SIN, ALU.mult), reads=rd, writes=wr)
        o('dve', lambda e: e.tensor_tensor(T1, T1, T2, ALU.add), reads=rd, writes=wr)
        o('dve', lambda e: e.tensor_tensor(T3, GI, COS, ALU.mult), reads=rd, writes=wr)
        o('dve', lambda e: e.tensor_tensor(T2, GR, SIN, ALU.mult), reads=rd, writes=wr)
        o('dve', lambda e: e.tensor_tensor(T3, T3, T2, ALU.subtract), reads=rd, writes=wr)
        fl = lambda a: a.rearrange("p g c -> p (g c)")
        o('dve', lambda e: e.tensor_tensor_scan(fl(GR), fl(AM), fl(T1), 0.0, ALU.mult, ALU.add), reads=rd, writes=wr)
        o('dve', lambda e: e.tensor_tensor_scan(fl(GI), fl(AM), fl(T3), 0.0, ALU.mult, ALU.add), reads=rd, writes=wr)
        o('dve', lambda e: e.tensor_tensor(T1, GR, COS, ALU.mult), reads=rd, writes=wr)
        o('dve', lambda e: e.tensor_tensor(T2, GI, SIN, ALU.mult), reads=rd, writes=wr)
        o('dve', lambda e: e.tensor_tensor(T1, T1, T2, ALU.subtract), reads=rd, writes=wr)
        o('dve', lambda e: e.tensor_tensor(T3, GI, COS, ALU.mult), reads=rd, writes=wr)
        o('dve', lambda e: e.tensor_tensor(T2, GR, SIN, ALU.mult), reads=rd, writes=wr)
        o('dve', lambda e: e.tensor_tensor(T3, T3, T2, ALU.add), reads=rd, writes=wr)
        cm_ = self.CARRY[:, 1:128].unsqueeze(1).to_broadcast([128, 8, 127])
        prd = [self.S5S, self.S5Cres, self.r_t1, self.r_gen7]
        for ri, Hf in ((0, T1), (1, T3)):
            o('pool', lambda e, a=self.HPB[:, ri, :, 1:128], x=Hf[:, :, 0:127], y=cm_: e.tensor_tensor(a, x, y, ALU.mult),
              reads=prd, writes=[self.r_hpb])
            o('pool', lambda e, a=self.HPB[:, ri, :, 0:1], x=self.H0[:, ri, gs].unsqueeze(2): e.tensor_copy(a, x),
              reads=[self.S5S], writes=[self.r_hpb])
            fv = self.FINALL[:, :, ri, gs].rearrange("p s g -> p g s")
            hv = Hf.rearrange("p g (s c) -> p g s c", s=4)[:, :, :, 31]
            o('pool', lambda e, a=fv, x=hv: e.tensor_copy(a, x), reads=prd, writes=[self.FINres])

    def s5_prefetch_params(self, sl):
        self.s5_layer_params(sl)
        self._s5_params_ready = sl

    def s5_layer(self, i):
        sl = i // 3
        if not getattr(self, '_s5_const_loaded', False):
            self._s5_const_loaded = True
            self.dma('sp', self.CARRY[:, :], self.d_carry, self.S5Cres, writes=[self.S5Cres])
            self.dma('sp', self.TRI[:, :], self.d_tri, self.S5Cres, writes=[self.S5Cres])
            self.dma('sp', self.ANTIF[:, :], self.d_antif, self.S5Cres, writes=[self.S5Cres])
        self.s.op('pool', lambda e: e.memset(self.MGF[:, :, :, 64:128], 0.0), writes=[self.r_mgf])
        self.s.op('pool', lambda e: e.memset(self.MGB[:, :, :, 0:64], 0.0), writes=[self.r_mgb])
        if getattr(self, '_s5_params_ready', None) != sl:
            self.s5_prefetch_params(sl)
        self._s5_params_ready = None
        for j in range(NF):
            self.s5_block(i, sl, j)
            self.mod_hook(1)
        for seg in range(4):
            for ri in range(2):
                b = self.newbank()
                self.mm(b, b.t[0:64, 0:128], self.FINALL[:, seg, ri, :], self.IDF[:, :], [self.FINres, self.IDFres])
                k = self.vf_i % 2
                self.vf_i += 1
                self.cp('act', self.VF[k][0:64, 0:128], b.t[0:64, 0:128], [b.res], [self.VFres[k]])
                for dr in range(2):
                    oseg = seg if dr == 0 else 3 - seg
                    self.dma('sp', self.o_s5[sl, oseg, dr, ri], self.VF[k][0:64, dr * 64:dr * 64 + 64], self.VFres[k],
                             reads=[self.VFres[k]], is_output=True)
        gb0 = self.cm.m[f'glub_{sl}'][0]
        for ch in range(4):
            w, wres = self.next_w(('glu', i, ch))
            for m2 in range(2):
                mt = ch * 2 + m2
                for h in range(2):
                    hs = slice(h * 512, (h + 1) * 512)
                    ba = self.newbank()
                    bg = self.newbank()
                    for kt in range(8):
                        self.mm(ba, ba.t[:, :], w[:, kt, m2 * 128:(m2 + 1) * 128], self.HT[:, kt, hs], [wres, self.HTres[kt]])
                    for kt in range(8):
                        self.mm(bg, bg.t[:, :], w[:, kt, 256 + m2 * 128:256 + (m2 + 1) * 128], self.HT[:, kt, hs],
                                [wres, self.HTres[kt]])
                    k = self.rl_i % 2
                    self.rl_i += 1
                    self.act(self.RL[k][:, :], bg.t[:, :], AF.Sigmoid, [bg.res, self.SPres], [self.RLres[k]],
                             bias=self.SPt[:, gb0 + 8 + mt:gb0 + 9 + mt], scale=1.0)
                    self.stt('dve', self.RL[k][:, :], ba.t[:, :], self.SPt[:, gb0 + mt:gb0 + mt + 1], self.RL[k][:, :],
                             ALU.add, ALU.mult, [ba.res, self.SPres, self.RLres[k]], [self.RLres[k]])
                    self.stt('dve', self.X[:, mt, hs], self.RL[k][:, :], self.MODT[:, 16 + mt:17 + mt], self.X[:, mt, hs],
                             ALU.mult, ALU.add, [self.RLres[k], self.MODres, self.Xres[mt]], [self.Xres[mt]])
```
